# Optimizing a Trainium2 kernel written in Bass

```python
import jax, jax.numpy as jnp
from jax import lax
import numpy as np

D_MODEL = 1024
BATCH = 8
SEQ = 4096
DEPTH = 1

HEAD_DIM = 64
RET_HEADS = 8
NSA_HEADS = 8
NSA_KV_HEADS = 2
NSA_GROUP = NSA_HEADS // NSA_KV_HEADS
D_RET = RET_HEADS * HEAD_DIM
D_NSA = NSA_HEADS * HEAD_DIM
D_MIX = D_RET + D_NSA
D_KV = NSA_KV_HEADS * HEAD_DIM
N_BRANCH = 3
RET_CHUNK = 128
ROPE_THETA = 10000.0
CMP_BLOCK = 32
CMP_STRIDE = 16
CMP_HIDDEN = 256
SLC_BLOCK = 64
SLC_TOPK = 16
SLC_QBLOCK = 64
WIN_SIZE = 512
WIN_QBLOCK = 128
EPS = 1e-6
NEG = -1e30
FORCE_BONUS = 1e4

IN_WIDTHS = (D_RET, D_RET, D_RET, D_RET,
             D_NSA, D_NSA,
             D_KV, D_KV, D_KV, D_KV, D_KV, D_KV,
             N_BRANCH * NSA_HEADS)
D_IN = sum(IN_WIDTHS)
SPLIT_POINTS = [int(s) for s in np.cumsum(IN_WIDTHS)[:-1]]

kernel_name = "hymba_retnet_nsa_layer"


def rms_norm(x, w):
    xf = x.astype(jnp.float32)
    y = xf * lax.rsqrt(jnp.mean(xf * xf, axis=-1, keepdims=True) + EPS)
    return (y * w.astype(jnp.float32)).astype(x.dtype)


def rope(x):
    T = x.shape[2]
    half = HEAD_DIM // 2
    inv = ROPE_THETA ** (-jnp.arange(half, dtype=jnp.float32) / half)
    ang = jnp.arange(T, dtype=jnp.float32)[:, None] * inv[None, :]
    cos, sin = jnp.cos(ang), jnp.sin(ang)
    x1, x2 = x[..., :half], x[..., half:]
    return jnp.concatenate([x1 * cos - x2 * sin, x1 * sin + x2 * cos], axis=-1)


def retention(q, k, v):
    B, H, T, d = q.shape
    C = RET_CHUNK
    N = T // C
    f32 = jnp.float32
    log_g = jnp.log1p(-jnp.exp2(-5.0 - jnp.arange(H, dtype=f32)))
    qc = rope(q.astype(f32)).reshape(B, H, N, C, d)
    kc = (rope(k.astype(f32)) * d ** -0.5).reshape(B, H, N, C, d)
    vc = v.astype(f32).reshape(B, H, N, C, d)
    pos = jnp.arange(C, dtype=f32)
    diff = pos[:, None] - pos[None, :]
    decay = jnp.where(diff >= 0, jnp.exp(log_g[:, None, None] * jnp.maximum(diff, 0.0)), 0.0)
    s = jnp.einsum('bhncd,bhnsd->bhncs', qc, kc) * decay[None, :, None]
    o_inner = jnp.einsum('bhncs,bhnse->bhnce', s, vc)
    zeta = jnp.exp(log_g[:, None] * (C - 1.0 - pos))
    kv = jnp.einsum('bhnsd,bhnse->nbhde', kc, vc * zeta[None, :, None, :, None])
    g_chunk = jnp.exp(log_g * C)[None, :, None, None]

    def step(R, kv_i):
        return R * g_chunk + kv_i, R

    _, R_prev = lax.scan(step, jnp.zeros((B, H, d, d), f32), kv)
    xi = jnp.exp(log_g[:, None] * (pos + 1.0))
    o_cross = jnp.einsum('bhncd,nbhde->bhnce', qc, R_prev) * xi[None, :, None, :, None]
    return (o_inner + o_cross).reshape(B, H, T, d)


def compress(kv, pos_emb, w1, w2):
    B, G, T, d = kv.shape
    n_cmp = (T - CMP_BLOCK) // CMP_STRIDE + 1
    idx = np.arange(n_cmp)[:, None] * CMP_STRIDE + np.arange(CMP_BLOCK)[None, :]
    blocks = kv[:, :, idx] + pos_emb
    flat = blocks.reshape(B, G, n_cmp, CMP_BLOCK * d)
    return jax.nn.silu(flat @ w1) @ w2


def compressed_branch(q, kc_raw, vc_raw, k_norm, pos_k, w1_k, w2_k, pos_v, w1_v, w2_v):
    B, G, R, T, d = q.shape
    kc = rms_norm(compress(kc_raw, pos_k, w1_k, w2_k), k_norm)
    vc = compress(vc_raw, pos_v, w1_v, w2_v)
    n_cmp = kc.shape[2]
    block_end = np.arange(n_cmp) * CMP_STRIDE + CMP_BLOCK - 1
    valid = jnp.asarray(block_end[None, :] <= np.arange(T)[:, None])
    s = jnp.einsum('bgrtd,bgnd->bgrtn', q, kc).astype(jnp.float32) * d ** -0.5
    p = jax.nn.softmax(jnp.where(valid, s, NEG), axis=-1)
    p = jnp.where(valid, p, 0.0)
    o = jnp.einsum('bgrtn,bgnd->bgrtd', p.astype(vc.dtype), vc)
    return o, p


def overlap_matrix(T):
    n_cmp = (T - CMP_BLOCK) // CMP_STRIDE + 1
    pos = np.arange(n_cmp)[:, None] * CMP_STRIDE + np.arange(CMP_BLOCK)[None, :]
    blk = pos // SLC_BLOCK
    return (blk[:, :, None] == np.arange(T // SLC_BLOCK)[None, None, :]).mean(axis=1).astype(np.float32)


def selected_branch(q, ks, vs, p_cmp):
    B, G, R, T, d = q.shape
    n_slc = T // SLC_BLOCK
    top = min(SLC_TOPK, n_slc)
    M = jnp.asarray(overlap_matrix(T))
    imp = jnp.einsum('bgtn,ns->bgts', p_cmp.sum(axis=2), M)
    t = jnp.arange(T)
    blk = jnp.arange(n_slc)
    valid = blk[None, :] * SLC_BLOCK <= t[:, None]
    force = (blk[None, :] == (t // SLC_BLOCK)[:, None]) | (blk[None, :] == 0)
    score = jnp.where(valid, jnp.where(force, imp + FORCE_BONUS, imp), NEG)
    _, idx = lax.top_k(score, top)
    ks_blk = ks.reshape(B, G, n_slc, SLC_BLOCK, d)
    vs_blk = vs.reshape(B, G, n_slc, SLC_BLOCK, d)
    nb = T // SLC_QBLOCK
    q_b = q.reshape(B, G, R, nb, SLC_QBLOCK, d).transpose(3, 0, 1, 2, 4, 5)
    idx_b = idx.reshape(B, G, nb, SLC_QBLOCK, top).transpose(2, 0, 1, 3, 4)
    t_b = t.reshape(nb, SLC_QBLOCK)
    gather = jax.vmap(jax.vmap(lambda arr, ib: arr[ib]))

    def block(args):
        qb, ib, tb = args
        flat = ib.reshape(B, G, SLC_QBLOCK * top)
        kg = gather(ks_blk, flat).reshape(B, G, SLC_QBLOCK, top, SLC_BLOCK, d)
        vg = gather(vs_blk, flat).reshape(B, G, SLC_QBLOCK, top, SLC_BLOCK, d)
        s = jnp.einsum('bgrqd,bgqkld->bgrqkl', qb, kg).astype(jnp.float32) * d ** -0.5
        key_pos = ib[..., None] * SLC_BLOCK + jnp.arange(SLC_BLOCK)
        mask = key_pos <= tb[None, None, :, None, None]
        s = jnp.where(mask[:, :, None], s, NEG).reshape(B, G, R, SLC_QBLOCK, top * SLC_BLOCK)
        p = jax.nn.softmax(s, axis=-1).reshape(B, G, R, SLC_QBLOCK, top, SLC_BLOCK)
        return jnp.einsum('bgrqkl,bgqkld->bgrqd', p.astype(vg.dtype), vg)

    o = lax.map(block, (q_b, idx_b, t_b))
    return o.transpose(1, 2, 3, 0, 4, 5).reshape(B, G, R, T, d)


def window_branch(q, kw, vw):
    B, G, R, T, d = q.shape
    nb = T // WIN_QBLOCK
    span = WIN_QBLOCK + WIN_SIZE
    idx = np.arange(nb)[:, None] * WIN_QBLOCK + np.arange(span)[None, :]
    pad = ((0, 0), (0, 0), (WIN_SIZE, 0), (0, 0))
    kp = jnp.pad(kw, pad)[:, :, idx]
    vp = jnp.pad(vw, pad)[:, :, idx]
    qb = q.reshape(B, G, R, nb, WIN_QBLOCK, d)
    tq = np.arange(nb)[:, None] * WIN_QBLOCK + np.arange(WIN_QBLOCK)[None, :]
    s_pos = idx - WIN_SIZE
    delta = tq[:, :, None] - s_pos[:, None, :]
    mask = jnp.asarray((delta >= 0) & (delta < WIN_SIZE) & (s_pos[:, None, :] >= 0))
    s = jnp.einsum('bgrnqd,bgnsd->bgrnqs', qb, kp).astype(jnp.float32) * d ** -0.5
    p = jax.nn.softmax(jnp.where(mask, s, NEG), axis=-1)
    o = jnp.einsum('bgrnqs,bgnsd->bgrnqd', p.astype(vp.dtype), vp)
    return o.reshape(B, G, R, T, d)


def hybrid_layer(x, norm_w, w_in, ret_norm_w, q_norm_w, k_norm_cmp, k_norm_slc, k_norm_win,
                 cmp_pos_k, cmp_w1_k, cmp_w2_k, cmp_pos_v, cmp_w1_v, cmp_w2_v, b_gate, w_out):
    B, T, _ = x.shape
    h = rms_norm(x, norm_w)
    proj = h @ w_in
    rq, rk, rv, rg, nq, ng, ck, cv, sk, sv, wk, wv, gl = jnp.split(proj, SPLIT_POINTS, axis=-1)

    def heads(t, n):
        return t.reshape(B, T, n, HEAD_DIM).transpose(0, 2, 1, 3)

    o_ret = retention(heads(rq, RET_HEADS), heads(rk, RET_HEADS), heads(rv, RET_HEADS))
    o_ret = rms_norm(o_ret.transpose(0, 2, 1, 3), ret_norm_w).astype(x.dtype)
    y_ret = o_ret.reshape(B, T, D_RET) * jax.nn.silu(rg)

    q = rms_norm(heads(nq, NSA_HEADS), q_norm_w).reshape(B, NSA_KV_HEADS, NSA_GROUP, T, HEAD_DIM)
    kc_raw, vc_raw = heads(ck, NSA_KV_HEADS), heads(cv, NSA_KV_HEADS)
    ks, vs = rms_norm(heads(sk, NSA_KV_HEADS), k_norm_slc), heads(sv, NSA_KV_HEADS)
    kw, vw = rms_norm(heads(wk, NSA_KV_HEADS), k_norm_win), heads(wv, NSA_KV_HEADS)
    o_cmp, p_cmp = compressed_branch(q, kc_raw, vc_raw, k_norm_cmp,
                                     cmp_pos_k, cmp_w1_k, cmp_w2_k, cmp_pos_v, cmp_w1_v, cmp_w2_v)
    o_slc = selected_branch(q, ks, vs, p_cmp)
    o_win = window_branch(q, kw, vw)
    gates = jax.nn.sigmoid((gl + b_gate).astype(jnp.float32))
    gates = gates.reshape(B, T, N_BRANCH, NSA_HEADS).transpose(2, 0, 3, 1)[..., None]
    shp = (B, NSA_HEADS, T, HEAD_DIM)
    o_nsa = (gates[0] * o_cmp.reshape(shp) + gates[1] * o_slc.reshape(shp)
             + gates[2] * o_win.reshape(shp)).astype(x.dtype)
    y_nsa = o_nsa.transpose(0, 2, 1, 3).reshape(B, T, D_NSA) * jax.nn.silu(ng)

    y = jnp.concatenate([y_ret, y_nsa], axis=-1) @ w_out
    return x + y


def setup_inputs(seed: int = 0) -> dict:
    key = jax.random.key(seed)
    ks = jax.random.split(key, 17)

    def nrm(k, shape, scale):
        return jax.random.normal(k, shape, jnp.float32) * scale

    L = DEPTH
    return {
        "x": nrm(ks[0], (BATCH, SEQ, D_MODEL), 1.0),
        "norm_w": 1.0 + nrm(ks[1], (L, D_MODEL), 0.01),
        "w_in": nrm(ks[2], (L, D_MODEL, D_IN), D_MODEL ** -0.5),
        "ret_norm_w": 1.0 + nrm(ks[3], (L, RET_HEADS, HEAD_DIM), 0.01),
        "q_norm_w": 1.0 + nrm(ks[4], (L, HEAD_DIM), 0.01),
        "k_norm_cmp": 1.0 + nrm(ks[5], (L, HEAD_DIM), 0.01),
        "k_norm_slc": 1.0 + nrm(ks[6], (L, HEAD_DIM), 0.01),
        "k_norm_win": 1.0 + nrm(ks[7], (L, HEAD_DIM), 0.01),
        "cmp_pos_k": nrm(ks[8], (L, CMP_BLOCK, HEAD_DIM), 0.02),
        "cmp_w1_k": nrm(ks[9], (L, CMP_BLOCK * HEAD_DIM, CMP_HIDDEN), (CMP_BLOCK * HEAD_DIM) ** -0.5),
        "cmp_w2_k": nrm(ks[10], (L, CMP_HIDDEN, HEAD_DIM), CMP_HIDDEN ** -0.5),
        "cmp_pos_v": nrm(ks[11], (L, CMP_BLOCK, HEAD_DIM), 0.02),
        "cmp_w1_v": nrm(ks[12], (L, CMP_BLOCK * HEAD_DIM, CMP_HIDDEN), (CMP_BLOCK * HEAD_DIM) ** -0.5),
        "cmp_w2_v": nrm(ks[13], (L, CMP_HIDDEN, HEAD_DIM), CMP_HIDDEN ** -0.5),
        "b_gate": nrm(ks[14], (L, N_BRANCH * NSA_HEADS), 0.01),
        "w_out": nrm(ks[15], (L, D_MIX, D_MODEL), D_MIX ** -0.5),
    }


def reference(x, norm_w, w_in, ret_norm_w, q_norm_w, k_norm_cmp, k_norm_slc, k_norm_win,
              cmp_pos_k, cmp_w1_k, cmp_w2_k, cmp_pos_v, cmp_w1_v, cmp_w2_v, b_gate, w_out):
    for layer in range(DEPTH):
        x = hybrid_layer(x, norm_w[layer], w_in[layer], ret_norm_w[layer], q_norm_w[layer],
                         k_norm_cmp[layer], k_norm_slc[layer], k_norm_win[layer],
                         cmp_pos_k[layer], cmp_w1_k[layer], cmp_w2_k[layer],
                         cmp_pos_v[layer], cmp_w1_v[layer], cmp_w2_v[layer],
                         b_gate[layer], w_out[layer])
    return x
```

```python
import contextlib
import numpy as np
import concourse.bass as bass
import concourse.mybir as mybir
from concourse.bass_utils import run_bass_kernel_spmd

F32 = mybir.dt.float32
BF16 = mybir.dt.bfloat16
ALU = mybir.AluOpType
AF = mybir.ActivationFunctionType
AX = mybir.AxisListType

T = 4096
D = 1024
NT = T // 128
NCOL1 = 768
NCOL2 = 3096
EPS = 1e-6
NEGB = -30000.0
DEBUG = False
SEQUENTIAL = False


class Res:
    __slots__ = ('name', 'w', 'r')

    def __init__(s, name=''):
        s.name = name
        s.w = None
        s.r = []


class Ins:
    __slots__ = ('eng', 'fn', 'deps', 'inc', 'val', 'sem', 'isdma')

    def __init__(s, eng, fn, isdma):
        s.eng = eng
        s.fn = fn
        s.deps = []
        s.inc = isdma
        s.val = None
        s.sem = None
        s.isdma = isdma


class _Rec:
    def __getattr__(s, name):
        def f(*a, **kw):
            s.call = (name, a, kw)
            return s
        return f


class Sched:
    COMPUTE = ('tensor', 'vector', 'scalar', 'gpsimd')

    def __init__(s, nc, ndma_sems=12):
        s.nc = nc
        s.streams = {e: [] for e in ('tensor', 'vector', 'scalar', 'gpsimd', 'sync')}
        s.ndma = ndma_sems
        s.dma_hist = []
        s.excl = set()

    def op(s, eng, fn, reads=(), writes=(), dma=False, accum=False):
        if fn is not None:
            rec = _Rec()
            fn(rec)
            fn = rec.call
        ins = Ins(eng, fn, dma)
        deps = {}

        def add(d):
            if d is None or d is ins:
                return
            deps[id(d)] = d
        if s.excl:
            extra = [r for r in reads if id(r) in s.excl and not any(r is w for w in writes)]
            if extra:
                writes = list(writes) + extra
        for r in reads:
            add(r.w)
        for w in writes:
            if w.w is not None and not (accum and w.w.eng == eng and not w.w.isdma):
                add(w.w)
            for rr in w.r:
                add(rr)
        if dma:
            k = len(s.dma_hist)
            if k >= s.ndma:
                add(s.dma_hist[k - s.ndma])
            s.dma_hist.append(ins)
        for d in deps.values():
            d.inc = True
        ins.deps = list(deps.values())
        for r in reads:
            r.r.append(ins)
        for w in writes:
            w.w = ins
            if not accum:
                w.r = []
        s.streams[eng].append(ins)
        return ins

    def barrier(s):
        marks = []
        for e in s.COMPUTE:
            if s.streams[e]:
                marks.append(s.streams[e][-1])
        marks += s.dma_hist[-s.ndma:]
        for m in marks:
            m.inc = True
        for e in list(s.streams):
            ins = s.op(e, None)
            ins.deps = list(marks)

    def emit(s, stack):
        nc = s.nc
        sems = {}
        for e in s.COMPUTE:
            sems[e] = stack.enter_context(nc.semaphore('sem_' + e))
        dsems = [stack.enter_context(nc.semaphore('dsem%d' % i)) for i in range(s.ndma)]
        for e in s.COMPUTE:
            c = 0
            for ins in s.streams[e]:
                if ins.inc and ins.fn is not None:
                    c += 1
                ins.val = c
                ins.sem = sems[e]
        for k, ins in enumerate(s.dma_hist):
            ins.sem = dsems[k % s.ndma]
            ins.val = 16 * (k // s.ndma + 1)
        block = stack.enter_context(nc.Block())

        def make(ename):
            stream = s.streams[ename]

            def body(eng):
                seen = {}
                for ins in stream:
                    need = {}
                    for d in ins.deps:
                        if d.fn is None or not d.val:
                            continue
                        key = id(d.sem)
                        if seen.get(key, 0) >= d.val:
                            continue
                        if key not in need or need[key][1] < d.val:
                            need[key] = (d.sem, d.val)
                    for key, (sm, v) in need.items():
                        eng.wait_ge(sm, v)
                        seen[key] = v
                    if ins.fn is None:
                        continue
                    nm_, a_, kw_ = ins.fn
                    r = getattr(eng, nm_)(*a_, **kw_)
                    if ins.inc:
                        r.then_inc(ins.sem, 16 if ins.isdma else 1)
            return body
        block.sync(make('sync'))
        block.tensor(make('tensor'))
        block.vector(make('vector'))
        block.scalar(make('scalar'))
        block.gpsimd(make('gpsimd'))


def _consts():
    c = {}
    c['ident'] = np.eye(128, dtype=np.float32)
    t = np.arange(T, dtype=np.float32)
    inv = (10000.0 ** (-np.arange(32, dtype=np.float32) / 32)).astype(np.float32)
    ang = t[:, None] * inv[None, :]
    tab = np.zeros((NT, 128, 128), np.float32)
    tab[:, :, 0:32] = np.cos(ang).reshape(NT, 128, 32)
    tab[:, :, 32:64] = np.sin(ang).reshape(NT, 128, 32)
    tt = np.arange(T)
    blk = np.arange(64)
    valid = blk[None, :] * 64 <= tt[:, None]
    force = (blk[None, :] == (tt // 64)[:, None]) | (blk[None, :] == 0)
    sb = np.where(valid, np.where(force, 1e4, 0.0), -1e30).astype(np.float32)
    tab[:, :, 64:128] = sb.reshape(NT, 128, 64)
    c['tab'] = tab
    h = np.arange(8, dtype=np.float64)
    logg = np.log1p(-np.exp2(-5.0 - h))
    pos = np.arange(128, dtype=np.float64)
    gq = np.exp(logg[None, :] * (pos[:, None] + 1.0))
    gk = np.exp(-logg[None, :] * (pos[:, None] + 1.0)) * 64 ** -0.5
    c['gqk'] = np.concatenate([gq, gk], axis=1).astype(np.float32)
    gC = np.exp(logg * 128.0)
    gct = np.zeros((128, 4, 64), np.float32)
    for a in range(2):
        for m in range(4):
            gct[a * 64:(a + 1) * 64, m, :] = gC[2 * m + a]
    c['gct'] = gct.reshape(128, 256)
    s = np.arange(128)
    caus = (s[:, None] <= s[None, :]).astype(np.float32)
    anti = (s[:, None] > s[None, :]).astype(np.float32)
    c['masks'] = np.concatenate([caus, anti], axis=1)
    n = np.arange(256)
    cm = np.zeros((NT, 128, 2, 128), np.float32)
    for i in range(NT):
        q = i * 128 + np.arange(128)
        v = (16 * n[:, None] + 31 <= q[None, :]) & (n[:, None] < 255)
        cm[i] = v.reshape(2, 128, 128).transpose(1, 0, 2)
    c['cmask'] = cm.reshape(NT, 128, 256)
    E = (np.arange(T)[None, :] // 64 == np.arange(64)[:, None]).astype(np.float32)
    c['ee'] = np.concatenate([E, E], axis=0)
    posn = np.arange(255)[:, None] * 16 + np.arange(32)[None, :]
    b = posn // 64
    M = (b[:, :, None] == np.arange(64)[None, None, :]).mean(axis=1).astype(np.float32)
    Mp = np.zeros((256, 64), np.float32)
    Mp[:255] = M
    c['mov'] = Mp.reshape(2, 128, 64).transpose(1, 0, 2).reshape(128, 128).copy()
    return c


CONST_SHAPES = {
    'ident': [128, 128], 'tab': [NT, 128, 128], 'gqk': [128, 16], 'gct': [128, 256],
    'masks': [128, 256], 'cmask': [NT, 128, 256], 'ee': [128, T], 'mov': [128, 128],
}
IN_SHAPES = {
    'x': [T, D], 'xT': [D, T], 'w1': [D, NCOL1], 'w2': [D, NCOL2], 'wout': [D, D],
    'normw': [128, 8], 'retw': [128, 512], 'qw': [128, 512], 'kcw': [128, 128],
    'ksw': [128, 128], 'kww': [128, 128], 'bg': [128, 24], 'pos': [128, 32],
    'cw1k': [2048, 256], 'cw1v': [2048, 256], 'cw2': [128, 256],
}


class _Stop(Exception):
    pass


def build_nc(stop=None):
    nc = bass.Bass("TRN2", target_bir_lowering=False)
    A = {}
    for k, shp in list(IN_SHAPES.items()) + list(CONST_SHAPES.items()):
        A[k] = nc.dram_tensor(k, shp, F32, kind="ExternalInput").ap()
    out = nc.dram_tensor("out", [T, D], F32, kind="ExternalOutput").ap()
    dbg = {}
    with contextlib.ExitStack() as st:
        S = Sched(nc)

        def ck(name):
            if stop == name:
                raise _Stop()

        def sb(name, shape, dt):
            return st.enter_context(nc.sbuf_tensor(name, shape, dt))

        WIN2 = sb("win2", [128, 8, NCOL2], BF16)
        WOUT = sb("wout_sb", [128, 8, D], BF16)
        KSA = sb("ksa", [128, 2, T], BF16)
        KWT = sb("kwt", [128, T], BF16)
        VS = sb("vs", [128, NT, 2, 65], BF16)
        VW = sb("vw", [128, NT, 2, 65], BF16)
        VCM = sb("vcm", [128, 2, 2, 65], BF16)
        MOV = sb("movb", [128, 2, 64], BF16)
        KCT = sb("kct", [128, 256], BF16)
        XS = [sb("xs%d" % i, [128, D], F32) for i in range(2)]
        XT = [sb("xt%d" % i, [128, 8, 128], F32) for i in range(2)]
        XB = [sb("xb%d" % i, [128, 8, 128], BF16) for i in range(2)]
        RSTD = sb("rstd", [128, NT], F32)
        NRSTD = sb("nrstd", [128, NT], F32)
        IDF = sb("idf", [128, 128], F32)
        IDB = sb("idb", [128, 128], BF16)
        NORMW = sb("normw_sb", [128, 8], F32)
        RETW = sb("retw_sb", [128, 512], F32)
        QW = sb("qw_sb", [128, 512], F32)
        KCW = sb("kcw_sb", [128, 128], F32)
        KSW = sb("ksw_sb", [128, 128], F32)
        KWW = sb("kww_sb", [128, 128], F32)
        BG = sb("bg_sb", [128, 24], F32)
        GQK = sb("gqk_sb", [128, 16], F32)
        GCT = sb("gct_sb", [128, 256], F32)
        MASKS = sb("masks_sb", [128, 256], F32)
        TAB = [sb("tab%d" % i, [128, 128], F32) for i in range(2)]
        CMK = [sb("cmk%d" % i, [128, 256], F32) for i in range(2)]
        RST = sb("rstate", [128, 4, 64], F32)
        RSB = sb("rstate_b", [128, 4, 64], BF16)
        MB = sb("maskbias", [128, 2, 512], BF16)
        ARENA = sb("arena", [128, 33792], BF16)
        pbs = [st.enter_context(nc.psum_tensor("pb%d" % i, [128, 512], F32)) for i in range(8)]
        PR = [Res("pb%d" % i) for i in range(8)]
        S.excl = set(id(r) for r in PR)

        class Carver:
            def __init__(s):
                s.off = 0

            def get(s, n_el, dt):
                nb = n_el * (2 if dt == BF16 else 4)
                nb16 = nb // 2
                a = ARENA[:, s.off:s.off + nb16]
                s.off += nb16
                assert s.off <= 33792, s.off
                if dt == F32:
                    a = a.bitcast(F32)
                return a

        R = {}

        def res(n):
            if n not in R:
                R[n] = Res(n)
            return R[n]
        r_ks = [Res("ks%d" % j) for j in range(NT)]
        r_kw = [Res("kw%d" % j) for j in range(NT)]
        r_vs = [Res("vs%d" % j) for j in range(NT)]
        r_vw = [Res("vw%d" % j) for j in range(NT)]
        r_ct = [Res("ct%d" % j) for j in range(NT)]
        r_xs = [Res("xs0"), Res("xs1")]
        r_xt = [Res("xt0"), Res("xt1")]
        r_xb = [Res("xb0"), Res("xb1")]
        r_tab = [Res("tab0"), Res("tab1")]
        r_cmk = [Res("cmk0"), Res("cmk1")]

        def dma(o, i, reads=(), writes=()):
            return S.op('sync', lambda e: e.dma_start(out=o, in_=i), reads=reads, writes=writes, dma=True)

        def V(fn, reads=(), writes=()):
            return S.op('vector', fn, reads=reads, writes=writes)

        def G(fn, reads=(), writes=()):
            return S.op('gpsimd', fn, reads=reads, writes=writes)

        def ACT(fn, reads=(), writes=()):
            return S.op('scalar', fn, reads=reads, writes=writes)

        def PE(fn, reads=(), writes=(), accum=False):
            return S.op('tensor', fn, reads=reads, writes=writes, accum=accum)

        def ldc(name, tile, r):
            dma(tile[:], A[name], writes=[r])

        def record():
            for nm, tl in (('ident', IDF), ('normw', NORMW), ('retw', RETW), ('qw', QW), ('kcw', KCW),
                           ('ksw', KSW), ('kww', KWW), ('bg', BG), ('gqk', GQK), ('gct', GCT), ('masks', MASKS)):
                ldc(nm, tl, res(nm))
            V(lambda e: e.tensor_copy(out=IDB[:], in_=IDF[:]), [res('ident')], [res('idb')])
            V(lambda e: e.tensor_scalar(out=MB[:].rearrange("p m (h q) -> p m h q", h=4),
                                        in0=MASKS[:].rearrange("p (m q) -> p m q", m=2)[:, :, None, :].to_broadcast([128, 2, 4, 128]),
                                        scalar1=-1.0, scalar2=-NEGB, op0=ALU.add, op1=ALU.mult), [res('masks')], [res('mb')])
            G(lambda e: e.memset(VS[:].rearrange("p a b c -> p (a b c)"), 1.0), [], r_vs)
            G(lambda e: e.memset(VW[:].rearrange("p a b c -> p (a b c)"), 1.0), [], r_vw)
            G(lambda e: e.memset(VCM[:].rearrange("p a b c -> p (a b c)"), 1.0), [], [res('vcm')])
            G(lambda e: e.memset(RST[:].rearrange("p a b -> p (a b)"), 0.0), [], [res('rst')])
            G(lambda e: e.memset(RSB[:].rearrange("p a b -> p (a b)"), 0.0), [], [res('rsb')])

            cnt = [0]

            SLOTS = [XS[0][:, :], XS[1][:, :], XT[0][:].rearrange("p c t -> p (c t)"), XT[1][:].rearrange("p c t -> p (c t)")]
            r_sl = [r_xs[0], r_xs[1], r_xt[0], r_xt[1]]

            def stage_cast(dst, srcs, scale=None, rd=(), wr=(), nslots=4):
                k = cnt[0] % nslots
                cnt[0] += 1
                SL = SLOTS[k]
                for (p0, p1, c0, c1, src) in srcs:
                    dma(SL[p0:p1, c0:c1], src, writes=[r_sl[k]])
                p0 = min(s_[0] for s_ in srcs)
                p1 = max(s_[1] for s_ in srcs)
                c1 = max(s_[3] for s_ in srcs)
                if cnt[0] % 2 == 0:
                    if scale is None:
                        V(lambda e: e.tensor_copy(out=dst, in_=SL[p0:p1, 0:c1]), [r_sl[k]] + list(rd), list(wr))
                    else:
                        V(lambda e: e.tensor_scalar(out=dst, in0=SL[p0:p1, 0:c1], scalar1=scale, scalar2=None,
                                                    op0=ALU.mult), [r_sl[k]] + list(rd), list(wr))
                else:
                    if scale is None:
                        ACT(lambda e: e.activation(out=dst, in_=SL[p0:p1, 0:c1], func=AF.Copy), [r_sl[k]] + list(rd), list(wr))
                    else:
                        ACT(lambda e: e.activation(out=dst, in_=SL[p0:p1, 0:c1], func=AF.Copy, scale=scale),
                            [r_sl[k]] + list(rd), list(wr))

            ca = Carver()
            CTS = ca.get(2 * T, BF16).rearrange("p (g r m) -> p g r m", g=2, r=16)
            W1 = ca.get(32 * 256, BF16).rearrange("p (l h) -> p l h", l=32)
            HTF = ca.get(2 * 2 * 2 * 256, BF16)
            HT = HTF.rearrange("p (k g c n) -> p k g c n", k=2, g=2, c=2)
            WIN1 = ca.get(8 * NCOL1, BF16).rearrange("p (c n) -> p c n", c=8)
            W2C = ca.get(256, BF16).rearrange("p (k c d) -> p k c d", k=2, c=2)
            POSB = ca.get(32, BF16)
            KV1 = ca.get(NCOL1, F32)
            KVB = ca.get(512, BF16)
            SQ1 = ca.get(256, F32)
            ST1 = ca.get(8, F32)
            ZT = ca.get(256, F32)
            ET = ca.get(256, F32)
            CBIAS = ca.get(4, F32)
            KCTM = ca.get(128, BF16)
            SQX = ca.get(1024, BF16)
            ONESB = ca.get(2, BF16)

            ck('consts')
            for c in range(8):
                rows = slice(c * 128, (c + 1) * 128)
                stage_cast(WIN1[:, c, :], [(0, 128, 0, NCOL1, A['w1'][rows, :])], scale=NORMW[:, c:c + 1],
                           rd=[res('normw')], wr=[res('win1')])
            W2STEPS = []
            for c in range(8):
                rows = slice(c * 128, (c + 1) * 128)
                for p in range(4):
                    c0 = p * 1024
                    c1 = min(NCOL2, c0 + 1024)
                    W2STEPS.append((WIN2[:, c, c0:c1], [(0, 128, 0, c1 - c0, A['w2'][rows, c0:c1])], NORMW[:, c:c + 1],
                                    [res('normw')], [res('win2')]))
            for c in range(8):
                rows = slice(c * 128, (c + 1) * 128)
                W2STEPS.append((WOUT[:, c, :], [(0, 128, 0, D, A['wout'][rows, :])], None, [], [res('wout')]))
            w1k = A['cw1k'].rearrange("(l d) h -> d l h", d=64)
            w1v = A['cw1v'].rearrange("(l d) h -> d l h", d=64)
            for p in range(8):
                k = cnt[0] % 2
                cnt[0] += 1
                xsv = XS[k][:, :].rearrange("p (l h) -> p l h", l=4)
                dma(xsv[0:64], w1k[:, p * 4:(p + 1) * 4, :], writes=[r_xs[k]])
                dma(xsv[64:128], w1v[:, p * 4:(p + 1) * 4, :], writes=[r_xs[k]])
                V(lambda e, k=k, p=p, xsv=xsv: e.tensor_copy(out=W1[:, p * 4:(p + 1) * 4, :], in_=xsv),
                  [r_xs[k]], [res('w1')])
            for p in range(4):
                k = cnt[0] % 2
                cnt[0] += 1
                cs = slice(p * 1024, (p + 1) * 1024)
                dma(XS[k][:, :], A['ee'][:, cs], writes=[r_xs[k]])
                V(lambda e, k=k, cs=cs: e.tensor_copy(out=KSA[64:128, 0, cs], in_=XS[k][64:128, :]), [r_xs[k]], [res('ksa_e')])
                ACT(lambda e, k=k, cs=cs: e.activation(out=KSA[0:64, 1, cs], in_=XS[k][0:64, :], func=AF.Copy), [r_xs[k]], [res('ksa_e')])
            k = cnt[0] % 2
            cnt[0] += 1
            dma(XS[k][:, 0:256], A['cw2'], writes=[r_xs[k]])
            dma(XS[k][:, 256:288], A['pos'], writes=[r_xs[k]])
            dma(XS[k][:, 288:416], A['mov'], writes=[r_xs[k]])
            V(lambda e, k=k: e.tensor_copy(out=W2C.rearrange("p k c d -> p (k c d)"), in_=XS[k][:, 0:256]), [r_xs[k]], [res('w2c')])
            V(lambda e, k=k: e.tensor_copy(out=POSB, in_=XS[k][:, 256:288]), [r_xs[k]], [res('posb')])
            V(lambda e, k=k: e.tensor_copy(out=MOV[:].rearrange("p a b -> p (a b)"), in_=XS[k][:, 288:416]), [r_xs[k]], [res('mov')])

            ck('weights')
            xTd = A['xT'].rearrange("(c p) t -> p c t", p=128)
            pbk = [0]
            pbset = [0, 1, 2]

            def nextpb():
                k = pbset[pbk[0] % len(pbset)]
                pbk[0] += 1
                return k

            def rsqrt_act(dst, src, scale, rd, wr):
                ACT(lambda e: e.activation(out=dst, in_=src, func=AF.Ln, scale=scale, bias=EPS), rd, wr)
                ACT(lambda e: e.activation(out=dst, in_=dst, func=AF.Exp, scale=-0.5), wr, wr)

            def loads1(i):
                k = i % 2
                ts = slice(i * 128, (i + 1) * 128)
                dma(XT[k][:], xTd[:, :, ts], writes=[r_xt[k]])

            def cast1(i):
                k = i % 2
                V(lambda e: e.tensor_copy(out=XB[k][:], in_=XT[k][:]), [r_xt[k]], [r_xb[k]])
            V(lambda e: e.memset(ONESB, 1.0), [], [res('onesb')])
            loads1(0)
            cast1(0)
            for i in range(NT):
                k = i % 2
                ts = slice(i * 128, (i + 1) * 128)
                if i + 1 < NT:
                    loads1(i + 1)
                ck('p1a')
                ACT(lambda e, k=k: e.activation(out=SQX, in_=XT[k][:].rearrange("p c t -> p (c t)"), func=AF.Square),
                    [r_xt[k]], [res('sqx')])
                bq = nextpb()
                for c in range(8):
                    PE(lambda e, c=c, bq=bq: e.matmul(pbs[bq][:, 0:1], lhsT=SQX[:, c * 128:(c + 1) * 128], rhs=ONESB[:, 0:1],
                                                      start=(c == 0), stop=(c == 7)),
                       [res('sqx'), res('onesb')], [PR[bq]], accum=(c > 0))
                rsqrt_act(RSTD[:, i:i + 1], pbs[bq][:, 0:1], 1.0 / D, [PR[bq]], [res('rstd%d' % i)])
                V(lambda e, i=i: e.tensor_scalar(out=NRSTD[:, i:i + 1], in0=RSTD[:, i:i + 1], scalar1=-1.0, scalar2=None, op0=ALU.mult),
                  [res('rstd%d' % i)], [res('nrstd%d' % i)])
                ck('p1b')
                b0 = nextpb()
                b1 = nextpb()
                for c in range(8):
                    PE(lambda e, c=c, k=k, b0=b0: e.matmul(pbs[b0][:, 0:384], lhsT=XB[k][:, c, :], rhs=WIN1[:, c, 0:384],
                                                          start=(c == 0), stop=(c == 7)),
                       [r_xb[k], res('win1')], [PR[b0]], accum=(c > 0))
                for c in range(8):
                    PE(lambda e, c=c, k=k, b1=b1: e.matmul(pbs[b1][:, 0:384], lhsT=XB[k][:, c, :], rhs=WIN1[:, c, 384:768],
                                                          start=(c == 0), stop=(c == 7)),
                       [r_xb[k], res('win1')], [PR[b1]], accum=(c > 0))
                rs = RSTD[:, i:i + 1]
                V(lambda e, b0=b0, rs=rs: e.tensor_scalar(out=KV1[:, 0:384], in0=pbs[b0][:, 0:384], scalar1=rs, scalar2=None,
                                                          op0=ALU.mult), [PR[b0], res('rstd%d' % i)], [res('kv1a')])
                V(lambda e, b1=b1, rs=rs: e.tensor_scalar(out=KV1[:, 384:768], in0=pbs[b1][:, 0:384], scalar1=rs, scalar2=None,
                                                          op0=ALU.mult), [PR[b1], res('rstd%d' % i)], [res('kv1b')])
                if i + 1 < NT:
                    cast1(i + 1)
                ck('p1c')
                ACT(lambda e: e.activation(out=KVB[:, 0:256], in_=KV1[:, 0:256], func=AF.Copy), [res('kv1a')], [res('kvb_c')])
                ACT(lambda e: e.activation(out=SQ1, in_=KV1[:, 256:512], func=AF.Square), [res('kv1a'), res('kv1b')], [res('sq1')])
                V(lambda e: e.tensor_reduce(out=ST1[:, 0:4], in_=SQ1.rearrange("p (a b) -> p a b", b=64), axis=AX.X, op=ALU.add),
                  [res('sq1')], [res('st1')])
                rsqrt_act(ST1[:, 0:4], ST1[:, 0:4], 1.0 / 64, [res('st1')], [res('st1')])
                V(lambda e: e.tensor_tensor(out=SQ1.rearrange("p (a b) -> p a b", b=64),
                                            in0=KV1[:, 256:512].rearrange("p (a b) -> p a b", b=64),
                                            in1=ST1[:, 0:4, None].to_broadcast([128, 4, 64]), op=ALU.mult),
                  [res('kv1a'), res('kv1b'), res('st1')], [res('sq1')])
                V(lambda e: e.tensor_tensor(out=KVB[:, 256:384], in0=SQ1[:, 0:128], in1=KSW[:], op=ALU.mult),
                  [res('sq1'), res('ksw')], [res('kvb_s')])
                V(lambda e: e.tensor_tensor(out=KVB[:, 384:512], in0=SQ1[:, 128:256], in1=KWW[:], op=ALU.mult),
                  [res('sq1'), res('kww')], [res('kvb_w')])
                ACT(lambda e, i=i: e.activation(out=VS[:, i, :, 0:64], in_=KV1[:, 512:640].rearrange("p (g d) -> p g d", g=2), func=AF.Copy),
                  [res('kv1b')], [r_vs[i]])
                ACT(lambda e, i=i: e.activation(out=VW[:, i, :, 0:64], in_=KV1[:, 640:768].rearrange("p (g d) -> p g d", g=2), func=AF.Copy),
                  [res('kv1b')], [r_vw[i]])
                ck('p1d')
                bt = nextpb()
                ptb = pbs[bt][:, :].bitcast(BF16)
                for m, rr in enumerate(('kvb_c', 'kvb_c', 'kvb_s', 'kvb_w')):
                    PE(lambda e, m=m, ptb=ptb: e.transpose(out=ptb[:, m * 128:(m + 1) * 128], in_=KVB[:, m * 128:(m + 1) * 128],
                                                           identity=IDB[:]),
                       [res(rr), res('idb')], [PR[bt]], accum=(m > 0))
                ck('p1e')
                V(lambda e, ptb=ptb, i=i: e.tensor_copy(out=CTS[:, :, :, i * 8:(i + 1) * 8],
                                                        in_=ptb[:, 0:256].rearrange("p (g m r) -> p g r m", g=2, r=16)),
                  [PR[bt]], [r_ct[i]])
                ck('p1f')
                V(lambda e, ptb=ptb, ts=ts: e.tensor_copy(out=KSA[0:64, 0, ts], in_=ptb[0:64, 256:384]),
                  [PR[bt]], [r_ks[i]])
                V(lambda e, ptb=ptb, ts=ts: e.tensor_copy(out=KSA[64:128, 1, ts], in_=ptb[64:128, 256:384]),
                  [PR[bt]], [r_ks[i]])
                ck('p1g')
                V(lambda e, ptb=ptb, ts=ts: e.tensor_copy(out=KWT[:, ts], in_=ptb[:, 384:512]), [PR[bt]], [r_kw[i]])
                for st_ in W2STEPS[i * len(W2STEPS) // NT:(i + 1) * len(W2STEPS) // NT]:
                    stage_cast(st_[0], st_[1], scale=st_[2], rd=st_[3], wr=st_[4], nslots=2)
                ck('p1t%d' % i)

            ck('pass1')
            for kind in range(2):
                bb = nextpb()
                rows = slice(kind * 64, (kind + 1) * 64)
                for hc in range(2):
                    col = kind * 2 + hc
                    for l in range(32):
                        PE(lambda e, rows=rows, hc=hc, l=l, col=col, bb=bb: e.matmul(
                            pbs[bb][:, col:col + 1], lhsT=W1[rows, l, hc * 128:(hc + 1) * 128], rhs=POSB[rows, l:l + 1],
                            start=(l == 0), stop=(l == 31)),
                           [res('w1'), res('posb')], [PR[bb]], accum=not (hc == 0 and l == 0))
                V(lambda e, bb=bb, kind=kind: e.tensor_copy(out=CBIAS[:, kind * 2:kind * 2 + 2], in_=pbs[bb][:, kind * 2:kind * 2 + 2]),
                  [PR[bb]], [res('cbias')])
            ck('c1')
            ck('c1b')
            G(lambda e: e.memset(HTF, 0.0), [], [res('ht')])
            for kind in range(2):
                rows = slice(kind * 64, (kind + 1) * 64)
                for g in range(2):
                    for hc in range(2):
                        b = nextpb()
                        for l in range(32):
                            PE(lambda e, rows=rows, g=g, hc=hc, l=l, b=b: e.matmul(
                                pbs[b][:, 0:255], lhsT=W1[rows, l, hc * 128:(hc + 1) * 128],
                                rhs=CTS[rows, g, l % 16, (l // 16):(l // 16) + 255], start=(l == 0), stop=(l == 31)),
                               [res('w1')] + r_ct, [PR[b]], accum=(l > 0))
                        ck('c2')
                        col = kind * 2 + hc
                        ACT(lambda e, b=b, col=col, kind=kind, g=g, hc=hc: e.activation(
                            out=HT[:, kind, g, hc, 0:255], in_=pbs[b][:, 0:255], func=AF.Silu, bias=CBIAS[:, col:col + 1]),
                            [PR[b], res('cbias')], [res('ht')])
            ck('c3')
            for nt_ in range(2):
                ns = slice(nt_ * 128, (nt_ + 1) * 128)
                b = nextpb()
                for kind in range(2):
                    for g in range(2):
                        for hc in range(2):
                            cs = slice(kind * 128 + g * 64, kind * 128 + g * 64 + 64)
                            PE(lambda e, kind=kind, g=g, hc=hc, cs=cs, ns=ns, b=b: e.matmul(
                                pbs[b][:, cs], lhsT=HT[:, kind, g, hc, ns], rhs=W2C[:, kind, hc, :],
                                start=(hc == 0), stop=(hc == 1)),
                               [res('ht'), res('w2c')], [PR[b]], accum=not (kind == 0 and g == 0 and hc == 0))
                ck('c4')
                V(lambda e, b=b, nt_=nt_: e.tensor_copy(out=VCM[:, nt_, :, 0:64],
                                                       in_=pbs[b][:, 128:256].rearrange("p (g d) -> p g d", g=2)),
                  [PR[b]], [res('vcm')])
                ck('c5')
                ACT(lambda e, b=b: e.activation(out=SQ1[:, 0:128], in_=pbs[b][:, 0:128], func=AF.Square), [PR[b]], [res('sq1')])
                ck('c6')
                V(lambda e: e.tensor_reduce(out=ST1[:, 0:2], in_=SQ1[:, 0:128].rearrange("p (a b) -> p a b", b=64), axis=AX.X,
                                            op=ALU.add), [res('sq1')], [res('st1')])
                rsqrt_act(ST1[:, 0:2], ST1[:, 0:2], 1.0 / 64, [res('st1')], [res('st1')])
                V(lambda e, b=b: e.tensor_tensor(out=SQ1[:, 0:128].rearrange("p (a b) -> p a b", b=64),
                                                 in0=pbs[b][:, 0:128].rearrange("p (a b) -> p a b", b=64),
                                                 in1=ST1[:, 0:2, None].to_broadcast([128, 2, 64]), op=ALU.mult),
                  [PR[b], res('st1')], [res('sq1')])
                V(lambda e: e.tensor_tensor(out=KCTM, in0=SQ1[:, 0:128], in1=KCW[:], op=ALU.mult),
                  [res('sq1'), res('kcw')], [res('kctm')])
                ck('c7')
                bt = nextpb()
                ptb = pbs[bt][:, :].bitcast(BF16)
                PE(lambda e, ptb=ptb: e.transpose(out=ptb[:, 0:128], in_=KCTM, identity=IDB[:]),
                   [res('kctm'), res('idb')], [PR[bt]])
                V(lambda e, ptb=ptb, ns=ns: e.tensor_copy(out=KCT[:, ns], in_=ptb[:, 0:128]), [PR[bt]], [res('kct')])

            ck('compress')
            S.barrier()

            pbset[:] = [0, 1]
            ABF = [2, 3]
            ABB = [4, 5]
            UBB = 6
            UF = 7
            cb = Carver()
            RQK = cb.get(512, F32)
            TMP1 = cb.get(256, F32)
            TMP2 = cb.get(256, F32)
            ROT = cb.get(512, F32)
            QKB = [cb.get(1024, BF16) for _ in range(2)]
            VTM = [cb.get(512, BF16) for _ in range(2)]
            QKT = [cb.get(1024, BF16) for _ in range(2)]
            STB = cb.get(1024, BF16)
            GG = [cb.get(1024, BF16) for _ in range(3)]
            NQ = cb.get(512, F32)
            GT = NQ
            SQ = cb.get(512, F32)
            SQB = cb.get(512, F32)
            QN = cb.get(512, BF16)
            QA = [cb.get(1024, BF16).rearrange("p (v k q) -> p v k q", v=2, k=4) for _ in range(3)]
            NP = 6
            PB_ = [cb.get(512, BF16) for _ in range(NP)]
            USBF = [cb.get(2 * 2 * 260, F32).rearrange("p (x g c) -> p x g c", x=2, g=2) for _ in range(2)]
            USBS = cb.get(2 * 260, F32).rearrange("p (g c) -> p g c", g=2)
            IMP = cb.get(128, F32)
            SCO = cb.get(128, F32)
            SC2 = cb.get(128, F32)
            M8 = cb.get(32, F32)
            SELB = cb.get(128, BF16)
            ONS = cb.get(512, F32)
            ORT = cb.get(512, F32)
            YB = [cb.get(1024, BF16) for _ in range(2)]
            YT = cb.get(1024, BF16).rearrange("p (c t) -> p c t", c=8)
            GL = [cb.get(24, F32) for _ in range(3)]
            CO = cb.get(24, F32)
            DEN = cb.get(24, F32)
            SS = cb.get(16, F32)
            r_p = [Res("p%d" % i) for i in range(NP)]
            pcount = [0]
            COLS = {'rq': 0, 'rk': 512, 'rv': 1024, 'rg': 1536, 'nq': 2048, 'ng': 2560, 'gl': 3072}
            SCALE = 0.125

            def proj(k, c0, n, b):
                for c in range(8):
                    PE(lambda e, c=c: e.matmul(pbs[b][:, 0:n], lhsT=XB[k][:, c, :], rhs=WIN2[:, c, c0:c0 + n],
                                               start=(c == 0), stop=(c == 7)),
                       [r_xb[k], res('win2')], [PR[b]], accum=(c > 0))

            def run_blocks(blocks, abset):
                nb = len(blocks)
                if nb == 0:
                    return
                abl = [None] * nb

                def qk(t):
                    ab = abset[t % 2]
                    bl = blocks[t]
                    PE(lambda e: e.matmul(pbs[ab][:, :], lhsT=bl[0], rhs=bl[2], start=True, stop=(bl[4] is None)),
                       list(bl[1]) + list(bl[3]), [PR[ab]])
                    if bl[4] is not None:
                        PE(lambda e: e.matmul(pbs[ab][:, :], lhsT=IDB[:], rhs=bl[4], start=False, stop=True),
                           [res('idb'), res('mb')], [PR[ab]], accum=True)
                    abl[t] = ab
                qk(0)
                for t in range(nb):
                    if t + 1 < nb:
                        qk(t + 1)
                    (lhsT_, lres_, rhs_, rres_, mask_pe, mask_ap, mask_res, vfn, v_res, ub, first, last, after) = blocks[t]
                    ab = abl[t]
                    pk = pcount[0] % NP
                    pcount[0] += 1
                    P = PB_[pk]
                    ACT(lambda e: e.activation(out=P, in_=pbs[ab][:, :], func=AF.Exp, scale=SCALE), [PR[ab]], [r_p[pk]])
                    if mask_ap is not None:
                        V(lambda e: e.tensor_tensor(out=P.rearrange("p (h q) -> p h q", h=4), in0=P.rearrange("p (h q) -> p h q", h=4),
                                                    in1=mask_ap[:, None, :].to_broadcast([128, 4, 128]), op=ALU.mult),
                          [r_p[pk]] + list(mask_res), [r_p[pk]])
                    for h in range(4):
                        PE(lambda e, h=h: e.matmul(pbs[ub][:, h * 65:(h + 1) * 65], lhsT=P[:, h * 128:(h + 1) * 128], rhs=vfn,
                                                   start=(first and h == 0), stop=(last and h == 3)),
                           [r_p[pk]] + list(v_res), [PR[ub]], accum=not (first and h == 0))
                    if after is not None:
                        after(P, pk)
                    yield 0.75

            def load_xs(i):
                k = i % 2
                dma(XS[k][:, :], A['x'][i * 128:(i + 1) * 128, :], writes=[r_xs[k]])

            def load_xt(i):
                k = i % 2
                dma(XT[k][:], xTd[:, :, i * 128:(i + 1) * 128], writes=[r_xt[k]])

            def load_tabs(i):
                k = i % 2
                dma(TAB[k][:], A['tab'][i], writes=[r_tab[k]])
                dma(CMK[k][:], A['cmask'][i], writes=[r_cmk[k]])

            def cast_xb(i):
                k = i % 2
                V(lambda e: e.tensor_copy(out=XB[k][:], in_=XT[k][:]), [r_xt[k]], [r_xb[k]])

            def stageA(i):
                k = i % 2
                k3 = i % 3
                rs = RSTD[:, i:i + 1]
                r_rs = res('rstd%d' % i)
                COS = TAB[k][:, 0:32]
                SIN = TAB[k][:, 32:64]
                gg = GG[k3]
                qa = QA[k3]
                gl = GL[k3]
                qkb = QKB[k]
                qkt = QKT[k]
                vtm = VTM[k]
                r_gate = res('gate%d' % k3)
                r_qaq = res('qa_q%d' % k3)
                r_gl = res('gl%d' % k3)
                r_qkt = res('qkt%d' % k)
                r_vtm = res('vtm%d' % k)
                nrs = NRSTD[:, i:i + 1]
                r_nrs = res('nrstd%d' % i)

                def rope(nm, off, gcol):
                    src = RQK.rearrange("p (h d) -> p h d", h=8)
                    x1 = src[:, :, 0:32]
                    x2 = src[:, :, 32:64]
                    cosb = COS[:, None, :].to_broadcast([128, 8, 32])
                    sinb = SIN[:, None, :].to_broadcast([128, 8, 32])
                    rot = ROT.rearrange("p (h d) -> p h d", h=8)
                    t1 = TMP1.rearrange("p (h d) -> p h d", h=8)
                    t2 = TMP2.rearrange("p (h d) -> p h d", h=8)
                    rr = [res('rqk'), r_tab[k]]
                    V(lambda e: e.tensor_tensor(out=t1, in0=x1, in1=cosb, op=ALU.mult), rr, [res('t1')])
                    V(lambda e: e.tensor_tensor(out=t2, in0=x2, in1=sinb, op=ALU.mult), rr, [res('t2')])
                    V(lambda e: e.tensor_tensor(out=rot[:, :, 0:32], in0=t1, in1=t2, op=ALU.subtract),
                      [res('t1'), res('t2')], [res('rot')])
                    V(lambda e: e.tensor_tensor(out=t1, in0=x1, in1=sinb, op=ALU.mult), rr, [res('t1')])
                    V(lambda e: e.tensor_tensor(out=t2, in0=x2, in1=cosb, op=ALU.mult), rr, [res('t2')])
                    V(lambda e: e.tensor_tensor(out=rot[:, :, 32:64], in0=t1, in1=t2, op=ALU.add),
                      [res('t1'), res('t2')], [res('rot')])
                    gt = GQK[:, gcol:gcol + 8]
                    V(lambda e: e.tensor_tensor(out=qkb[:, off:off + 512].rearrange("p (h d) -> p h d", h=8), in0=rot,
                                                in1=gt[:, :, None].to_broadcast([128, 8, 64]), op=ALU.mult),
                      [res('rot'), res('gqk')], [res('qkb%s%d' % (nm, k))])

                b = nextpb()
                proj(k, COLS['rq'], 512, b)
                V(lambda e: e.tensor_scalar(out=RQK, in0=pbs[b][:, :], scalar1=rs, scalar2=None, op0=ALU.mult),
                  [PR[b], r_rs], [res('rqk')])
                rope('rq', 0, 0)
                b = nextpb()
                proj(k, COLS['rv'], 512, b)
                ACT(lambda e: e.activation(out=vtm, in_=pbs[b][:, :], func=AF.Copy, scale=rs), [PR[b], r_rs], [r_vtm])
                for nm, off in (('rg', 0), ('ng', 512)):
                    b = ABF[0] if nm == 'rg' else ABF[1]
                    proj(k, COLS[nm], 512, b)
                    ACT(lambda e: e.activation(out=GT, in_=pbs[b][:, :], func=AF.Exp, scale=nrs), [PR[b], r_nrs], [res('nq')])
                    ACT(lambda e: e.activation(out=GT, in_=GT, func=AF.Ln, bias=1.0), [res('nq')], [res('nq')])
                    ACT(lambda e: e.activation(out=GT, in_=GT, func=AF.Exp, scale=-1.0), [res('nq')], [res('nq')])
                    V(lambda e: e.scalar_tensor_tensor(out=gg[:, off:off + 512], in0=pbs[b][:, :], scalar=rs, in1=GT,
                                                       op0=ALU.mult, op1=ALU.mult), [PR[b], r_rs, res('nq')], [r_gate])
                b = nextpb()
                proj(k, COLS['gl'], 24, b)
                V(lambda e: e.scalar_tensor_tensor(out=gl, in0=pbs[b][:, 0:24], scalar=rs, in1=BG[:], op0=ALU.mult, op1=ALU.add),
                  [PR[b], r_rs, res('bg')], [r_gl])
                ACT(lambda e: e.activation(out=gl, in_=gl, func=AF.Exp, scale=-1.0), [r_gl], [r_gl])
                ACT(lambda e: e.activation(out=gl, in_=gl, func=AF.Ln, bias=1.0), [r_gl], [r_gl])
                ACT(lambda e: e.activation(out=gl, in_=gl, func=AF.Exp, scale=-1.0), [r_gl], [r_gl])
                b = nextpb()
                proj(k, COLS['nq'], 512, b)
                V(lambda e: e.tensor_scalar(out=NQ, in0=pbs[b][:, :], scalar1=rs, scalar2=None, op0=ALU.mult),
                  [PR[b], r_rs], [res('nq')])
                b = nextpb()
                proj(k, COLS['rk'], 512, b)
                V(lambda e: e.tensor_scalar(out=RQK, in0=pbs[b][:, :], scalar1=rs, scalar2=None, op0=ALU.mult),
                  [PR[b], r_rs], [res('rqk')])
                yield 6.0
                rope('rk', 512, 8)
                ACT(lambda e: e.activation(out=SQ, in_=NQ, func=AF.Square), [res('nq')], [res('sq')])
                V(lambda e: e.tensor_reduce(out=SS[:, 0:8], in_=SQ.rearrange("p (h d) -> p h d", h=8), axis=AX.X, op=ALU.add),
                  [res('sq')], [res('ss')])
                rsqrt_act(SS[:, 0:8], SS[:, 0:8], 1.0 / 64, [res('ss')], [res('ss')])
                V(lambda e: e.tensor_tensor(out=SQ.rearrange("p (h d) -> p h d", h=8), in0=NQ.rearrange("p (h d) -> p h d", h=8),
                                            in1=SS[:, 0:8, None].to_broadcast([128, 8, 64]), op=ALU.mult),
                  [res('nq'), res('ss')], [res('sq')])
                V(lambda e: e.tensor_tensor(out=QN, in0=SQ, in1=QW[:], op=ALU.mult), [res('sq'), res('qw')], [res('qn')])
                yield 4.0
                bt = nextpb()
                ptb = pbs[bt][:, :].bitcast(BF16)
                for m in range(8):
                    nm = 'rq' if m < 4 else 'rk'
                    PE(lambda e, m=m: e.transpose(out=ptb[:, m * 128:(m + 1) * 128], in_=qkb[:, m * 128:(m + 1) * 128],
                                                  identity=IDB[:]),
                       [res('qkb%s%d' % (nm, k)), res('idb')], [PR[bt]], accum=(m > 0))
                V(lambda e: e.tensor_copy(out=qkt, in_=ptb), [PR[bt]], [r_qkt])
                if i + 1 < NT:
                    cast_xb(i + 1)
                yield 0.5
                bt2 = nextpb()
                ptq = pbs[bt2][:, :].bitcast(BF16)
                for m in range(4):
                    PE(lambda e, m=m: e.transpose(out=ptq[:, m * 128:(m + 1) * 128], in_=QN[:, m * 128:(m + 1) * 128],
                                                  identity=IDB[:]),
                       [res('qn'), res('idb')], [PR[bt2]], accum=(m > 0))
                V(lambda e: e.tensor_copy(out=qa[0:64, 0].rearrange("p k q -> p (k q)"), in_=ptq[0:64, 0:512]),
                  [PR[bt2]], [r_qaq])
                V(lambda e: e.tensor_copy(out=qa[64:128, 1].rearrange("p k q -> p (k q)"), in_=ptq[64:128, 0:512]),
                  [PR[bt2]], [r_qaq])
                yield 0.5

            def stageB(i):
                k = i % 2
                k3 = i % 3
                gg = GG[k3]
                qa = QA[k3]
                yb = YB[k]
                usbf = USBF[k]
                qkb = QKB[k]
                qkt = QKT[k]
                vtm = VTM[k]
                r_gate = res('gate%d' % k3)
                r_qaq = res('qa_q%d' % k3)
                r_qas = res('qa_s%d' % k3)
                r_qkt = res('qkt%d' % k)
                r_vtm = res('vtm%d' % k)
                QsT = qkt[:, 0:512].rearrange("p (m t) -> p m t", m=4)
                KsT = qkt[:, 512:1024].rearrange("p (m t) -> p m t", m=4)
                Kstm = qkb[:, 512:1024]
                for half in range(2):
                    ab = ABF[half]
                    for hh in range(4):
                        h = 2 * hh + half
                        rows = slice((h % 2) * 64, (h % 2) * 64 + 64)
                        PE(lambda e, h=h, hh=hh, rows=rows: e.matmul(pbs[ab][:, hh * 128:(hh + 1) * 128], lhsT=KsT[rows, h // 2, :],
                                                                    rhs=QsT[rows, h // 2, :], start=True, stop=True),
                           [r_qkt], [PR[ab]], accum=(hh > 0))
                    V(lambda e: e.tensor_tensor(
                        out=STB[:, half * 512:(half + 1) * 512].rearrange("p (h q) -> p h q", h=4),
                        in0=pbs[ab][:, :].rearrange("p (h q) -> p h q", h=4),
                        in1=MASKS[:, None, 0:128].to_broadcast([128, 4, 128]), op=ALU.mult),
                      [PR[ab], res('masks')], [res('stb%d' % half)])
                yield 0.1
                ob = UF
                for h in range(8):
                    rows = slice((h % 2) * 64, (h % 2) * 64 + 64)
                    so = (h % 2) * 512 + (h // 2) * 128
                    PE(lambda e, h=h, so=so: e.matmul(pbs[ob][:, h * 64:(h + 1) * 64], lhsT=STB[:, so:so + 128],
                                                      rhs=vtm[:, h * 64:(h + 1) * 64], start=True, stop=False),
                       [res('stb%d' % (h % 2)), r_vtm], [PR[ob]], accum=(h > 0))
                    PE(lambda e, h=h, rows=rows: e.matmul(pbs[ob][:, h * 64:(h + 1) * 64], lhsT=QsT[rows, h // 2, :],
                                                          rhs=RSB[rows, h // 2, :], start=False, stop=True),
                       [r_qkt, res('rsb')], [PR[ob]], accum=True)
                bkv = nextpb()
                for m in range(4):
                    PE(lambda e, m=m: e.matmul(pbs[bkv][:, m * 128:(m + 1) * 128], lhsT=Kstm[:, m * 128:(m + 1) * 128],
                                               rhs=vtm[:, m * 128:(m + 1) * 128], start=True, stop=True),
                       [res('qkbrk%d' % k), r_vtm], [PR[bkv]], accum=(m > 0))
                kvv = pbs[bkv][:, :].rearrange("p (m c) -> p m c", m=4)
                V(lambda e: e.tensor_tensor(out=RST[0:64], in0=RST[0:64], in1=kvv[0:64, :, 0:64], op=ALU.add),
                  [PR[bkv], res('rst')], [res('rst')])
                V(lambda e: e.tensor_tensor(out=RST[64:128], in0=RST[64:128], in1=kvv[64:128, :, 64:128], op=ALU.add),
                  [PR[bkv], res('rst')], [res('rst')])
                V(lambda e: e.tensor_tensor(out=RST[:].rearrange("p m c -> p (m c)"), in0=RST[:].rearrange("p m c -> p (m c)"),
                                            in1=GCT[:], op=ALU.mult), [res('rst'), res('gct')], [res('rst')])
                V(lambda e: e.tensor_copy(out=RSB[:], in_=RST[:]), [res('rst')], [res('rsb')])
                yield 0.5
                ACT(lambda e: e.activation(out=ORT, in_=pbs[ob][:, :], func=AF.Square), [PR[ob]], [res('ort')])
                V(lambda e: e.tensor_reduce(out=SS[:, 8:16], in_=ORT.rearrange("p (h d) -> p h d", h=8), axis=AX.X, op=ALU.add),
                  [res('ort')], [res('ss2')])
                rsqrt_act(SS[:, 8:16], SS[:, 8:16], 1.0 / 64, [res('ss2')], [res('ss2')])
                V(lambda e: e.tensor_tensor(out=ORT.rearrange("p (h d) -> p h d", h=8),
                                            in0=pbs[ob][:, :].rearrange("p (h d) -> p h d", h=8),
                                            in1=SS[:, 8:16, None].to_broadcast([128, 8, 64]), op=ALU.mult),
                  [PR[ob], res('ss2')], [res('ort')])
                V(lambda e: e.tensor_tensor(out=ORT, in0=ORT, in1=RETW[:], op=ALU.mult), [res('ort'), res('retw')], [res('ort')])
                V(lambda e: e.tensor_tensor(out=yb[:, 0:512], in0=ORT, in1=gg[:, 0:512], op=ALU.mult),
                  [res('ort'), r_gate], [res('yb0%d' % k)])
                yield 3.0
                nts = [0] if 8 * i + 6 < 128 else [0, 1]
                for g in range(2):
                    rows = slice(g * 64, (g + 1) * 64)
                    qrhs = qa[rows, g].rearrange("p k q -> p (k q)")
                    ub = UF
                    ib = nextpb()
                    cmp_blocks = []
                    for idx, nt_ in enumerate(nts):
                        ns = slice(nt_ * 128, (nt_ + 1) * 128)

                        def after(P, pk, ib=ib, nt_=nt_, idx=idx, g=g, ub=ub):
                            for h in range(4):
                                PE(lambda e, h=h: e.matmul(pbs[ib][:, h * 64:(h + 1) * 64], lhsT=P[:, h * 128:(h + 1) * 128],
                                                           rhs=MOV[:, nt_, :], start=(idx == 0 and h == 0),
                                                           stop=(idx == len(nts) - 1 and h == 3)),
                                   [r_p[pk], res('mov')], [PR[ib]], accum=not (idx == 0 and h == 0))
                            if idx == len(nts) - 1:
                                V(lambda e: e.tensor_copy(out=usbf[:, 0, g, :], in_=pbs[ub][:, 0:260]), [PR[ub]], [res('usbc%d%d' % (k, g))])
                        cmp_blocks.append((KCT[rows, ns], [res('kct')], qrhs, [r_qaq], None,
                                           CMK[k][:, nt_ * 128:(nt_ + 1) * 128], [r_cmk[k]],
                                           VCM[:, nt_, g, :], [res('vcm')], ub, idx == 0, idx == len(nts) - 1, after))
                    for _ in run_blocks(cmp_blocks, ABF):
                        pass
                    r_usb = res('usbc%d%d' % (k, g))
                    ucv = usbf[:, 0, g, :].rearrange("p (h c) -> p h c", h=4)
                    V(lambda e: e.tensor_scalar(out=DEN[:, g * 4:(g + 1) * 4], in0=ucv[:, :, 64], scalar1=1e-30, scalar2=None,
                                                op0=ALU.max), [r_usb], [res('den0%d' % g)])
                    V(lambda e: e.reciprocal(out=DEN[:, g * 4:(g + 1) * 4], in_=DEN[:, g * 4:(g + 1) * 4]),
                      [res('den0%d' % g)], [res('den0%d' % g)])
                    for h in range(4):
                        if h == 0:
                            V(lambda e: e.tensor_scalar(out=IMP[:, g * 64:(g + 1) * 64], in0=pbs[ib][:, 0:64],
                                                        scalar1=DEN[:, g * 4:g * 4 + 1], scalar2=None, op0=ALU.mult),
                              [PR[ib], res('den0%d' % g)], [res('imp%d' % g)])
                        else:
                            V(lambda e, h=h: e.scalar_tensor_tensor(
                                out=IMP[:, g * 64:(g + 1) * 64], in0=pbs[ib][:, h * 64:(h + 1) * 64],
                                scalar=DEN[:, g * 4 + h:g * 4 + h + 1], in1=IMP[:, g * 64:(g + 1) * 64], op0=ALU.mult, op1=ALU.add),
                              [PR[ib], res('den0%d' % g), res('imp%d' % g)], [res('imp%d' % g)])
                    yield 1.0
                    sco = SCO[:, g * 64:(g + 1) * 64]
                    sc2 = SC2[:, g * 64:(g + 1) * 64]
                    V(lambda e: e.tensor_tensor(out=sco, in0=IMP[:, g * 64:(g + 1) * 64], in1=TAB[k][:, 64:128], op=ALU.add),
                      [res('imp%d' % g), r_tab[k]], [res('sco%d' % g)])
                    V(lambda e: e.max(out=M8[:, g * 16:g * 16 + 8], in_=sco), [res('sco%d' % g)], [res('m8a%d' % g)])
                    V(lambda e: e.match_replace(out=sc2, in_to_replace=M8[:, g * 16:g * 16 + 8], in_values=sco, imm_value=-3e38),
                      [res('sco%d' % g), res('m8a%d' % g)], [res('sc2%d' % g)])
                    V(lambda e: e.max(out=M8[:, g * 16 + 8:g * 16 + 16], in_=sc2), [res('sc2%d' % g)], [res('m8b%d' % g)])
                    V(lambda e: e.tensor_scalar(out=SELB[:, (1 - g) * 64:(2 - g) * 64], in0=sco,
                                                scalar1=M8[:, g * 16 + 15:g * 16 + 16], scalar2=NEGB, op0=ALU.is_lt, op1=ALU.mult),
                      [res('sco%d' % g), res('m8b%d' % g)], [res('selb%d' % g)])
                    yield (0.5 if g == 0 else 3.0)
                win_blocks = []
                for g in range(2):
                    rows = slice(g * 64, (g + 1) * 64)
                    qrhs = qa[rows, g].rearrange("p k q -> p (k q)")
                    ub = UF
                    j0 = max(0, i - 4)
                    for j in range(j0, i + 1):
                        js = slice(j * 128, (j + 1) * 128)
                        if j == i:
                            mk = MB[:, 0, :]
                        elif j == i - 4:
                            mk = MB[:, 1, :]
                        else:
                            mk = None
                        after = None
                        if j == i:
                            def after(P, pk, ub=ub, g=g):
                                V(lambda e: e.tensor_copy(out=usbf[:, 1, g, :], in_=pbs[ub][:, 0:260]), [PR[ub]], [res('usbw%d%d' % (k, g))])
                        win_blocks.append((KWT[rows, js], [r_kw[j]], qrhs, [r_qaq], mk, None, [],
                                           VW[:, j, g, :], [r_vw[j]], ub, j == j0, j == i, after))
                yield from run_blocks(win_blocks, ABF)
                bs = nextpb()
                pts = pbs[bs][:, :].bitcast(BF16)
                PE(lambda e: e.transpose(out=pts[:, 0:128], in_=SELB, identity=IDB[:]),
                   [res('selb0'), res('selb1'), res('idb')], [PR[bs]])
                V(lambda e: e.tensor_copy(out=qa[64:128, 0], in_=pts[64:128, None, 0:128].to_broadcast([64, 4, 128])),
                  [PR[bs]], [r_qas])
                V(lambda e: e.tensor_copy(out=qa[0:64, 1], in_=pts[0:64, None, 0:128].to_broadcast([64, 4, 128])),
                  [PR[bs]], [r_qas])
                yield 0.5

            def back(i):
                k = i % 2
                k3 = i % 3
                ts = slice(i * 128, (i + 1) * 128)
                gg = GG[k3]
                qa = QA[k3]
                gl = GL[k3]
                yb = YB[k]
                usbf = USBF[k]
                r_gate = res('gate%d' % k3)
                r_qaq = res('qa_q%d' % k3)
                r_qas = res('qa_s%d' % k3)
                r_gl = res('gl%d' % k3)
                r_uf = [res('usbc%d%d' % (k, g)) for g in range(2)] + [res('usbw%d%d' % (k, g)) for g in range(2)]
                r_us = [res('usbs0'), res('usbs1')]
                uf = usbf.rearrange("p x g (h c) -> p x (g h) c", h=4)
                us = USBS.rearrange("p g (h c) -> p (g h) c", h=4)
                cov = CO.rearrange("p (x h) -> p x h", x=3)
                glv = gl.rearrange("p (x h) -> p x h", x=3)
                onv = ONS.rearrange("p (h d) -> p h d", h=8)
                sqv = SQB.rearrange("p (h d) -> p h d", h=8)
                for x_, xb_ in ((0, 0), (1, 2)):
                    V(lambda e: e.tensor_scalar(out=cov[:, xb_, :], in0=uf[:, x_, :, 64], scalar1=1e-30, scalar2=None, op0=ALU.max),
                      r_uf, [res('co%d' % xb_)])
                    V(lambda e: e.reciprocal(out=cov[:, xb_, :], in_=cov[:, xb_, :]), [res('co%d' % xb_)], [res('co%d' % xb_)])
                    V(lambda e: e.tensor_tensor(out=cov[:, xb_, :], in0=cov[:, xb_, :], in1=glv[:, xb_, :], op=ALU.mult),
                      [res('co%d' % xb_), r_gl], [res('co%d' % xb_)])
                V(lambda e: e.tensor_tensor(out=onv, in0=uf[:, 0, :, 0:64], in1=cov[:, 0, :, None].to_broadcast([128, 8, 64]), op=ALU.mult),
                  r_uf + [res('co0')], [res('ons')])
                V(lambda e: e.tensor_tensor(out=sqv, in0=uf[:, 1, :, 0:64], in1=cov[:, 2, :, None].to_broadcast([128, 8, 64]), op=ALU.mult),
                  r_uf + [res('co2')], [res('sqb')])
                V(lambda e: e.tensor_tensor(out=ONS, in0=ONS, in1=SQB, op=ALU.add), [res('ons'), res('sqb')], [res('ons')])
                yield

                def ytrans(half):
                    by = nextpb()
                    pty = pbs[by][:, :].bitcast(BF16)
                    for m in range(4):
                        c = half * 4 + m
                        PE(lambda e, m=m, c=c: e.transpose(out=pty[:, m * 128:(m + 1) * 128], in_=yb[:, c * 128:(c + 1) * 128],
                                                           identity=IDB[:]),
                           [res('yb%d%d' % (half, k)), res('idb')], [PR[by]], accum=(m > 0))
                    V(lambda e: e.tensor_copy(out=YT[:, half * 4:(half + 1) * 4, :].rearrange("p c t -> p (c t)"),
                                              in_=pty[:, 0:512]), [PR[by]], [res('yt%d' % half)])
                ytrans(0)
                yield
                def late_dve(g):
                    hs = slice(4 * g, 4 * g + 4)
                    cs = slice(g * 256, (g + 1) * 256)
                    r_c = res('co1%d' % g)
                    V(lambda e: e.tensor_scalar(out=cov[:, 1, hs], in0=us[:, hs, 64], scalar1=1e-30, scalar2=None, op0=ALU.max),
                      [res('usbs%d' % g)], [r_c])
                    V(lambda e: e.reciprocal(out=cov[:, 1, hs], in_=cov[:, 1, hs]), [r_c], [r_c])
                    V(lambda e: e.tensor_tensor(out=cov[:, 1, hs], in0=cov[:, 1, hs], in1=glv[:, 1, hs], op=ALU.mult),
                      [r_c, r_gl], [r_c])
                    V(lambda e: e.tensor_tensor(out=sqv[:, hs, :], in0=us[:, hs, 0:64],
                                                in1=cov[:, 1, hs, None].to_broadcast([128, 4, 64]), op=ALU.mult),
                      [res('usbs%d' % g), r_c, res('sqb')], [res('sqb%d' % g)])
                    V(lambda e: e.tensor_tensor(out=ONS[:, cs], in0=ONS[:, cs], in1=SQB[:, cs], op=ALU.add),
                      [res('ons'), res('sqb%d' % g), res('sqb')], [res('ons%d' % g)])
                    V(lambda e: e.tensor_tensor(out=yb[:, 512 + g * 256:512 + (g + 1) * 256], in0=ONS[:, cs],
                                                in1=gg[:, 512 + g * 256:512 + (g + 1) * 256], op=ALU.mult),
                      [res('ons%d' % g), res('ons'), r_gate], [res('yb1%d%d' % (k, g))])

                def late_tr(g):
                    by = nextpb()
                    pty = pbs[by][:, :].bitcast(BF16)
                    for m in range(2):
                        c = 4 + 2 * g + m
                        PE(lambda e, m=m, c=c: e.transpose(out=pty[:, m * 128:(m + 1) * 128], in_=yb[:, c * 128:(c + 1) * 128],
                                                           identity=IDB[:]),
                           [res('yb1%d%d' % (k, g)), res('idb')], [PR[by]], accum=(m > 0))
                    V(lambda e: e.tensor_copy(out=YT[:, 4 + 2 * g:6 + 2 * g, :].rearrange("p c t -> p (c t)"),
                                              in_=pty[:, 0:256]), [PR[by]], [res('yt1%d' % g)])

                slc_blocks = []
                ntr = min(4, i)
                for g in range(2):
                    qaug = qa[:, g].rearrange("p k q -> p (k q)")
                    ub = UBB
                    for j in range(i + 1):
                        js = slice(j * 128, (j + 1) * 128)
                        after = None
                        if j == i:
                            def after(P, pk, ub=ub, g=g):
                                V(lambda e: e.tensor_copy(out=USBS[:, g, :], in_=pbs[ub][:, 0:260]), [PR[ub]], [res('usbs%d' % g)])
                                late_dve(g)
                                if g == 1 and ntr == i:
                                    late_tr(0)
                        elif g == 1 and j == ntr:
                            def after(P, pk):
                                late_tr(0)
                        slc_blocks.append((KSA[:, g, js], [r_ks[j], res('ksa_e')], qaug, [r_qaq, r_qas],
                                           MB[:, 0, :] if j == i else None, None, [],
                                           VS[:, j, g, :], [r_vs[j]], ub, j == 0, j == i, after))
                yield from run_blocks(slc_blocks, ABB)
                late_tr(1)
                for nh in range(2):
                    bo = nextpb()
                    for c in range(8):
                        PE(lambda e, c=c: e.matmul(pbs[bo][:, :], lhsT=YT[:, c, :], rhs=WOUT[:, c, nh * 512:(nh + 1) * 512],
                                                   start=(c == 0), stop=(c == 7)),
                           [res('yt0') if c < 4 else res('yt1%d' % ((c - 4) // 2)), res('wout')], [PR[bo]], accum=(c > 0))
                    V(lambda e: e.tensor_tensor(out=XS[k][:, nh * 512:(nh + 1) * 512], in0=pbs[bo][:, :],
                                                in1=XS[k][:, nh * 512:(nh + 1) * 512], op=ALU.add),
                      [PR[bo], r_xs[k]], [r_xs[k]])
                dma(out[ts, :], XS[k][:, :], reads=[r_xs[k]])
                yield

            def merged3(gA, gB, gK, scale, a_delay=0.0):
                gens = {'A': gA, 'B': gB, 'K': gK}
                alive = {n: (g is not None) for n, g in gens.items()}
                ready = {'A': a_delay, 'B': 0.0}
                now = 0.0
                while alive['A'] or alive['B'] or alive['K']:
                    cand = [n for n in ('A', 'B') if alive[n] and ready[n] <= now]
                    if cand:
                        n = min(cand, key=lambda z: ready[z])
                        try:
                            c = next(gens[n])
                            c = 0.5 if c is None else c
                            ready[n] = now + c * scale
                            now += 0.3
                        except StopIteration:
                            alive[n] = False
                    elif alive['K']:
                        try:
                            next(gens['K'])
                            now += 0.75
                        except StopIteration:
                            alive['K'] = False
                    else:
                        pend = [ready[n] for n in ('A', 'B') if alive[n]]
                        now = min(pend)

            FRONT_COST = 30.0
            load_xs(0)
            load_tabs(0)
            load_tabs(1)
            load_xt(0)
            cast_xb(0)
            load_xt(1)
            merged3(stageA(0), None, None, 1.0)
            load_xt(2)
            merged3(stageA(1), stageB(0), None, 1.0)
            for i in range(NT):
                if i + 1 < NT:
                    load_xs(i + 1)
                if i + 2 < NT:
                    load_tabs(i + 2)
                if i + 3 < NT:
                    load_xt(i + 3)
                back_time = 0.75 * 2 * (i + 1)
                scale = max(1.0, back_time / FRONT_COST)
                merged3(stageA(i + 2) if i + 2 < NT else None, stageB(i + 1) if i + 1 < NT else None, back(i), scale, a_delay=0.5)

        try:
            record()
        except _Stop:
            pass
        fin = S.op('sync', None)
        fin.deps = list(S.dma_hist)
        S.emit(st)
    return nc


_CACHE = {}


def _prep_shared(inp):
    f = np.float32
    w_in = np.asarray(inp['w_in'][0], f)
    sp = np.cumsum([0, 512, 512, 512, 512, 512, 512, 128, 128, 128, 128, 128, 128, 24])
    seg = {n: (sp[i], sp[i + 1]) for i, n in enumerate(['rq', 'rk', 'rv', 'rg', 'nq', 'ng', 'ck', 'cv', 'sk', 'sv', 'wk', 'wv', 'gl'])}

    def cols(n, a=None, b=None):
        s0, s1 = seg[n]
        idx = np.arange(s0, s1)
        return idx if a is None else idx[a:b]
    c1 = np.concatenate([cols('ck', 0, 64), cols('cv', 0, 64), cols('ck', 64, 128), cols('cv', 64, 128),
                         cols('sk'), cols('wk'), cols('sv'), cols('wv')])
    nq_pairs = np.concatenate([np.concatenate([cols('nq', kk * 64, kk * 64 + 64), cols('nq', (4 + kk) * 64, (4 + kk) * 64 + 64)])
                               for kk in range(4)])
    c2 = np.concatenate([cols('rq'), cols('rk'), cols('rv'), cols('rg'), nq_pairs, cols('ng'), cols('gl')])
    sh = {}
    sh['w1'] = np.ascontiguousarray(w_in[:, c1])
    sh['w2'] = np.ascontiguousarray(w_in[:, c2])
    sh['wout'] = np.ascontiguousarray(np.asarray(inp['w_out'][0], f))
    sh['normw'] = np.ascontiguousarray(np.asarray(inp['norm_w'][0], f).reshape(8, 128).T)
    sh['retw'] = np.ascontiguousarray(np.broadcast_to(np.asarray(inp['ret_norm_w'][0], f).reshape(1, 512), (128, 512)))
    sh['qw'] = np.ascontiguousarray(np.broadcast_to(np.tile(np.asarray(inp['q_norm_w'][0], f), 8)[None, :], (128, 512)))
    for nm, key in (('kcw', 'k_norm_cmp'), ('ksw', 'k_norm_slc'), ('kww', 'k_norm_win')):
        sh[nm] = np.ascontiguousarray(np.broadcast_to(np.tile(np.asarray(inp[key][0], f), 2)[None, :], (128, 128)))
    sh['bg'] = np.ascontiguousarray(np.broadcast_to(np.asarray(inp['b_gate'][0], f)[None, :], (128, 24)))
    sh['pos'] = np.ascontiguousarray(np.concatenate([np.asarray(inp['cmp_pos_k'][0], f).T, np.asarray(inp['cmp_pos_v'][0], f).T], axis=0))
    sh['cw1k'] = np.ascontiguousarray(np.asarray(inp['cmp_w1_k'][0], f))
    sh['cw1v'] = np.ascontiguousarray(np.asarray(inp['cmp_w1_v'][0], f))
    w2k = np.asarray(inp['cmp_w2_k'][0], f).reshape(2, 128, 64).transpose(1, 0, 2)
    w2v = np.asarray(inp['cmp_w2_v'][0], f).reshape(2, 128, 64).transpose(1, 0, 2)
    sh['cw2'] = np.ascontiguousarray(np.stack([w2k, w2v], axis=1).reshape(128, 256))
    return sh


def kernel(**inp):
    if 'nc' not in _CACHE:
        _CACHE['nc'] = build_nc()
        _CACHE['consts'] = _consts()
    nc = _CACHE['nc']
    sh = _prep_shared(inp)
    sh.update(_CACHE['consts'])
    x = np.asarray(inp['x'], np.float32)
    in_maps = []
    for b in range(8):
        m = dict(sh)
        m['x'] = np.ascontiguousarray(x[b])
        m['xT'] = np.ascontiguousarray(x[b].T)
        in_maps.append(m)
    res = run_bass_kernel_spmd(nc, in_maps, core_ids=list(range(8)))
    return np.stack([np.asarray(r['out'], np.float32) for r in res.results], axis=0)
```

```python
import contextlib
import numpy as np
import concourse.bass as bass
import concourse.mybir as mybir
from concourse.bass_utils import run_bass_kernel_spmd

F32 = mybir.dt.float32
BF16 = mybir.dt.bfloat16
ALU = mybir.AluOpType
AF = mybir.ActivationFunctionType
AX = mybir.AxisListType

T = 4096
D = 1024
NT = T // 128
NCOL1 = 768
NCOL2 = 3096
EPS = 1e-6
NEGB = -30000.0
DEBUG = False
SEQUENTIAL = False


class Res:
    __slots__ = ('name', 'w', 'r')

    def __init__(s, name=''):
        s.name = name
        s.w = None
        s.r = []


class Ins:
    __slots__ = ('eng', 'fn', 'deps', 'inc', 'val', 'sem', 'isdma')

    def __init__(s, eng, fn, isdma):
        s.eng = eng
        s.fn = fn
        s.deps = []
        s.inc = isdma
        s.val = None
        s.sem = None
        s.isdma = isdma


class _Rec:
    def __getattr__(s, name):
        def f(*a, **kw):
            s.call = (name, a, kw)
            return s
        return f


class Sched:
    COMPUTE = ('tensor', 'vector', 'scalar', 'gpsimd')

    def __init__(s, nc, ndma_sems=12):
        s.nc = nc
        s.streams = {e: [] for e in ('tensor', 'vector', 'scalar', 'gpsimd', 'sync')}
        s.ndma = ndma_sems
        s.dma_hist = []
        s.excl = set()

    def op(s, eng, fn, reads=(), writes=(), dma=False, accum=False):
        if fn is not None:
            rec = _Rec()
            fn(rec)
            fn = rec.call
        ins = Ins(eng, fn, dma)
        deps = {}

        def add(d):
            if d is None or d is ins:
                return
            deps[id(d)] = d
        if s.excl:
            extra = [r for r in reads if id(r) in s.excl and not any(r is w for w in writes)]
            if extra:
                writes = list(writes) + extra
        for r in reads:
            add(r.w)
        for w in writes:
            if w.w is not None and not (accum and w.w.eng == eng and not w.w.isdma):
                add(w.w)
            for rr in w.r:
                add(rr)
        if dma:
            k = len(s.dma_hist)
            if k >= s.ndma:
                add(s.dma_hist[k - s.ndma])
            s.dma_hist.append(ins)
        for d in deps.values():
            d.inc = True
        ins.deps = list(deps.values())
        for r in reads:
            r.r.append(ins)
        for w in writes:
            w.w = ins
            if not accum:
                w.r = []
        s.streams[eng].append(ins)
        return ins

    def barrier(s):
        marks = []
        for e in s.COMPUTE:
            if s.streams[e]:
                marks.append(s.streams[e][-1])
        marks += s.dma_hist[-s.ndma:]
        for m in marks:
            m.inc = True
        for e in list(s.streams):
            ins = s.op(e, None)
            ins.deps = list(marks)

    def emit(s, stack):
        nc = s.nc
        sems = {}
        for e in s.COMPUTE:
            sems[e] = stack.enter_context(nc.semaphore('sem_' + e))
        dsems = [stack.enter_context(nc.semaphore('dsem%d' % i)) for i in range(s.ndma)]
        for e in s.COMPUTE:
            c = 0
            for ins in s.streams[e]:
                if ins.inc and ins.fn is not None:
                    c += 1
                ins.val = c
                ins.sem = sems[e]
        for k, ins in enumerate(s.dma_hist):
            ins.sem = dsems[k % s.ndma]
            ins.val = 16 * (k // s.ndma + 1)
        block = stack.enter_context(nc.Block())

        def make(ename):
            stream = s.streams[ename]

            def body(eng):
                seen = {}
                for ins in stream:
                    need = {}
                    for d in ins.deps:
                        if d.fn is None or not d.val:
                            continue
                        key = id(d.sem)
                        if seen.get(key, 0) >= d.val:
                            continue
                        if key not in need or need[key][1] < d.val:
                            need[key] = (d.sem, d.val)
                    for key, (sm, v) in need.items():
                        eng.wait_ge(sm, v)
                        seen[key] = v
                    if ins.fn is None:
                        continue
                    nm_, a_, kw_ = ins.fn
                    r = getattr(eng, nm_)(*a_, **kw_)
                    if ins.inc:
                        r.then_inc(ins.sem, 16 if ins.isdma else 1)
            return body
        block.sync(make('sync'))
        block.tensor(make('tensor'))
        block.vector(make('vector'))
        block.scalar(make('scalar'))
        block.gpsimd(make('gpsimd'))


def _consts():
    c = {}
    c['ident'] = np.eye(128, dtype=np.float32)
    t = np.arange(T, dtype=np.float32)
    inv = (10000.0 ** (-np.arange(32, dtype=np.float32) / 32)).astype(np.float32)
    ang = t[:, None] * inv[None, :]
    tab = np.zeros((NT, 128, 128), np.float32)
    tab[:, :, 0:32] = np.cos(ang).reshape(NT, 128, 32)
    tab[:, :, 32:64] = np.sin(ang).reshape(NT, 128, 32)
    tt = np.arange(T)
    blk = np.arange(64)
    valid = blk[None, :] * 64 <= tt[:, None]
    force = (blk[None, :] == (tt // 64)[:, None]) | (blk[None, :] == 0)
    sb = np.where(valid, np.where(force, 1e4, 0.0), -1e30).astype(np.float32)
    tab[:, :, 64:128] = sb.reshape(NT, 128, 64)
    c['tab'] = tab
    h = np.arange(8, dtype=np.float64)
    logg = np.log1p(-np.exp2(-5.0 - h))
    pos = np.arange(128, dtype=np.float64)
    gq = np.exp(logg[None, :] * (pos[:, None] + 1.0))
    gk = np.exp(-logg[None, :] * (pos[:, None] + 1.0)) * 64 ** -0.5
    c['gqk'] = np.concatenate([gq, gk], axis=1).astype(np.float32)
    gC = np.exp(logg * 128.0)
    gct = np.zeros((128, 4, 64), np.float32)
    for a in range(2):
        for m in range(4):
            gct[a * 64:(a + 1) * 64, m, :] = gC[2 * m + a]
    c['gct'] = gct.reshape(128, 256)
    s = np.arange(128)
    caus = (s[:, None] <= s[None, :]).astype(np.float32)
    anti = (s[:, None] > s[None, :]).astype(np.float32)
    c['masks'] = np.concatenate([caus, anti], axis=1)
    n = np.arange(256)
    cm = np.zeros((NT, 128, 2, 128), np.float32)
    for i in range(NT):
        q = i * 128 + np.arange(128)
        v = (16 * n[:, None] + 31 <= q[None, :]) & (n[:, None] < 255)
        cm[i] = v.reshape(2, 128, 128).transpose(1, 0, 2)
    c['cmask'] = cm.reshape(NT, 128, 256)
    E = (np.arange(T)[None, :] // 64 == np.arange(64)[:, None]).astype(np.float32)
    c['ee'] = np.concatenate([E, E], axis=0)
    posn = np.arange(255)[:, None] * 16 + np.arange(32)[None, :]
    b = posn // 64
    M = (b[:, :, None] == np.arange(64)[None, None, :]).mean(axis=1).astype(np.float32)
    Mp = np.zeros((256, 64), np.float32)
    Mp[:255] = M
    c['mov'] = Mp.reshape(2, 128, 64).transpose(1, 0, 2).reshape(128, 128).copy()
    return c


CONST_SHAPES = {
    'ident': [128, 128], 'tab': [NT, 128, 128], 'gqk': [128, 16], 'gct': [128, 256],
    'masks': [128, 256], 'cmask': [NT, 128, 256], 'ee': [128, T], 'mov': [128, 128],
}
IN_SHAPES = {
    'x': [T, D], 'xT': [D, T], 'w1': [D, NCOL1], 'w2': [D, NCOL2], 'wout': [D, D],
    'normw': [128, 8], 'retw': [128, 512], 'qw': [128, 512], 'kcw': [128, 128],
    'ksw': [128, 128], 'kww': [128, 128], 'bg': [128, 24], 'pos': [128, 32],
    'cw1k': [2048, 256], 'cw1v': [2048, 256], 'cw2': [128, 256],
}


class _Stop(Exception):
    pass


def build_nc(stop=None):
    nc = bass.Bass("TRN2", target_bir_lowering=False)
    A = {}
    for k, shp in list(IN_SHAPES.items()) + list(CONST_SHAPES.items()):
        A[k] = nc.dram_tensor(k, shp, F32, kind="ExternalInput").ap()
    out = nc.dram_tensor("out", [T, D], F32, kind="ExternalOutput").ap()
    dbg = {}
    with contextlib.ExitStack() as st:
        S = Sched(nc)

        def ck(name):
            if stop == name:
                raise _Stop()

        def sb(name, shape, dt):
            return st.enter_context(nc.sbuf_tensor(name, shape, dt))

        WIN2 = sb("win2", [128, 8, NCOL2], BF16)
        WOUT = sb("wout_sb", [128, 8, D], BF16)
        KSA = sb("ksa", [128, 2, T], BF16)
        KWT = sb("kwt", [128, T], BF16)
        VS = sb("vs", [128, NT, 2, 65], BF16)
        VW = sb("vw", [128, NT, 2, 65], BF16)
        VCM = sb("vcm", [128, 2, 2, 65], BF16)
        MOV = sb("movb", [128, 2, 64], BF16)
        KCT = sb("kct", [128, 256], BF16)
        XS = [sb("xs%d" % i, [128, D], F32) for i in range(2)]
        XT = [sb("xt%d" % i, [128, 8, 128], F32) for i in range(2)]
        XB = [sb("xb%d" % i, [128, 8, 128], BF16) for i in range(2)]
        RSTD = sb("rstd", [128, NT], F32)
        NRSTD = sb("nrstd", [128, NT], F32)
        IDF = sb("idf", [128, 128], F32)
        IDB = sb("idb", [128, 128], BF16)
        NORMW = sb("normw_sb", [128, 8], F32)
        RETW = sb("retw_sb", [128, 512], F32)
        QW = sb("qw_sb", [128, 512], F32)
        KCW = sb("kcw_sb", [128, 128], F32)
        KSW = sb("ksw_sb", [128, 128], F32)
        KWW = sb("kww_sb", [128, 128], F32)
        BG = sb("bg_sb", [128, 24], F32)
        GQK = sb("gqk_sb", [128, 16], F32)
        GCT = sb("gct_sb", [128, 256], F32)
        MASKS = sb("masks_sb", [128, 256], F32)
        TAB = [sb("tab%d" % i, [128, 128], F32) for i in range(2)]
        CMK = [sb("cmk%d" % i, [128, 256], F32) for i in range(2)]
        RST = sb("rstate", [128, 4, 64], F32)
        RSB = sb("rstate_b", [128, 4, 64], BF16)
        MB = sb("maskbias", [128, 2, 512], BF16)
        ARENA = sb("arena", [128, 33792], BF16)
        pbs = [st.enter_context(nc.psum_tensor("pb%d" % i, [128, 512], F32)) for i in range(8)]
        PR = [Res("pb%d" % i) for i in range(8)]
        S.excl = set(id(r) for r in PR)

        class Carver:
            def __init__(s):
                s.off = 0

            def get(s, n_el, dt):
                nb = n_el * (2 if dt == BF16 else 4)
                nb16 = nb // 2
                a = ARENA[:, s.off:s.off + nb16]
                s.off += nb16
                assert s.off <= 33792, s.off
                if dt == F32:
                    a = a.bitcast(F32)
                return a

        R = {}

        def res(n):
            if n not in R:
                R[n] = Res(n)
            return R[n]
        r_ks = [Res("ks%d" % j) for j in range(NT)]
        r_kw = [Res("kw%d" % j) for j in range(NT)]
        r_vs = [Res("vs%d" % j) for j in range(NT)]
        r_vw = [Res("vw%d" % j) for j in range(NT)]
        r_ct = [Res("ct%d" % j) for j in range(NT)]
        r_xs = [Res("xs0"), Res("xs1")]
        r_xt = [Res("xt0"), Res("xt1")]
        r_xb = [Res("xb0"), Res("xb1")]
        r_tab = [Res("tab0"), Res("tab1")]
        r_cmk = [Res("cmk0"), Res("cmk1")]

        def dma(o, i, reads=(), writes=()):
            return S.op('sync', lambda e: e.dma_start(out=o, in_=i), reads=reads, writes=writes, dma=True)

        def V(fn, reads=(), writes=()):
            return S.op('vector', fn, reads=reads, writes=writes)

        def G(fn, reads=(), writes=()):
            return S.op('gpsimd', fn, reads=reads, writes=writes)

        def ACT(fn, reads=(), writes=()):
            return S.op('scalar', fn, reads=reads, writes=writes)

        def PE(fn, reads=(), writes=(), accum=False):
            return S.op('tensor', fn, reads=reads, writes=writes, accum=accum)

        def ldc(name, tile, r):
            dma(tile[:], A[name], writes=[r])

        def record():
            for nm, tl in (('ident', IDF), ('normw', NORMW), ('retw', RETW), ('qw', QW), ('kcw', KCW),
                           ('ksw', KSW), ('kww', KWW), ('bg', BG), ('gqk', GQK), ('gct', GCT), ('masks', MASKS)):
                ldc(nm, tl, res(nm))
            V(lambda e: e.tensor_copy(out=IDB[:], in_=IDF[:]), [res('ident')], [res('idb')])
            V(lambda e: e.tensor_scalar(out=MB[:].rearrange("p m (h q) -> p m h q", h=4),
                                        in0=MASKS[:].rearrange("p (m q) -> p m q", m=2)[:, :, None, :].to_broadcast([128, 2, 4, 128]),
                                        scalar1=-1.0, scalar2=-NEGB, op0=ALU.add, op1=ALU.mult), [res('masks')], [res('mb')])
            G(lambda e: e.memset(VS[:].rearrange("p a b c -> p (a b c)"), 1.0), [], r_vs)
            G(lambda e: e.memset(VW[:].rearrange("p a b c -> p (a b c)"), 1.0), [], r_vw)
            G(lambda e: e.memset(VCM[:].rearrange("p a b c -> p (a b c)"), 1.0), [], [res('vcm')])
            G(lambda e: e.memset(RST[:].rearrange("p a b -> p (a b)"), 0.0), [], [res('rst')])
            G(lambda e: e.memset(RSB[:].rearrange("p a b -> p (a b)"), 0.0), [], [res('rsb')])

            cnt = [0]

            SLOTS = [XS[0][:, :], XS[1][:, :], XT[0][:].rearrange("p c t -> p (c t)"), XT[1][:].rearrange("p c t -> p (c t)")]
            r_sl = [r_xs[0], r_xs[1], r_xt[0], r_xt[1]]

            def stage_cast(dst, srcs, scale=None, rd=(), wr=(), nslots=4):
                k = cnt[0] % nslots
                cnt[0] += 1
                SL = SLOTS[k]
                for (p0, p1, c0, c1, src) in srcs:
                    dma(SL[p0:p1, c0:c1], src, writes=[r_sl[k]])
                p0 = min(s_[0] for s_ in srcs)
                p1 = max(s_[1] for s_ in srcs)
                c1 = max(s_[3] for s_ in srcs)
                if cnt[0] % 2 == 0:
                    if scale is None:
                        V(lambda e: e.tensor_copy(out=dst, in_=SL[p0:p1, 0:c1]), [r_sl[k]] + list(rd), list(wr))
                    else:
                        V(lambda e: e.tensor_scalar(out=dst, in0=SL[p0:p1, 0:c1], scalar1=scale, scalar2=None,
                                                    op0=ALU.mult), [r_sl[k]] + list(rd), list(wr))
                else:
                    if scale is None:
                        ACT(lambda e: e.activation(out=dst, in_=SL[p0:p1, 0:c1], func=AF.Copy), [r_sl[k]] + list(rd), list(wr))
                    else:
                        ACT(lambda e: e.activation(out=dst, in_=SL[p0:p1, 0:c1], func=AF.Copy, scale=scale),
                            [r_sl[k]] + list(rd), list(wr))

            ca = Carver()
            CTS = ca.get(2 * T, BF16).rearrange("p (g r m) -> p g r m", g=2, r=16)
            W1 = ca.get(32 * 256, BF16).rearrange("p (l h) -> p l h", l=32)
            HTF = ca.get(2 * 2 * 2 * 256, BF16)
            HT = HTF.rearrange("p (k g c n) -> p k g c n", k=2, g=2, c=2)
            WIN1 = ca.get(8 * NCOL1, BF16).rearrange("p (c n) -> p c n", c=8)
            W2C = ca.get(256, BF16).rearrange("p (k c d) -> p k c d", k=2, c=2)
            POSB = ca.get(32, BF16)
            KV1 = ca.get(NCOL1, F32)
            KVB = ca.get(512, BF16)
            SQ1 = ca.get(256, F32)
            ST1 = ca.get(8, F32)
            ZT = ca.get(256, F32)
            ET = ca.get(256, F32)
            CBIAS = ca.get(4, F32)
            KCTM = ca.get(128, BF16)
            SQX = ca.get(1024, BF16)
            ONESB = ca.get(2, BF16)

            ck('consts')
            for c in range(8):
                rows = slice(c * 128, (c + 1) * 128)
                stage_cast(WIN1[:, c, :], [(0, 128, 0, NCOL1, A['w1'][rows, :])], scale=NORMW[:, c:c + 1],
                           rd=[res('normw')], wr=[res('win1')])
            W2STEPS = []
            for c in range(8):
                rows = slice(c * 128, (c + 1) * 128)
                for p in range(4):
                    c0 = p * 1024
                    c1 = min(NCOL2, c0 + 1024)
                    W2STEPS.append((WIN2[:, c, c0:c1], [(0, 128, 0, c1 - c0, A['w2'][rows, c0:c1])], NORMW[:, c:c + 1],
                                    [res('normw')], [res('win2')]))
            for c in range(8):
                rows = slice(c * 128, (c + 1) * 128)
                W2STEPS.append((WOUT[:, c, :], [(0, 128, 0, D, A['wout'][rows, :])], None, [], [res('wout')]))
            w1k = A['cw1k'].rearrange("(l d) h -> d l h", d=64)
            w1v = A['cw1v'].rearrange("(l d) h -> d l h", d=64)
            for p in range(8):
                k = cnt[0] % 2
                cnt[0] += 1
                xsv = XS[k][:, :].rearrange("p (l h) -> p l h", l=4)
                dma(xsv[0:64], w1k[:, p * 4:(p + 1) * 4, :], writes=[r_xs[k]])
                dma(xsv[64:128], w1v[:, p * 4:(p + 1) * 4, :], writes=[r_xs[k]])
                V(lambda e, k=k, p=p, xsv=xsv: e.tensor_copy(out=W1[:, p * 4:(p + 1) * 4, :], in_=xsv),
                  [r_xs[k]], [res('w1')])
            for p in range(4):
                k = cnt[0] % 2
                cnt[0] += 1
                cs = slice(p * 1024, (p + 1) * 1024)
                dma(XS[k][:, :], A['ee'][:, cs], writes=[r_xs[k]])
                V(lambda e, k=k, cs=cs: e.tensor_copy(out=KSA[64:128, 0, cs], in_=XS[k][64:128, :]), [r_xs[k]], [res('ksa_e')])
                ACT(lambda e, k=k, cs=cs: e.activation(out=KSA[0:64, 1, cs], in_=XS[k][0:64, :], func=AF.Copy), [r_xs[k]], [res('ksa_e')])
            k = cnt[0] % 2
            cnt[0] += 1
            dma(XS[k][:, 0:256], A['cw2'], writes=[r_xs[k]])
            dma(XS[k][:, 256:288], A['pos'], writes=[r_xs[k]])
            dma(XS[k][:, 288:416], A['mov'], writes=[r_xs[k]])
            V(lambda e, k=k: e.tensor_copy(out=W2C.rearrange("p k c d -> p (k c d)"), in_=XS[k][:, 0:256]), [r_xs[k]], [res('w2c')])
            V(lambda e, k=k: e.tensor_copy(out=POSB, in_=XS[k][:, 256:288]), [r_xs[k]], [res('posb')])
            V(lambda e, k=k: e.tensor_copy(out=MOV[:].rearrange("p a b -> p (a b)"), in_=XS[k][:, 288:416]), [r_xs[k]], [res('mov')])

            ck('weights')
            xTd = A['xT'].rearrange("(c p) t -> p c t", p=128)
            pbk = [0]
            pbset = [0, 1, 2]

            def nextpb():
                k = pbset[pbk[0] % len(pbset)]
                pbk[0] += 1
                return k

            def rsqrt_act(dst, src, scale, rd, wr):
                ACT(lambda e: e.activation(out=dst, in_=src, func=AF.Ln, scale=scale, bias=EPS), rd, wr)
                ACT(lambda e: e.activation(out=dst, in_=dst, func=AF.Exp, scale=-0.5), wr, wr)

            def loads1(i):
                k = i % 2
                ts = slice(i * 128, (i + 1) * 128)
                dma(XT[k][:], xTd[:, :, ts], writes=[r_xt[k]])

            def cast1(i):
                k = i % 2
                V(lambda e: e.tensor_copy(out=XB[k][:], in_=XT[k][:]), [r_xt[k]], [r_xb[k]])
            V(lambda e: e.memset(ONESB, 1.0), [], [res('onesb')])
            loads1(0)
            cast1(0)
            for i in range(NT):
                k = i % 2
                ts = slice(i * 128, (i + 1) * 128)
                if i + 1 < NT:
                    loads1(i + 1)
                ck('p1a')
                ACT(lambda e, k=k: e.activation(out=SQX, in_=XT[k][:].rearrange("p c t -> p (c t)"), func=AF.Square),
                    [r_xt[k]], [res('sqx')])
                bq = nextpb()
                for c in range(8):
                    PE(lambda e, c=c, bq=bq: e.matmul(pbs[bq][:, 0:1], lhsT=SQX[:, c * 128:(c + 1) * 128], rhs=ONESB[:, 0:1],
                                                      start=(c == 0), stop=(c == 7)),
                       [res('sqx'), res('onesb')], [PR[bq]], accum=(c > 0))
                rsqrt_act(RSTD[:, i:i + 1], pbs[bq][:, 0:1], 1.0 / D, [PR[bq]], [res('rstd%d' % i)])
                V(lambda e, i=i: e.tensor_scalar(out=NRSTD[:, i:i + 1], in0=RSTD[:, i:i + 1], scalar1=-1.0, scalar2=None, op0=ALU.mult),
                  [res('rstd%d' % i)], [res('nrstd%d' % i)])
                ck('p1b')
                b0 = nextpb()
                b1 = nextpb()
                for c in range(8):
                    PE(lambda e, c=c, k=k, b0=b0: e.matmul(pbs[b0][:, 0:384], lhsT=XB[k][:, c, :], rhs=WIN1[:, c, 0:384],
                                                          start=(c == 0), stop=(c == 7)),
                       [r_xb[k], res('win1')], [PR[b0]], accum=(c > 0))
                for c in range(8):
                    PE(lambda e, c=c, k=k, b1=b1: e.matmul(pbs[b1][:, 0:384], lhsT=XB[k][:, c, :], rhs=WIN1[:, c, 384:768],
                                                          start=(c == 0), stop=(c == 7)),
                       [r_xb[k], res('win1')], [PR[b1]], accum=(c > 0))
                rs = RSTD[:, i:i + 1]
                V(lambda e, b0=b0, rs=rs: e.tensor_scalar(out=KV1[:, 0:384], in0=pbs[b0][:, 0:384], scalar1=rs, scalar2=None,
                                                          op0=ALU.mult), [PR[b0], res('rstd%d' % i)], [res('kv1a')])
                V(lambda e, b1=b1, rs=rs: e.tensor_scalar(out=KV1[:, 384:768], in0=pbs[b1][:, 0:384], scalar1=rs, scalar2=None,
                                                          op0=ALU.mult), [PR[b1], res('rstd%d' % i)], [res('kv1b')])
                if i + 1 < NT:
                    cast1(i + 1)
                ck('p1c')
                ACT(lambda e: e.activation(out=KVB[:, 0:256], in_=KV1[:, 0:256], func=AF.Copy), [res('kv1a')], [res('kvb_c')])
                ACT(lambda e: e.activation(out=SQ1, in_=KV1[:, 256:512], func=AF.Square), [res('kv1a'), res('kv1b')], [res('sq1')])
                V(lambda e: e.tensor_reduce(out=ST1[:, 0:4], in_=SQ1.rearrange("p (a b) -> p a b", b=64), axis=AX.X, op=ALU.add),
                  [res('sq1')], [res('st1')])
                rsqrt_act(ST1[:, 0:4], ST1[:, 0:4], 1.0 / 64, [res('st1')], [res('st1')])
                V(lambda e: e.tensor_tensor(out=SQ1.rearrange("p (a b) -> p a b", b=64),
                                            in0=KV1[:, 256:512].rearrange("p (a b) -> p a b", b=64),
                                            in1=ST1[:, 0:4, None].to_broadcast([128, 4, 64]), op=ALU.mult),
                  [res('kv1a'), res('kv1b'), res('st1')], [res('sq1')])
                V(lambda e: e.tensor_tensor(out=KVB[:, 256:384], in0=SQ1[:, 0:128], in1=KSW[:], op=ALU.mult),
                  [res('sq1'), res('ksw')], [res('kvb_s')])
                V(lambda e: e.tensor_tensor(out=KVB[:, 384:512], in0=SQ1[:, 128:256], in1=KWW[:], op=ALU.mult),
                  [res('sq1'), res('kww')], [res('kvb_w')])
                ACT(lambda e, i=i: e.activation(out=VS[:, i, :, 0:64], in_=KV1[:, 512:640].rearrange("p (g d) -> p g d", g=2), func=AF.Copy),
                  [res('kv1b')], [r_vs[i]])
                ACT(lambda e, i=i: e.activation(out=VW[:, i, :, 0:64], in_=KV1[:, 640:768].rearrange("p (g d) -> p g d", g=2), func=AF.Copy),
                  [res('kv1b')], [r_vw[i]])
                ck('p1d')
                bt = nextpb()
                ptb = pbs[bt][:, :].bitcast(BF16)
                for m, rr in enumerate(('kvb_c', 'kvb_c', 'kvb_s', 'kvb_w')):
                    PE(lambda e, m=m, ptb=ptb: e.transpose(out=ptb[:, m * 128:(m + 1) * 128], in_=KVB[:, m * 128:(m + 1) * 128],
                                                           identity=IDB[:]),
                       [res(rr), res('idb')], [PR[bt]], accum=(m > 0))
                ck('p1e')
                V(lambda e, ptb=ptb, i=i: e.tensor_copy(out=CTS[:, :, :, i * 8:(i + 1) * 8],
                                                        in_=ptb[:, 0:256].rearrange("p (g m r) -> p g r m", g=2, r=16)),
                  [PR[bt]], [r_ct[i]])
                ck('p1f')
                V(lambda e, ptb=ptb, ts=ts: e.tensor_copy(out=KSA[0:64, 0, ts], in_=ptb[0:64, 256:384]),
                  [PR[bt]], [r_ks[i]])
                V(lambda e, ptb=ptb, ts=ts: e.tensor_copy(out=KSA[64:128, 1, ts], in_=ptb[64:128, 256:384]),
                  [PR[bt]], [r_ks[i]])
                ck('p1g')
                V(lambda e, ptb=ptb, ts=ts: e.tensor_copy(out=KWT[:, ts], in_=ptb[:, 384:512]), [PR[bt]], [r_kw[i]])
                for st_ in W2STEPS[i * len(W2STEPS) // NT:(i + 1) * len(W2STEPS) // NT]:
                    stage_cast(st_[0], st_[1], scale=st_[2], rd=st_[3], wr=st_[4], nslots=2)
                ck('p1t%d' % i)

            ck('pass1')
            for kind in range(2):
                bb = nextpb()
                rows = slice(kind * 64, (kind + 1) * 64)
                for hc in range(2):
                    col = kind * 2 + hc
                    for l in range(32):
                        PE(lambda e, rows=rows, hc=hc, l=l, col=col, bb=bb: e.matmul(
                            pbs[bb][:, col:col + 1], lhsT=W1[rows, l, hc * 128:(hc + 1) * 128], rhs=POSB[rows, l:l + 1],
                            start=(l == 0), stop=(l == 31)),
                           [res('w1'), res('posb')], [PR[bb]], accum=not (hc == 0 and l == 0))
                V(lambda e, bb=bb, kind=kind: e.tensor_copy(out=CBIAS[:, kind * 2:kind * 2 + 2], in_=pbs[bb][:, kind * 2:kind * 2 + 2]),
                  [PR[bb]], [res('cbias')])
            ck('c1')
            ck('c1b')
            G(lambda e: e.memset(HTF, 0.0), [], [res('ht')])
            for kind in range(2):
                rows = slice(kind * 64, (kind + 1) * 64)
                for g in range(2):
                    for hc in range(2):
                        b = nextpb()
                        for l in range(32):
                            PE(lambda e, rows=rows, g=g, hc=hc, l=l, b=b: e.matmul(
                                pbs[b][:, 0:255], lhsT=W1[rows, l, hc * 128:(hc + 1) * 128],
                                rhs=CTS[rows, g, l % 16, (l // 16):(l // 16) + 255], start=(l == 0), stop=(l == 31)),
                               [res('w1')] + r_ct, [PR[b]], accum=(l > 0))
                        ck('c2')
                        col = kind * 2 + hc
                        ACT(lambda e, b=b, col=col, kind=kind, g=g, hc=hc: e.activation(
                            out=HT[:, kind, g, hc, 0:255], in_=pbs[b][:, 0:255], func=AF.Silu, bias=CBIAS[:, col:col + 1]),
                            [PR[b], res('cbias')], [res('ht')])
            ck('c3')
            for nt_ in range(2):
                ns = slice(nt_ * 128, (nt_ + 1) * 128)
                b = nextpb()
                for kind in range(2):
                    for g in range(2):
                        for hc in range(2):
                            cs = slice(kind * 128 + g * 64, kind * 128 + g * 64 + 64)
                            PE(lambda e, kind=kind, g=g, hc=hc, cs=cs, ns=ns, b=b: e.matmul(
                                pbs[b][:, cs], lhsT=HT[:, kind, g, hc, ns], rhs=W2C[:, kind, hc, :],
                                start=(hc == 0), stop=(hc == 1)),
                               [res('ht'), res('w2c')], [PR[b]], accum=not (kind == 0 and g == 0 and hc == 0))
                ck('c4')
                V(lambda e, b=b, nt_=nt_: e.tensor_copy(out=VCM[:, nt_, :, 0:64],
                                                       in_=pbs[b][:, 128:256].rearrange("p (g d) -> p g d", g=2)),
                  [PR[b]], [res('vcm')])
                ck('c5')
                ACT(lambda e, b=b: e.activation(out=SQ1[:, 0:128], in_=pbs[b][:, 0:128], func=AF.Square), [PR[b]], [res('sq1')])
                ck('c6')
                V(lambda e: e.tensor_reduce(out=ST1[:, 0:2], in_=SQ1[:, 0:128].rearrange("p (a b) -> p a b", b=64), axis=AX.X,
                                            op=ALU.add), [res('sq1')], [res('st1')])
                rsqrt_act(ST1[:, 0:2], ST1[:, 0:2], 1.0 / 64, [res('st1')], [res('st1')])
                V(lambda e, b=b: e.tensor_tensor(out=SQ1[:, 0:128].rearrange("p (a b) -> p a b", b=64),
                                                 in0=pbs[b][:, 0:128].rearrange("p (a b) -> p a b", b=64),
                                                 in1=ST1[:, 0:2, None].to_broadcast([128, 2, 64]), op=ALU.mult),
                  [PR[b], res('st1')], [res('sq1')])
                V(lambda e: e.tensor_tensor(out=KCTM, in0=SQ1[:, 0:128], in1=KCW[:], op=ALU.mult),
                  [res('sq1'), res('kcw')], [res('kctm')])
                ck('c7')
                bt = nextpb()
                ptb = pbs[bt][:, :].bitcast(BF16)
                PE(lambda e, ptb=ptb: e.transpose(out=ptb[:, 0:128], in_=KCTM, identity=IDB[:]),
                   [res('kctm'), res('idb')], [PR[bt]])
                V(lambda e, ptb=ptb, ns=ns: e.tensor_copy(out=KCT[:, ns], in_=ptb[:, 0:128]), [PR[bt]], [res('kct')])

            ck('compress')
            S.barrier()

            pbset[:] = [0, 1]
            ABF = [2, 3]
            ABB = [4, 5]
            UBB = 6
            UF = 7
            cb = Carver()
            RQK = cb.get(512, F32)
            TMP1 = cb.get(256, F32)
            TMP2 = cb.get(256, F32)
            ROT = cb.get(512, F32)
            QKB = [cb.get(1024, BF16) for _ in range(2)]
            VTM = [cb.get(512, BF16) for _ in range(2)]
            QKT = [cb.get(1024, BF16) for _ in range(2)]
            STB = cb.get(1024, BF16)
            GG = [cb.get(1024, BF16) for _ in range(3)]
            NQ = cb.get(512, F32)
            GT = NQ
            SQ = cb.get(512, F32)
            SQB = cb.get(512, F32)
            QN = cb.get(512, BF16)
            QA = [cb.get(1024, BF16).rearrange("p (v k q) -> p v k q", v=2, k=4) for _ in range(3)]
            NP = 6
            PB_ = [cb.get(512, BF16) for _ in range(NP)]
            USBF = [cb.get(2 * 2 * 260, F32).rearrange("p (x g c) -> p x g c", x=2, g=2) for _ in range(2)]
            USBS = cb.get(2 * 260, F32).rearrange("p (g c) -> p g c", g=2)
            IMP = cb.get(128, F32)
            SCO = cb.get(128, F32)
            SC2 = cb.get(128, F32)
            M8 = cb.get(32, F32)
            SELB = cb.get(128, BF16)
            ONS = cb.get(512, F32)
            ORT = cb.get(512, F32)
            YB = [cb.get(1024, BF16) for _ in range(2)]
            YT = cb.get(1024, BF16).rearrange("p (c t) -> p c t", c=8)
            GL = [cb.get(24, F32) for _ in range(3)]
            CO = cb.get(24, F32)
            DEN = cb.get(24, F32)
            SS = cb.get(16, F32)
            r_p = [Res("p%d" % i) for i in range(NP)]
            pcount = [0]
            COLS = {'rq': 0, 'rk': 512, 'rv': 1024, 'rg': 1536, 'nq': 2048, 'ng': 2560, 'gl': 3072}
            SCALE = 0.125

            def proj(k, c0, n, b):
                for c in range(8):
                    PE(lambda e, c=c: e.matmul(pbs[b][:, 0:n], lhsT=XB[k][:, c, :], rhs=WIN2[:, c, c0:c0 + n],
                                               start=(c == 0), stop=(c == 7)),
                       [r_xb[k], res('win2')], [PR[b]], accum=(c > 0))

            def run_blocks(blocks, abset):
                nb = len(blocks)
                if nb == 0:
                    return
                abl = [None] * nb

                def qk(t):
                    ab = abset[t % 2]
                    bl = blocks[t]
                    PE(lambda e: e.matmul(pbs[ab][:, :], lhsT=bl[0], rhs=bl[2], start=True, stop=(bl[4] is None)),
                       list(bl[1]) + list(bl[3]), [PR[ab]])
                    if bl[4] is not None:
                        PE(lambda e: e.matmul(pbs[ab][:, :], lhsT=IDB[:], rhs=bl[4], start=False, stop=True),
                           [res('idb'), res('mb')], [PR[ab]], accum=True)
                    abl[t] = ab
                qk(0)
                for t in range(nb):
                    if t + 1 < nb:
                        qk(t + 1)
                    (lhsT_, lres_, rhs_, rres_, mask_pe, mask_ap, mask_res, vfn, v_res, ub, first, last, after) = blocks[t]
                    ab = abl[t]
                    pk = pcount[0] % NP
                    pcount[0] += 1
                    P = PB_[pk]
                    ACT(lambda e: e.activation(out=P, in_=pbs[ab][:, :], func=AF.Exp, scale=SCALE), [PR[ab]], [r_p[pk]])
                    if mask_ap is not None:
                        V(lambda e: e.tensor_tensor(out=P.rearrange("p (h q) -> p h q", h=4), in0=P.rearrange("p (h q) -> p h q", h=4),
                                                    in1=mask_ap[:, None, :].to_broadcast([128, 4, 128]), op=ALU.mult),
                          [r_p[pk]] + list(mask_res), [r_p[pk]])
                    for h in range(4):
                        PE(lambda e, h=h: e.matmul(pbs[ub][:, h * 65:(h + 1) * 65], lhsT=P[:, h * 128:(h + 1) * 128], rhs=vfn,
                                                   start=(first and h == 0), stop=(last and h == 3)),
                           [r_p[pk]] + list(v_res), [PR[ub]], accum=not (first and h == 0))
                    if after is not None:
                        after(P, pk)
                    yield 0.75

            def load_xs(i):
                k = i % 2
                dma(XS[k][:, :], A['x'][i * 128:(i + 1) * 128, :], writes=[r_xs[k]])

            def load_xt(i):
                k = i % 2
                dma(XT[k][:], xTd[:, :, i * 128:(i + 1) * 128], writes=[r_xt[k]])

            def load_tabs(i):
                k = i % 2
                dma(TAB[k][:], A['tab'][i], writes=[r_tab[k]])
                dma(CMK[k][:], A['cmask'][i], writes=[r_cmk[k]])

            def cast_xb(i):
                k = i % 2
                V(lambda e: e.tensor_copy(out=XB[k][:], in_=XT[k][:]), [r_xt[k]], [r_xb[k]])

            def stageA(i):
                k = i % 2
                k3 = i % 3
                rs = RSTD[:, i:i + 1]
                r_rs = res('rstd%d' % i)
                COS = TAB[k][:, 0:32]
                SIN = TAB[k][:, 32:64]
                gg = GG[k3]
                qa = QA[k3]
                gl = GL[k3]
                qkb = QKB[k]
                qkt = QKT[k]
                vtm = VTM[k]
                r_gate = res('gate%d' % k3)
                r_qaq = res('qa_q%d' % k3)
                r_gl = res('gl%d' % k3)
                r_qkt = res('qkt%d' % k)
                r_vtm = res('vtm%d' % k)
                nrs = NRSTD[:, i:i + 1]
                r_nrs = res('nrstd%d' % i)

                def rope(nm, off, gcol):
                    src = RQK.rearrange("p (h d) -> p h d", h=8)
                    x1 = src[:, :, 0:32]
                    x2 = src[:, :, 32:64]
                    cosb = COS[:, None, :].to_broadcast([128, 8, 32])
                    sinb = SIN[:, None, :].to_broadcast([128, 8, 32])
                    rot = ROT.rearrange("p (h d) -> p h d", h=8)
                    t1 = TMP1.rearrange("p (h d) -> p h d", h=8)
                    t2 = TMP2.rearrange("p (h d) -> p h d", h=8)
                    rr = [res('rqk'), r_tab[k]]
                    V(lambda e: e.tensor_tensor(out=t1, in0=x1, in1=cosb, op=ALU.mult), rr, [res('t1')])
                    V(lambda e: e.tensor_tensor(out=t2, in0=x2, in1=sinb, op=ALU.mult), rr, [res('t2')])
                    V(lambda e: e.tensor_tensor(out=rot[:, :, 0:32], in0=t1, in1=t2, op=ALU.subtract),
                      [res('t1'), res('t2')], [res('rot')])
                    V(lambda e: e.tensor_tensor(out=t1, in0=x1, in1=sinb, op=ALU.mult), rr, [res('t1')])
                    V(lambda e: e.tensor_tensor(out=t2, in0=x2, in1=cosb, op=ALU.mult), rr, [res('t2')])
                    V(lambda e: e.tensor_tensor(out=rot[:, :, 32:64], in0=t1, in1=t2, op=ALU.add),
                      [res('t1'), res('t2')], [res('rot')])
                    gt = GQK[:, gcol:gcol + 8]
                    V(lambda e: e.tensor_tensor(out=qkb[:, off:off + 512].rearrange("p (h d) -> p h d", h=8), in0=rot,
                                                in1=gt[:, :, None].to_broadcast([128, 8, 64]), op=ALU.mult),
                      [res('rot'), res('gqk')], [res('qkb%s%d' % (nm, k))])

                b = nextpb()
                proj(k, COLS['rq'], 512, b)
                V(lambda e: e.tensor_scalar(out=RQK, in0=pbs[b][:, :], scalar1=rs, scalar2=None, op0=ALU.mult),
                  [PR[b], r_rs], [res('rqk')])
                rope('rq', 0, 0)
                b = nextpb()
                proj(k, COLS['rv'], 512, b)
                ACT(lambda e: e.activation(out=vtm, in_=pbs[b][:, :], func=AF.Copy, scale=rs), [PR[b], r_rs], [r_vtm])
                for nm, off in (('rg', 0), ('ng', 512)):
                    b = ABF[0] if nm == 'rg' else ABF[1]
                    proj(k, COLS[nm], 512, b)
                    ACT(lambda e: e.activation(out=GT, in_=pbs[b][:, :], func=AF.Exp, scale=nrs), [PR[b], r_nrs], [res('nq')])
                    ACT(lambda e: e.activation(out=GT, in_=GT, func=AF.Ln, bias=1.0), [res('nq')], [res('nq')])
                    ACT(lambda e: e.activation(out=GT, in_=GT, func=AF.Exp, scale=-1.0), [res('nq')], [res('nq')])
                    V(lambda e: e.scalar_tensor_tensor(out=gg[:, off:off + 512], in0=pbs[b][:, :], scalar=rs, in1=GT,
                                                       op0=ALU.mult, op1=ALU.mult), [PR[b], r_rs, res('nq')], [r_gate])
                b = UF
                proj(k, COLS['gl'], 24, b)
                V(lambda e: e.scalar_tensor_tensor(out=gl, in0=pbs[b][:, 0:24], scalar=rs, in1=BG[:], op0=ALU.mult, op1=ALU.add),
                  [PR[b], r_rs, res('bg')], [r_gl])
                ACT(lambda e: e.activation(out=gl, in_=gl, func=AF.Exp, scale=-1.0), [r_gl], [r_gl])
                ACT(lambda e: e.activation(out=gl, in_=gl, func=AF.Ln, bias=1.0), [r_gl], [r_gl])
                ACT(lambda e: e.activation(out=gl, in_=gl, func=AF.Exp, scale=-1.0), [r_gl], [r_gl])
                b = nextpb()
                proj(k, COLS['nq'], 512, b)
                V(lambda e: e.tensor_scalar(out=NQ, in0=pbs[b][:, :], scalar1=rs, scalar2=None, op0=ALU.mult),
                  [PR[b], r_rs], [res('nq')])
                b = nextpb()
                proj(k, COLS['rk'], 512, b)
                V(lambda e: e.tensor_scalar(out=RQK, in0=pbs[b][:, :], scalar1=rs, scalar2=None, op0=ALU.mult),
                  [PR[b], r_rs], [res('rqk')])
                yield 6.0
                rope('rk', 512, 8)
                ACT(lambda e: e.activation(out=SQ, in_=NQ, func=AF.Square), [res('nq')], [res('sq')])
                V(lambda e: e.tensor_reduce(out=SS[:, 0:8], in_=SQ.rearrange("p (h d) -> p h d", h=8), axis=AX.X, op=ALU.add),
                  [res('sq')], [res('ss')])
                rsqrt_act(SS[:, 0:8], SS[:, 0:8], 1.0 / 64, [res('ss')], [res('ss')])
                V(lambda e: e.tensor_tensor(out=SQ.rearrange("p (h d) -> p h d", h=8), in0=NQ.rearrange("p (h d) -> p h d", h=8),
                                            in1=SS[:, 0:8, None].to_broadcast([128, 8, 64]), op=ALU.mult),
                  [res('nq'), res('ss')], [res('sq')])
                V(lambda e: e.tensor_tensor(out=QN, in0=SQ, in1=QW[:], op=ALU.mult), [res('sq'), res('qw')], [res('qn')])
                yield 4.0
                bt = nextpb()
                ptb = pbs[bt][:, :].bitcast(BF16)
                for m in range(8):
                    nm = 'rq' if m < 4 else 'rk'
                    PE(lambda e, m=m: e.transpose(out=ptb[:, m * 128:(m + 1) * 128], in_=qkb[:, m * 128:(m + 1) * 128],
                                                  identity=IDB[:]),
                       [res('qkb%s%d' % (nm, k)), res('idb')], [PR[bt]], accum=(m > 0))
                V(lambda e: e.tensor_copy(out=qkt, in_=ptb), [PR[bt]], [r_qkt])
                if i + 1 < NT:
                    cast_xb(i + 1)
                yield 0.5
                bt2 = nextpb()
                ptq = pbs[bt2][:, :].bitcast(BF16)
                for m in range(4):
                    PE(lambda e, m=m: e.transpose(out=ptq[:, m * 128:(m + 1) * 128], in_=QN[:, m * 128:(m + 1) * 128],
                                                  identity=IDB[:]),
                       [res('qn'), res('idb')], [PR[bt2]], accum=(m > 0))
                V(lambda e: e.tensor_copy(out=qa[0:64, 0].rearrange("p k q -> p (k q)"), in_=ptq[0:64, 0:512]),
                  [PR[bt2]], [r_qaq])
                V(lambda e: e.tensor_copy(out=qa[64:128, 1].rearrange("p k q -> p (k q)"), in_=ptq[64:128, 0:512]),
                  [PR[bt2]], [r_qaq])
                yield 0.5

            def stageB(i):
                k = i % 2
                k3 = i % 3
                gg = GG[k3]
                qa = QA[k3]
                yb = YB[k]
                usbf = USBF[k]
                qkb = QKB[k]
                qkt = QKT[k]
                vtm = VTM[k]
                r_gate = res('gate%d' % k3)
                r_qaq = res('qa_q%d' % k3)
                r_qas = res('qa_s%d' % k3)
                r_qkt = res('qkt%d' % k)
                r_vtm = res('vtm%d' % k)
                QsT = qkt[:, 0:512].rearrange("p (m t) -> p m t", m=4)
                KsT = qkt[:, 512:1024].rearrange("p (m t) -> p m t", m=4)
                Kstm = qkb[:, 512:1024]
                for half in range(2):
                    ab = ABF[half]
                    for hh in range(4):
                        h = 2 * hh + half
                        rows = slice((h % 2) * 64, (h % 2) * 64 + 64)
                        PE(lambda e, h=h, hh=hh, rows=rows: e.matmul(pbs[ab][:, hh * 128:(hh + 1) * 128], lhsT=KsT[rows, h // 2, :],
                                                                    rhs=QsT[rows, h // 2, :], start=True, stop=True),
                           [r_qkt], [PR[ab]], accum=(hh > 0))
                    V(lambda e: e.tensor_tensor(
                        out=STB[:, half * 512:(half + 1) * 512].rearrange("p (h q) -> p h q", h=4),
                        in0=pbs[ab][:, :].rearrange("p (h q) -> p h q", h=4),
                        in1=MASKS[:, None, 0:128].to_broadcast([128, 4, 128]), op=ALU.mult),
                      [PR[ab], res('masks')], [res('stb%d' % half)])
                yield 2.0
                ob = UF
                for h in range(8):
                    rows = slice((h % 2) * 64, (h % 2) * 64 + 64)
                    so = (h % 2) * 512 + (h // 2) * 128
                    PE(lambda e, h=h, so=so: e.matmul(pbs[ob][:, h * 64:(h + 1) * 64], lhsT=STB[:, so:so + 128],
                                                      rhs=vtm[:, h * 64:(h + 1) * 64], start=True, stop=False),
                       [res('stb%d' % (h % 2)), r_vtm], [PR[ob]], accum=(h > 0))
                    PE(lambda e, h=h, rows=rows: e.matmul(pbs[ob][:, h * 64:(h + 1) * 64], lhsT=QsT[rows, h // 2, :],
                                                          rhs=RSB[rows, h // 2, :], start=False, stop=True),
                       [r_qkt, res('rsb')], [PR[ob]], accum=True)
                bkv = nextpb()
                for m in range(4):
                    PE(lambda e, m=m: e.matmul(pbs[bkv][:, m * 128:(m + 1) * 128], lhsT=Kstm[:, m * 128:(m + 1) * 128],
                                               rhs=vtm[:, m * 128:(m + 1) * 128], start=True, stop=True),
                       [res('qkbrk%d' % k), r_vtm], [PR[bkv]], accum=(m > 0))
                kvv = pbs[bkv][:, :].rearrange("p (m c) -> p m c", m=4)
                V(lambda e: e.tensor_tensor(out=RST[0:64], in0=RST[0:64], in1=kvv[0:64, :, 0:64], op=ALU.add),
                  [PR[bkv], res('rst')], [res('rst')])
                V(lambda e: e.tensor_tensor(out=RST[64:128], in0=RST[64:128], in1=kvv[64:128, :, 64:128], op=ALU.add),
                  [PR[bkv], res('rst')], [res('rst')])
                V(lambda e: e.tensor_tensor(out=RST[:].rearrange("p m c -> p (m c)"), in0=RST[:].rearrange("p m c -> p (m c)"),
                                            in1=GCT[:], op=ALU.mult), [res('rst'), res('gct')], [res('rst')])
                V(lambda e: e.tensor_copy(out=RSB[:], in_=RST[:]), [res('rst')], [res('rsb')])
                yield 0.5
                ACT(lambda e: e.activation(out=ORT, in_=pbs[ob][:, :], func=AF.Square), [PR[ob]], [res('ort')])
                V(lambda e: e.tensor_reduce(out=SS[:, 8:16], in_=ORT.rearrange("p (h d) -> p h d", h=8), axis=AX.X, op=ALU.add),
                  [res('ort')], [res('ss2')])
                rsqrt_act(SS[:, 8:16], SS[:, 8:16], 1.0 / 64, [res('ss2')], [res('ss2')])
                V(lambda e: e.tensor_tensor(out=ORT.rearrange("p (h d) -> p h d", h=8),
                                            in0=pbs[ob][:, :].rearrange("p (h d) -> p h d", h=8),
                                            in1=SS[:, 8:16, None].to_broadcast([128, 8, 64]), op=ALU.mult),
                  [PR[ob], res('ss2')], [res('ort')])
                V(lambda e: e.tensor_tensor(out=ORT, in0=ORT, in1=RETW[:], op=ALU.mult), [res('ort'), res('retw')], [res('ort')])
                V(lambda e: e.tensor_tensor(out=yb[:, 0:512], in0=ORT, in1=gg[:, 0:512], op=ALU.mult),
                  [res('ort'), r_gate], [res('yb0%d' % k)])
                yield 3.0
                nts = [0] if 8 * i + 6 < 128 else [0, 1]
                for g in range(2):
                    rows = slice(g * 64, (g + 1) * 64)
                    qrhs = qa[rows, g].rearrange("p k q -> p (k q)")
                    ub = UF
                    ib = nextpb()
                    cmp_blocks = []
                    for idx, nt_ in enumerate(nts):
                        ns = slice(nt_ * 128, (nt_ + 1) * 128)

                        def after(P, pk, ib=ib, nt_=nt_, idx=idx, g=g, ub=ub):
                            for h in range(4):
                                PE(lambda e, h=h: e.matmul(pbs[ib][:, h * 64:(h + 1) * 64], lhsT=P[:, h * 128:(h + 1) * 128],
                                                           rhs=MOV[:, nt_, :], start=(idx == 0 and h == 0),
                                                           stop=(idx == len(nts) - 1 and h == 3)),
                                   [r_p[pk], res('mov')], [PR[ib]], accum=not (idx == 0 and h == 0))
                            if idx == len(nts) - 1:
                                V(lambda e: e.tensor_copy(out=usbf[:, 0, g, :], in_=pbs[ub][:, 0:260]), [PR[ub]], [res('usbc%d%d' % (k, g))])
                        cmp_blocks.append((KCT[rows, ns], [res('kct')], qrhs, [r_qaq], None,
                                           CMK[k][:, nt_ * 128:(nt_ + 1) * 128], [r_cmk[k]],
                                           VCM[:, nt_, g, :], [res('vcm')], ub, idx == 0, idx == len(nts) - 1, after))
                    for _ in run_blocks(cmp_blocks, ABF):
                        pass
                    r_usb = res('usbc%d%d' % (k, g))
                    ucv = usbf[:, 0, g, :].rearrange("p (h c) -> p h c", h=4)
                    V(lambda e: e.tensor_scalar(out=DEN[:, g * 4:(g + 1) * 4], in0=ucv[:, :, 64], scalar1=1e-30, scalar2=None,
                                                op0=ALU.max), [r_usb], [res('den0%d' % g)])
                    V(lambda e: e.reciprocal(out=DEN[:, g * 4:(g + 1) * 4], in_=DEN[:, g * 4:(g + 1) * 4]),
                      [res('den0%d' % g)], [res('den0%d' % g)])
                    for h in range(4):
                        if h == 0:
                            V(lambda e: e.tensor_scalar(out=IMP[:, g * 64:(g + 1) * 64], in0=pbs[ib][:, 0:64],
                                                        scalar1=DEN[:, g * 4:g * 4 + 1], scalar2=None, op0=ALU.mult),
                              [PR[ib], res('den0%d' % g)], [res('imp%d' % g)])
                        else:
                            V(lambda e, h=h: e.scalar_tensor_tensor(
                                out=IMP[:, g * 64:(g + 1) * 64], in0=pbs[ib][:, h * 64:(h + 1) * 64],
                                scalar=DEN[:, g * 4 + h:g * 4 + h + 1], in1=IMP[:, g * 64:(g + 1) * 64], op0=ALU.mult, op1=ALU.add),
                              [PR[ib], res('den0%d' % g), res('imp%d' % g)], [res('imp%d' % g)])
                    yield 1.0
                    sco = SCO[:, g * 64:(g + 1) * 64]
                    sc2 = SC2[:, g * 64:(g + 1) * 64]
                    V(lambda e: e.tensor_tensor(out=sco, in0=IMP[:, g * 64:(g + 1) * 64], in1=TAB[k][:, 64:128], op=ALU.add),
                      [res('imp%d' % g), r_tab[k]], [res('sco%d' % g)])
                    V(lambda e: e.max(out=M8[:, g * 16:g * 16 + 8], in_=sco), [res('sco%d' % g)], [res('m8a%d' % g)])
                    V(lambda e: e.match_replace(out=sc2, in_to_replace=M8[:, g * 16:g * 16 + 8], in_values=sco, imm_value=-3e38),
                      [res('sco%d' % g), res('m8a%d' % g)], [res('sc2%d' % g)])
                    V(lambda e: e.max(out=M8[:, g * 16 + 8:g * 16 + 16], in_=sc2), [res('sc2%d' % g)], [res('m8b%d' % g)])
                    V(lambda e: e.tensor_scalar(out=SELB[:, (1 - g) * 64:(2 - g) * 64], in0=sco,
                                                scalar1=M8[:, g * 16 + 15:g * 16 + 16], scalar2=NEGB, op0=ALU.is_lt, op1=ALU.mult),
                      [res('sco%d' % g), res('m8b%d' % g)], [res('selb%d' % g)])
                    yield (0.5 if g == 0 else 3.0)
                win_blocks = []
                for g in range(2):
                    rows = slice(g * 64, (g + 1) * 64)
                    qrhs = qa[rows, g].rearrange("p k q -> p (k q)")
                    ub = UF
                    j0 = max(0, i - 4)
                    for j in range(j0, i + 1):
                        js = slice(j * 128, (j + 1) * 128)
                        if j == i:
                            mk = MB[:, 0, :]
                        elif j == i - 4:
                            mk = MB[:, 1, :]
                        else:
                            mk = None
                        after = None
                        if j == i:
                            def after(P, pk, ub=ub, g=g):
                                V(lambda e: e.tensor_copy(out=usbf[:, 1, g, :], in_=pbs[ub][:, 0:260]), [PR[ub]], [res('usbw%d%d' % (k, g))])
                        win_blocks.append((KWT[rows, js], [r_kw[j]], qrhs, [r_qaq], mk, None, [],
                                           VW[:, j, g, :], [r_vw[j]], ub, j == j0, j == i, after))
                yield from run_blocks(win_blocks, ABF)
                bs = nextpb()
                pts = pbs[bs][:, :].bitcast(BF16)
                PE(lambda e: e.transpose(out=pts[:, 0:128], in_=SELB, identity=IDB[:]),
                   [res('selb0'), res('selb1'), res('idb')], [PR[bs]])
                V(lambda e: e.tensor_copy(out=qa[64:128, 0], in_=pts[64:128, None, 0:128].to_broadcast([64, 4, 128])),
                  [PR[bs]], [r_qas])
                V(lambda e: e.tensor_copy(out=qa[0:64, 1], in_=pts[0:64, None, 0:128].to_broadcast([64, 4, 128])),
                  [PR[bs]], [r_qas])
                yield 0.5

            def back(i):
                k = i % 2
                k3 = i % 3
                ts = slice(i * 128, (i + 1) * 128)
                gg = GG[k3]
                qa = QA[k3]
                gl = GL[k3]
                yb = YB[k]
                usbf = USBF[k]
                r_gate = res('gate%d' % k3)
                r_qaq = res('qa_q%d' % k3)
                r_qas = res('qa_s%d' % k3)
                r_gl = res('gl%d' % k3)
                r_uf = [res('usbc%d%d' % (k, g)) for g in range(2)] + [res('usbw%d%d' % (k, g)) for g in range(2)]
                r_us = [res('usbs0'), res('usbs1')]
                uf = usbf.rearrange("p x g (h c) -> p x (g h) c", h=4)
                us = USBS.rearrange("p g (h c) -> p (g h) c", h=4)
                cov = CO.rearrange("p (x h) -> p x h", x=3)
                glv = gl.rearrange("p (x h) -> p x h", x=3)
                onv = ONS.rearrange("p (h d) -> p h d", h=8)
                sqv = SQB.rearrange("p (h d) -> p h d", h=8)
                for x_, xb_ in ((0, 0), (1, 2)):
                    V(lambda e: e.tensor_scalar(out=cov[:, xb_, :], in0=uf[:, x_, :, 64], scalar1=1e-30, scalar2=None, op0=ALU.max),
                      r_uf, [res('co%d' % xb_)])
                    V(lambda e: e.reciprocal(out=cov[:, xb_, :], in_=cov[:, xb_, :]), [res('co%d' % xb_)], [res('co%d' % xb_)])
                    V(lambda e: e.tensor_tensor(out=cov[:, xb_, :], in0=cov[:, xb_, :], in1=glv[:, xb_, :], op=ALU.mult),
                      [res('co%d' % xb_), r_gl], [res('co%d' % xb_)])
                V(lambda e: e.tensor_tensor(out=onv, in0=uf[:, 0, :, 0:64], in1=cov[:, 0, :, None].to_broadcast([128, 8, 64]), op=ALU.mult),
                  r_uf + [res('co0')], [res('ons')])
                V(lambda e: e.tensor_tensor(out=sqv, in0=uf[:, 1, :, 0:64], in1=cov[:, 2, :, None].to_broadcast([128, 8, 64]), op=ALU.mult),
                  r_uf + [res('co2')], [res('sqb')])
                V(lambda e: e.tensor_tensor(out=ONS, in0=ONS, in1=SQB, op=ALU.add), [res('ons'), res('sqb')], [res('ons')])
                yield

                def ytrans(half):
                    by = nextpb()
                    pty = pbs[by][:, :].bitcast(BF16)
                    for m in range(4):
                        c = half * 4 + m
                        PE(lambda e, m=m, c=c: e.transpose(out=pty[:, m * 128:(m + 1) * 128], in_=yb[:, c * 128:(c + 1) * 128],
                                                           identity=IDB[:]),
                           [res('yb%d%d' % (half, k)), res('idb')], [PR[by]], accum=(m > 0))
                    V(lambda e: e.tensor_copy(out=YT[:, half * 4:(half + 1) * 4, :].rearrange("p c t -> p (c t)"),
                                              in_=pty[:, 0:512]), [PR[by]], [res('yt%d' % half)])
                ytrans(0)
                yield
                def late_dve(g):
                    hs = slice(4 * g, 4 * g + 4)
                    cs = slice(g * 256, (g + 1) * 256)
                    r_c = res('co1%d' % g)
                    V(lambda e: e.tensor_scalar(out=cov[:, 1, hs], in0=us[:, hs, 64], scalar1=1e-30, scalar2=None, op0=ALU.max),
                      [res('usbs%d' % g)], [r_c])
                    V(lambda e: e.reciprocal(out=cov[:, 1, hs], in_=cov[:, 1, hs]), [r_c], [r_c])
                    V(lambda e: e.tensor_tensor(out=cov[:, 1, hs], in0=cov[:, 1, hs], in1=glv[:, 1, hs], op=ALU.mult),
                      [r_c, r_gl], [r_c])
                    V(lambda e: e.tensor_tensor(out=sqv[:, hs, :], in0=us[:, hs, 0:64],
                                                in1=cov[:, 1, hs, None].to_broadcast([128, 4, 64]), op=ALU.mult),
                      [res('usbs%d' % g), r_c, res('sqb')], [res('sqb%d' % g)])
                    V(lambda e: e.tensor_tensor(out=ONS[:, cs], in0=ONS[:, cs], in1=SQB[:, cs], op=ALU.add),
                      [res('ons'), res('sqb%d' % g), res('sqb')], [res('ons%d' % g)])
                    V(lambda e: e.tensor_tensor(out=yb[:, 512 + g * 256:512 + (g + 1) * 256], in0=ONS[:, cs],
                                                in1=gg[:, 512 + g * 256:512 + (g + 1) * 256], op=ALU.mult),
                      [res('ons%d' % g), res('ons'), r_gate], [res('yb1%d%d' % (k, g))])

                def late_tr(g, bank=None):
                    by = nextpb() if bank is None else bank
                    pty = pbs[by][:, :].bitcast(BF16)
                    for m in range(2):
                        c = 4 + 2 * g + m
                        PE(lambda e, m=m, c=c: e.transpose(out=pty[:, m * 128:(m + 1) * 128], in_=yb[:, c * 128:(c + 1) * 128],
                                                           identity=IDB[:]),
                           [res('yb1%d%d' % (k, g)), res('idb')], [PR[by]], accum=(m > 0))
                    V(lambda e: e.tensor_copy(out=YT[:, 4 + 2 * g:6 + 2 * g, :].rearrange("p c t -> p (c t)"),
                                              in_=pty[:, 0:256]), [PR[by]], [res('yt1%d' % g)])

                slc_blocks = []
                ntr = min(4, i)
                for g in range(2):
                    qaug = qa[:, g].rearrange("p k q -> p (k q)")
                    ub = UBB
                    for j in range(i + 1):
                        js = slice(j * 128, (j + 1) * 128)
                        after = None
                        if j == i:
                            def after(P, pk, ub=ub, g=g):
                                V(lambda e: e.tensor_copy(out=USBS[:, g, :], in_=pbs[ub][:, 0:260]), [PR[ub]], [res('usbs%d' % g)])
                                late_dve(g)
                                if g == 1 and ntr == i:
                                    late_tr(0)
                        elif g == 1 and j == ntr:
                            def after(P, pk):
                                late_tr(0)
                        slc_blocks.append((KSA[:, g, js], [r_ks[j], res('ksa_e')], qaug, [r_qaq, r_qas],
                                           MB[:, 0, :] if j == i else None, None, [],
                                           VS[:, j, g, :], [r_vs[j]], ub, j == 0, j == i, after))
                yield from run_blocks(slc_blocks, ABB)
                bos = [nextpb(), nextpb()]

                def oproj(nh, c):
                    bo = bos[nh]
                    PE(lambda e: e.matmul(pbs[bo][:, :], lhsT=YT[:, c, :], rhs=WOUT[:, c, nh * 512:(nh + 1) * 512],
                                          start=(c == 0), stop=(c == 7)),
                       [res('yt0') if c < 4 else res('yt1%d' % ((c - 4) // 2)), res('wout')], [PR[bo]], accum=(c > 0))
                for nh in range(2):
                    for c in range(6):
                        oproj(nh, c)
                late_tr(1, bank=ABB[0])
                for nh in range(2):
                    for c in (6, 7):
                        oproj(nh, c)
                    V(lambda e: e.tensor_tensor(out=XS[k][:, nh * 512:(nh + 1) * 512], in0=pbs[bos[nh]][:, :],
                                                in1=XS[k][:, nh * 512:(nh + 1) * 512], op=ALU.add),
                      [PR[bos[nh]], r_xs[k]], [r_xs[k]])
                dma(out[ts, :], XS[k][:, :], reads=[r_xs[k]])
                yield

            def merged3(gA, gB, gK, scale):
                gens = {'A': gA, 'B': gB, 'K': gK}
                alive = {n: (g is not None) for n, g in gens.items()}
                ready = {'A': 0.0, 'B': 0.0}
                now = 0.0
                while alive['A'] or alive['B'] or alive['K']:
                    cand = [n for n in ('A', 'B') if alive[n] and ready[n] <= now]
                    if cand:
                        n = min(cand, key=lambda z: ready[z])
                        try:
                            c = next(gens[n])
                            c = 0.5 if c is None else c
                            ready[n] = now + c * scale
                            now += 0.3
                        except StopIteration:
                            alive[n] = False
                    elif alive['K']:
                        try:
                            next(gens['K'])
                            now += 0.75
                        except StopIteration:
                            alive['K'] = False
                    else:
                        pend = [ready[n] for n in ('A', 'B') if alive[n]]
                        now = min(pend)

            FRONT_COST = 30.0
            load_xs(0)
            load_tabs(0)
            load_tabs(1)
            load_xt(0)
            cast_xb(0)
            load_xt(1)
            merged3(stageA(0), None, None, 1.0)
            load_xt(2)
            merged3(stageA(1), stageB(0), None, 1.0)
            for i in range(NT):
                if i + 1 < NT:
                    load_xs(i + 1)
                if i + 2 < NT:
                    load_tabs(i + 2)
                if i + 3 < NT:
                    load_xt(i + 3)
                back_time = 0.75 * 2 * (i + 1)
                scale = max(1.0, back_time / FRONT_COST)
                merged3(stageA(i + 2) if i + 2 < NT else None, stageB(i + 1) if i + 1 < NT else None, back(i), scale)

        try:
            record()
        except _Stop:
            pass
        fin = S.op('sync', None)
        fin.deps = list(S.dma_hist)
        S.emit(st)
    return nc


_CACHE = {}


def _prep_shared(inp):
    f = np.float32
    w_in = np.asarray(inp['w_in'][0], f)
    sp = np.cumsum([0, 512, 512, 512, 512, 512, 512, 128, 128, 128, 128, 128, 128, 24])
    seg = {n: (sp[i], sp[i + 1]) for i, n in enumerate(['rq', 'rk', 'rv', 'rg', 'nq', 'ng', 'ck', 'cv', 'sk', 'sv', 'wk', 'wv', 'gl'])}

    def cols(n, a=None, b=None):
        s0, s1 = seg[n]
        idx = np.arange(s0, s1)
        return idx if a is None else idx[a:b]
    c1 = np.concatenate([cols('ck', 0, 64), cols('cv', 0, 64), cols('ck', 64, 128), cols('cv', 64, 128),
                         cols('sk'), cols('wk'), cols('sv'), cols('wv')])
    nq_pairs = np.concatenate([np.concatenate([cols('nq', kk * 64, kk * 64 + 64), cols('nq', (4 + kk) * 64, (4 + kk) * 64 + 64)])
                               for kk in range(4)])
    c2 = np.concatenate([cols('rq'), cols('rk'), cols('rv'), cols('rg'), nq_pairs, cols('ng'), cols('gl')])
    sh = {}
    sh['w1'] = np.ascontiguousarray(w_in[:, c1])
    sh['w2'] = np.ascontiguousarray(w_in[:, c2])
    sh['wout'] = np.ascontiguousarray(np.asarray(inp['w_out'][0], f))
    sh['normw'] = np.ascontiguousarray(np.asarray(inp['norm_w'][0], f).reshape(8, 128).T)
    sh['retw'] = np.ascontiguousarray(np.broadcast_to(np.asarray(inp['ret_norm_w'][0], f).reshape(1, 512), (128, 512)))
    sh['qw'] = np.ascontiguousarray(np.broadcast_to(np.tile(np.asarray(inp['q_norm_w'][0], f), 8)[None, :], (128, 512)))
    for nm, key in (('kcw', 'k_norm_cmp'), ('ksw', 'k_norm_slc'), ('kww', 'k_norm_win')):
        sh[nm] = np.ascontiguousarray(np.broadcast_to(np.tile(np.asarray(inp[key][0], f), 2)[None, :], (128, 128)))
    sh['bg'] = np.ascontiguousarray(np.broadcast_to(np.asarray(inp['b_gate'][0], f)[None, :], (128, 24)))
    sh['pos'] = np.ascontiguousarray(np.concatenate([np.asarray(inp['cmp_pos_k'][0], f).T, np.asarray(inp['cmp_pos_v'][0], f).T], axis=0))
    sh['cw1k'] = np.ascontiguousarray(np.asarray(inp['cmp_w1_k'][0], f))
    sh['cw1v'] = np.ascontiguousarray(np.asarray(inp['cmp_w1_v'][0], f))
    w2k = np.asarray(inp['cmp_w2_k'][0], f).reshape(2, 128, 64).transpose(1, 0, 2)
    w2v = np.asarray(inp['cmp_w2_v'][0], f).reshape(2, 128, 64).transpose(1, 0, 2)
    sh['cw2'] = np.ascontiguousarray(np.stack([w2k, w2v], axis=1).reshape(128, 256))
    return sh


def kernel(**inp):
    if 'nc' not in _CACHE:
        _CACHE['nc'] = build_nc()
        _CACHE['consts'] = _consts()
    nc = _CACHE['nc']
    sh = _prep_shared(inp)
    sh.update(_CACHE['consts'])
    x = np.asarray(inp['x'], np.float32)
    in_maps = []
    for b in range(8):
        m = dict(sh)
        m['x'] = np.ascontiguousarray(x[b])
        m['xT'] = np.ascontiguousarray(x[b].T)
        in_maps.append(m)
    res = run_bass_kernel_spmd(nc, in_maps, core_ids=list(range(8)))
    return np.stack([np.asarray(r['out'], np.float32) for r in res.results], axis=0)
```

```python
import contextlib
import numpy as np
import concourse.bass as bass
import concourse.mybir as mybir
from concourse.bass_utils import run_bass_kernel_spmd

F32 = mybir.dt.float32
BF16 = mybir.dt.bfloat16
ALU = mybir.AluOpType
AF = mybir.ActivationFunctionType
AX = mybir.AxisListType

T = 4096
D = 1024
NT = T // 128
NCOL1 = 768
NCOL2 = 3096
EPS = 1e-6
NEGB = -30000.0
DEBUG = False
SEQUENTIAL = False


class Res:
    __slots__ = ('name', 'w', 'r')

    def __init__(s, name=''):
        s.name = name
        s.w = None
        s.r = []


class Ins:
    __slots__ = ('eng', 'fn', 'deps', 'inc', 'val', 'sem', 'isdma')

    def __init__(s, eng, fn, isdma):
        s.eng = eng
        s.fn = fn
        s.deps = []
        s.inc = isdma
        s.val = None
        s.sem = None
        s.isdma = isdma


class _Rec:
    def __getattr__(s, name):
        def f(*a, **kw):
            s.call = (name, a, kw)
            return s
        return f


class Sched:
    COMPUTE = ('tensor', 'vector', 'scalar', 'gpsimd')

    def __init__(s, nc, ndma_sems=12):
        s.nc = nc
        s.streams = {e: [] for e in ('tensor', 'vector', 'scalar', 'gpsimd', 'sync')}
        s.ndma = ndma_sems
        s.dma_hist = []
        s.excl = set()

    def op(s, eng, fn, reads=(), writes=(), dma=False, accum=False):
        if fn is not None:
            rec = _Rec()
            fn(rec)
            fn = rec.call
        ins = Ins(eng, fn, dma)
        deps = {}

        def add(d):
            if d is None or d is ins:
                return
            deps[id(d)] = d
        if s.excl:
            extra = [r for r in reads if id(r) in s.excl and not any(r is w for w in writes)]
            if extra:
                writes = list(writes) + extra
        for r in reads:
            add(r.w)
        for w in writes:
            if w.w is not None and (dma or w.w.isdma or w.w.eng != eng):
                add(w.w)
            for rr in w.r:
                if dma or rr.isdma or rr.eng != eng:
                    add(rr)
        if dma:
            k = len(s.dma_hist)
            if k >= s.ndma:
                add(s.dma_hist[k - s.ndma])
            s.dma_hist.append(ins)
        for d in deps.values():
            d.inc = True
        ins.deps = list(deps.values())
        for r in reads:
            r.r.append(ins)
        for w in writes:
            w.w = ins
            if not accum:
                w.r = []
        s.streams[eng].append(ins)
        return ins

    def barrier(s):
        marks = []
        for e in s.COMPUTE:
            if s.streams[e]:
                marks.append(s.streams[e][-1])
        marks += s.dma_hist[-s.ndma:]
        for m in marks:
            m.inc = True
        for e in list(s.streams):
            ins = s.op(e, None)
            ins.deps = list(marks)

    def emit(s, stack):
        nc = s.nc
        sems = {}
        for e in s.COMPUTE:
            sems[e] = stack.enter_context(nc.semaphore('sem_' + e))
        dsems = [stack.enter_context(nc.semaphore('dsem%d' % i)) for i in range(s.ndma)]
        for e in s.COMPUTE:
            c = 0
            for ins in s.streams[e]:
                if ins.inc and ins.fn is not None:
                    c += 1
                ins.val = c
                ins.sem = sems[e]
        for k, ins in enumerate(s.dma_hist):
            ins.sem = dsems[k % s.ndma]
            ins.val = 16 * (k // s.ndma + 1)
        block = stack.enter_context(nc.Block())

        def make(ename):
            stream = s.streams[ename]

            def body(eng):
                seen = {}
                for ins in stream:
                    need = {}
                    for d in ins.deps:
                        if d.fn is None or not d.val:
                            continue
                        key = id(d.sem)
                        if seen.get(key, 0) >= d.val:
                            continue
                        if key not in need or need[key][1] < d.val:
                            need[key] = (d.sem, d.val)
                    for key, (sm, v) in need.items():
                        eng.wait_ge(sm, v)
                        seen[key] = v
                    if ins.fn is None:
                        continue
                    nm_, a_, kw_ = ins.fn
                    r = getattr(eng, nm_)(*a_, **kw_)
                    if ins.inc:
                        r.then_inc(ins.sem, 16 if ins.isdma else 1)
            return body
        block.sync(make('sync'))
        block.tensor(make('tensor'))
        block.vector(make('vector'))
        block.scalar(make('scalar'))
        block.gpsimd(make('gpsimd'))


def _consts():
    c = {}
    c['ident'] = np.eye(128, dtype=np.float32)
    t = np.arange(T, dtype=np.float32)
    inv = (10000.0 ** (-np.arange(32, dtype=np.float32) / 32)).astype(np.float32)
    ang = t[:, None] * inv[None, :]
    tab = np.zeros((NT, 128, 128), np.float32)
    tab[:, :, 0:32] = np.cos(ang).reshape(NT, 128, 32)
    tab[:, :, 32:64] = np.sin(ang).reshape(NT, 128, 32)
    tt = np.arange(T)
    blk = np.arange(64)
    valid = blk[None, :] * 64 <= tt[:, None]
    force = (blk[None, :] == (tt // 64)[:, None]) | (blk[None, :] == 0)
    sb = np.where(valid, np.where(force, 1e4, 0.0), -1e30).astype(np.float32)
    tab[:, :, 64:128] = sb.reshape(NT, 128, 64)
    c['tab'] = tab
    h = np.arange(8, dtype=np.float64)
    logg = np.log1p(-np.exp2(-5.0 - h))
    pos = np.arange(128, dtype=np.float64)
    gq = np.exp(logg[None, :] * (pos[:, None] + 1.0))
    gk = np.exp(-logg[None, :] * (pos[:, None] + 1.0)) * 64 ** -0.5
    c['gqk'] = np.concatenate([gq, gk], axis=1).astype(np.float32)
    gC = np.exp(logg * 128.0)
    gct = np.zeros((128, 4, 64), np.float32)
    for a in range(2):
        for m in range(4):
            gct[a * 64:(a + 1) * 64, m, :] = gC[2 * m + a]
    c['gct'] = gct.reshape(128, 256)
    s = np.arange(128)
    caus = (s[:, None] <= s[None, :]).astype(np.float32)
    anti = (s[:, None] > s[None, :]).astype(np.float32)
    c['masks'] = np.concatenate([caus, anti], axis=1)
    n = np.arange(256)
    cm = np.zeros((NT, 128, 2, 128), np.float32)
    for i in range(NT):
        q = i * 128 + np.arange(128)
        v = (16 * n[:, None] + 31 <= q[None, :]) & (n[:, None] < 255)
        cm[i] = v.reshape(2, 128, 128).transpose(1, 0, 2)
    c['cmask'] = cm.reshape(NT, 128, 256)
    E = (np.arange(T)[None, :] // 64 == np.arange(64)[:, None]).astype(np.float32)
    c['ee'] = np.concatenate([E, E], axis=0)
    posn = np.arange(255)[:, None] * 16 + np.arange(32)[None, :]
    b = posn // 64
    M = (b[:, :, None] == np.arange(64)[None, None, :]).mean(axis=1).astype(np.float32)
    Mp = np.zeros((256, 64), np.float32)
    Mp[:255] = M
    c['mov'] = Mp.reshape(2, 128, 64).transpose(1, 0, 2).reshape(128, 128).copy()
    return c


CONST_SHAPES = {
    'ident': [128, 128], 'tab': [NT, 128, 128], 'gqk': [128, 16], 'gct': [128, 256],
    'masks': [128, 256], 'cmask': [NT, 128, 256], 'ee': [128, T], 'mov': [128, 128],
}
IN_SHAPES = {
    'x': [T, D], 'xT': [D, T], 'w1': [D, NCOL1], 'w2': [D, NCOL2], 'wout': [D, D],
    'normw': [128, 8], 'retw': [128, 512], 'qw': [128, 512], 'kcw': [128, 128],
    'ksw': [128, 128], 'kww': [128, 128], 'bg': [128, 24], 'pos': [128, 32],
    'cw1k': [2048, 256], 'cw1v': [2048, 256], 'cw2': [128, 256],
}


class _Stop(Exception):
    pass


def build_nc(stop=None):
    nc = bass.Bass("TRN2", target_bir_lowering=False)
    A = {}
    for k, shp in list(IN_SHAPES.items()) + list(CONST_SHAPES.items()):
        A[k] = nc.dram_tensor(k, shp, F32, kind="ExternalInput").ap()
    out = nc.dram_tensor("out", [T, D], F32, kind="ExternalOutput").ap()
    dbg = {}
    with contextlib.ExitStack() as st:
        S = Sched(nc)

        def ck(name):
            if stop == name:
                raise _Stop()

        def sb(name, shape, dt):
            return st.enter_context(nc.sbuf_tensor(name, shape, dt))

        WIN2 = sb("win2", [128, 8, NCOL2], BF16)
        WOUT = sb("wout_sb", [128, 8, D], BF16)
        KSA = sb("ksa", [128, 2, T], BF16)
        KWT = sb("kwt", [128, T], BF16)
        VS = sb("vs", [128, NT, 2, 65], BF16)
        VW = sb("vw", [128, NT, 2, 65], BF16)
        VCM = sb("vcm", [128, 2, 2, 65], BF16)
        MOV = sb("movb", [128, 2, 64], BF16)
        KCT = sb("kct", [128, 256], BF16)
        XS = [sb("xs%d" % i, [128, D], F32) for i in range(2)]
        XT = [sb("xt%d" % i, [128, 8, 128], F32) for i in range(2)]
        XB = [sb("xb%d" % i, [128, 8, 128], BF16) for i in range(2)]
        RSTD = sb("rstd", [128, NT], F32)
        NRSTD = sb("nrstd", [128, NT], F32)
        IDF = sb("idf", [128, 128], F32)
        IDB = sb("idb", [128, 128], BF16)
        NORMW = sb("normw_sb", [128, 8], F32)
        RETW = sb("retw_sb", [128, 512], F32)
        QW = sb("qw_sb", [128, 512], F32)
        KCW = sb("kcw_sb", [128, 128], F32)
        KSW = sb("ksw_sb", [128, 128], F32)
        KWW = sb("kww_sb", [128, 128], F32)
        BG = sb("bg_sb", [128, 24], F32)
        GQK = sb("gqk_sb", [128, 16], F32)
        GCT = sb("gct_sb", [128, 256], F32)
        MASKS = sb("masks_sb", [128, 256], F32)
        TAB = [sb("tab%d" % i, [128, 128], F32) for i in range(2)]
        CMK = [sb("cmk%d" % i, [128, 256], F32) for i in range(2)]
        RST = sb("rstate", [128, 4, 64], F32)
        RSB = sb("rstate_b", [128, 4, 64], BF16)
        MB = sb("maskbias", [128, 2, 512], BF16)
        ARENA = sb("arena", [128, 33792], BF16)
        pbs = [st.enter_context(nc.psum_tensor("pb%d" % i, [128, 512], F32)) for i in range(8)]
        PR = [Res("pb%d" % i) for i in range(8)]
        S.excl = set(id(r) for r in PR)

        class Carver:
            def __init__(s):
                s.off = 0

            def get(s, n_el, dt):
                nb = n_el * (2 if dt == BF16 else 4)
                nb16 = nb // 2
                a = ARENA[:, s.off:s.off + nb16]
                s.off += nb16
                assert s.off <= 33792, s.off
                if dt == F32:
                    a = a.bitcast(F32)
                return a

        R = {}

        def res(n):
            if n not in R:
                R[n] = Res(n)
            return R[n]
        r_ks = [Res("ks%d" % j) for j in range(NT)]
        r_kw = [Res("kw%d" % j) for j in range(NT)]
        r_vs = [Res("vs%d" % j) for j in range(NT)]
        r_vw = [Res("vw%d" % j) for j in range(NT)]
        r_ct = [Res("ct%d" % j) for j in range(NT)]
        r_xs = [Res("xs0"), Res("xs1")]
        r_xt = [Res("xt0"), Res("xt1")]
        r_xb = [Res("xb0"), Res("xb1")]
        r_tab = [Res("tab0"), Res("tab1")]
        r_cmk = [Res("cmk0"), Res("cmk1")]

        def dma(o, i, reads=(), writes=()):
            return S.op('sync', lambda e: e.dma_start(out=o, in_=i), reads=reads, writes=writes, dma=True)

        def V(fn, reads=(), writes=()):
            return S.op('vector', fn, reads=reads, writes=writes)

        def G(fn, reads=(), writes=()):
            return S.op('gpsimd', fn, reads=reads, writes=writes)

        def ACT(fn, reads=(), writes=()):
            return S.op('scalar', fn, reads=reads, writes=writes)

        def PE(fn, reads=(), writes=(), accum=False):
            return S.op('tensor', fn, reads=reads, writes=writes, accum=accum)

        def ldc(name, tile, r):
            dma(tile[:], A[name], writes=[r])

        def record():
            for nm, tl in (('ident', IDF), ('normw', NORMW), ('retw', RETW), ('qw', QW), ('kcw', KCW),
                           ('ksw', KSW), ('kww', KWW), ('bg', BG), ('gqk', GQK), ('gct', GCT), ('masks', MASKS)):
                ldc(nm, tl, res(nm))
            V(lambda e: e.tensor_copy(out=IDB[:], in_=IDF[:]), [res('ident')], [res('idb')])
            V(lambda e: e.tensor_scalar(out=MB[:].rearrange("p m (h q) -> p m h q", h=4),
                                        in0=MASKS[:].rearrange("p (m q) -> p m q", m=2)[:, :, None, :].to_broadcast([128, 2, 4, 128]),
                                        scalar1=-1.0, scalar2=-NEGB, op0=ALU.add, op1=ALU.mult), [res('masks')], [res('mb')])
            G(lambda e: e.memset(VS[:].rearrange("p a b c -> p (a b c)"), 1.0), [], r_vs)
            G(lambda e: e.memset(VW[:].rearrange("p a b c -> p (a b c)"), 1.0), [], r_vw)
            G(lambda e: e.memset(VCM[:].rearrange("p a b c -> p (a b c)"), 1.0), [], [res('vcm')])
            G(lambda e: e.memset(RST[:].rearrange("p a b -> p (a b)"), 0.0), [], [res('rst')])
            G(lambda e: e.memset(RSB[:].rearrange("p a b -> p (a b)"), 0.0), [], [res('rsb')])

            cnt = [0]

            SLOTS = [XS[0][:, :], XS[1][:, :], XT[0][:].rearrange("p c t -> p (c t)"), XT[1][:].rearrange("p c t -> p (c t)")]
            r_sl = [r_xs[0], r_xs[1], r_xt[0], r_xt[1]]

            def stage_cast(dst, srcs, scale=None, rd=(), wr=(), nslots=4):
                k = cnt[0] % nslots
                cnt[0] += 1
                SL = SLOTS[k]
                for (p0, p1, c0, c1, src) in srcs:
                    dma(SL[p0:p1, c0:c1], src, writes=[r_sl[k]])
                p0 = min(s_[0] for s_ in srcs)
                p1 = max(s_[1] for s_ in srcs)
                c1 = max(s_[3] for s_ in srcs)
                if cnt[0] % 2 == 0:
                    if scale is None:
                        V(lambda e: e.tensor_copy(out=dst, in_=SL[p0:p1, 0:c1]), [r_sl[k]] + list(rd), list(wr))
                    else:
                        V(lambda e: e.tensor_scalar(out=dst, in0=SL[p0:p1, 0:c1], scalar1=scale, scalar2=None,
                                                    op0=ALU.mult), [r_sl[k]] + list(rd), list(wr))
                else:
                    if scale is None:
                        ACT(lambda e: e.activation(out=dst, in_=SL[p0:p1, 0:c1], func=AF.Copy), [r_sl[k]] + list(rd), list(wr))
                    else:
                        ACT(lambda e: e.activation(out=dst, in_=SL[p0:p1, 0:c1], func=AF.Copy, scale=scale),
                            [r_sl[k]] + list(rd), list(wr))

            ca = Carver()
            CTS = ca.get(2 * T, BF16).rearrange("p (g r m) -> p g r m", g=2, r=16)
            W1 = ca.get(32 * 256, BF16).rearrange("p (l h) -> p l h", l=32)
            HTF = ca.get(2 * 2 * 2 * 256, BF16)
            HT = HTF.rearrange("p (k g c n) -> p k g c n", k=2, g=2, c=2)
            WIN1 = ca.get(8 * NCOL1, BF16).rearrange("p (c n) -> p c n", c=8)
            W2C = ca.get(256, BF16).rearrange("p (k c d) -> p k c d", k=2, c=2)
            POSB = ca.get(32, BF16)
            KV1 = ca.get(NCOL1, F32)
            KVB = ca.get(512, BF16)
            SQ1 = ca.get(256, F32)
            ST1 = ca.get(8, F32)
            ZT = ca.get(256, F32)
            ET = ca.get(256, F32)
            CBIAS = ca.get(4, F32)
            KCTM = ca.get(128, BF16)
            SQX = ca.get(1024, BF16)
            ONESB = ca.get(2, BF16)

            ck('consts')
            for c in range(8):
                rows = slice(c * 128, (c + 1) * 128)
                stage_cast(WIN1[:, c, :], [(0, 128, 0, NCOL1, A['w1'][rows, :])], scale=NORMW[:, c:c + 1],
                           rd=[res('normw')], wr=[res('win1')])
            W2STEPS = []
            for c in range(8):
                rows = slice(c * 128, (c + 1) * 128)
                for p in range(4):
                    c0 = p * 1024
                    c1 = min(NCOL2, c0 + 1024)
                    W2STEPS.append((WIN2[:, c, c0:c1], [(0, 128, 0, c1 - c0, A['w2'][rows, c0:c1])], NORMW[:, c:c + 1],
                                    [res('normw')], [res('win2')]))
            for c in range(8):
                rows = slice(c * 128, (c + 1) * 128)
                W2STEPS.append((WOUT[:, c, :], [(0, 128, 0, D, A['wout'][rows, :])], None, [], [res('wout')]))
            w1k = A['cw1k'].rearrange("(l d) h -> d l h", d=64)
            w1v = A['cw1v'].rearrange("(l d) h -> d l h", d=64)
            for p in range(8):
                k = cnt[0] % 2
                cnt[0] += 1
                xsv = XS[k][:, :].rearrange("p (l h) -> p l h", l=4)
                dma(xsv[0:64], w1k[:, p * 4:(p + 1) * 4, :], writes=[r_xs[k]])
                dma(xsv[64:128], w1v[:, p * 4:(p + 1) * 4, :], writes=[r_xs[k]])
                V(lambda e, k=k, p=p, xsv=xsv: e.tensor_copy(out=W1[:, p * 4:(p + 1) * 4, :], in_=xsv),
                  [r_xs[k]], [res('w1')])
            for p in range(4):
                k = cnt[0] % 2
                cnt[0] += 1
                cs = slice(p * 1024, (p + 1) * 1024)
                dma(XS[k][:, :], A['ee'][:, cs], writes=[r_xs[k]])
                V(lambda e, k=k, cs=cs: e.tensor_copy(out=KSA[64:128, 0, cs], in_=XS[k][64:128, :]), [r_xs[k]], [res('ksa_e')])
                ACT(lambda e, k=k, cs=cs: e.activation(out=KSA[0:64, 1, cs], in_=XS[k][0:64, :], func=AF.Copy), [r_xs[k]], [res('ksa_e')])
            k = cnt[0] % 2
            cnt[0] += 1
            dma(XS[k][:, 0:256], A['cw2'], writes=[r_xs[k]])
            dma(XS[k][:, 256:288], A['pos'], writes=[r_xs[k]])
            dma(XS[k][:, 288:416], A['mov'], writes=[r_xs[k]])
            V(lambda e, k=k: e.tensor_copy(out=W2C.rearrange("p k c d -> p (k c d)"), in_=XS[k][:, 0:256]), [r_xs[k]], [res('w2c')])
            V(lambda e, k=k: e.tensor_copy(out=POSB, in_=XS[k][:, 256:288]), [r_xs[k]], [res('posb')])
            V(lambda e, k=k: e.tensor_copy(out=MOV[:].rearrange("p a b -> p (a b)"), in_=XS[k][:, 288:416]), [r_xs[k]], [res('mov')])

            ck('weights')
            xTd = A['xT'].rearrange("(c p) t -> p c t", p=128)
            pbk = [0]
            pbset = [0, 1, 2]

            def nextpb():
                k = pbset[pbk[0] % len(pbset)]
                pbk[0] += 1
                return k

            def rsqrt_act(dst, src, scale, rd, wr):
                ACT(lambda e: e.activation(out=dst, in_=src, func=AF.Ln, scale=scale, bias=EPS), rd, wr)
                ACT(lambda e: e.activation(out=dst, in_=dst, func=AF.Exp, scale=-0.5), wr, wr)

            def loads1(i):
                k = i % 2
                ts = slice(i * 128, (i + 1) * 128)
                dma(XT[k][:], xTd[:, :, ts], writes=[r_xt[k]])

            def cast1(i):
                k = i % 2
                V(lambda e: e.tensor_copy(out=XB[k][:], in_=XT[k][:]), [r_xt[k]], [r_xb[k]])
            V(lambda e: e.memset(ONESB, 1.0), [], [res('onesb')])
            loads1(0)
            cast1(0)
            for i in range(NT):
                k = i % 2
                ts = slice(i * 128, (i + 1) * 128)
                if i + 1 < NT:
                    loads1(i + 1)
                ck('p1a')
                ACT(lambda e, k=k: e.activation(out=SQX, in_=XT[k][:].rearrange("p c t -> p (c t)"), func=AF.Square),
                    [r_xt[k]], [res('sqx')])
                bq = nextpb()
                for c in range(8):
                    PE(lambda e, c=c, bq=bq: e.matmul(pbs[bq][:, 0:1], lhsT=SQX[:, c * 128:(c + 1) * 128], rhs=ONESB[:, 0:1],
                                                      start=(c == 0), stop=(c == 7)),
                       [res('sqx'), res('onesb')], [PR[bq]], accum=(c > 0))
                rsqrt_act(RSTD[:, i:i + 1], pbs[bq][:, 0:1], 1.0 / D, [PR[bq]], [res('rstd%d' % i)])
                V(lambda e, i=i: e.tensor_scalar(out=NRSTD[:, i:i + 1], in0=RSTD[:, i:i + 1], scalar1=-1.0, scalar2=None, op0=ALU.mult),
                  [res('rstd%d' % i)], [res('nrstd%d' % i)])
                ck('p1b')
                b0 = nextpb()
                b1 = nextpb()
                for c in range(8):
                    PE(lambda e, c=c, k=k, b0=b0: e.matmul(pbs[b0][:, 0:384], lhsT=XB[k][:, c, :], rhs=WIN1[:, c, 0:384],
                                                          start=(c == 0), stop=(c == 7)),
                       [r_xb[k], res('win1')], [PR[b0]], accum=(c > 0))
                for c in range(8):
                    PE(lambda e, c=c, k=k, b1=b1: e.matmul(pbs[b1][:, 0:384], lhsT=XB[k][:, c, :], rhs=WIN1[:, c, 384:768],
                                                          start=(c == 0), stop=(c == 7)),
                       [r_xb[k], res('win1')], [PR[b1]], accum=(c > 0))
                rs = RSTD[:, i:i + 1]
                V(lambda e, b0=b0, rs=rs: e.tensor_scalar(out=KV1[:, 0:384], in0=pbs[b0][:, 0:384], scalar1=rs, scalar2=None,
                                                          op0=ALU.mult), [PR[b0], res('rstd%d' % i)], [res('kv1a')])
                V(lambda e, b1=b1, rs=rs: e.tensor_scalar(out=KV1[:, 384:768], in0=pbs[b1][:, 0:384], scalar1=rs, scalar2=None,
                                                          op0=ALU.mult), [PR[b1], res('rstd%d' % i)], [res('kv1b')])
                if i + 1 < NT:
                    cast1(i + 1)
                ck('p1c')
                ACT(lambda e: e.activation(out=KVB[:, 0:256], in_=KV1[:, 0:256], func=AF.Copy), [res('kv1a')], [res('kvb_c')])
                ACT(lambda e: e.activation(out=SQ1, in_=KV1[:, 256:512], func=AF.Square), [res('kv1a'), res('kv1b')], [res('sq1')])
                V(lambda e: e.tensor_reduce(out=ST1[:, 0:4], in_=SQ1.rearrange("p (a b) -> p a b", b=64), axis=AX.X, op=ALU.add),
                  [res('sq1')], [res('st1')])
                rsqrt_act(ST1[:, 0:4], ST1[:, 0:4], 1.0 / 64, [res('st1')], [res('st1')])
                V(lambda e: e.tensor_tensor(out=SQ1.rearrange("p (a b) -> p a b", b=64),
                                            in0=KV1[:, 256:512].rearrange("p (a b) -> p a b", b=64),
                                            in1=ST1[:, 0:4, None].to_broadcast([128, 4, 64]), op=ALU.mult),
                  [res('kv1a'), res('kv1b'), res('st1')], [res('sq1')])
                V(lambda e: e.tensor_tensor(out=KVB[:, 256:384], in0=SQ1[:, 0:128], in1=KSW[:], op=ALU.mult),
                  [res('sq1'), res('ksw')], [res('kvb_s')])
                V(lambda e: e.tensor_tensor(out=KVB[:, 384:512], in0=SQ1[:, 128:256], in1=KWW[:], op=ALU.mult),
                  [res('sq1'), res('kww')], [res('kvb_w')])
                ACT(lambda e, i=i: e.activation(out=VS[:, i, :, 0:64], in_=KV1[:, 512:640].rearrange("p (g d) -> p g d", g=2), func=AF.Copy),
                  [res('kv1b')], [r_vs[i]])
                ACT(lambda e, i=i: e.activation(out=VW[:, i, :, 0:64], in_=KV1[:, 640:768].rearrange("p (g d) -> p g d", g=2), func=AF.Copy),
                  [res('kv1b')], [r_vw[i]])
                ck('p1d')
                bt = nextpb()
                ptb = pbs[bt][:, :].bitcast(BF16)
                for m, rr in enumerate(('kvb_c', 'kvb_c', 'kvb_s', 'kvb_w')):
                    PE(lambda e, m=m, ptb=ptb: e.transpose(out=ptb[:, m * 128:(m + 1) * 128], in_=KVB[:, m * 128:(m + 1) * 128],
                                                           identity=IDB[:]),
                       [res(rr), res('idb')], [PR[bt]], accum=(m > 0))
                ck('p1e')
                V(lambda e, ptb=ptb, i=i: e.tensor_copy(out=CTS[:, :, :, i * 8:(i + 1) * 8],
                                                        in_=ptb[:, 0:256].rearrange("p (g m r) -> p g r m", g=2, r=16)),
                  [PR[bt]], [r_ct[i]])
                ck('p1f')
                V(lambda e, ptb=ptb, ts=ts: e.tensor_copy(out=KSA[0:64, 0, ts], in_=ptb[0:64, 256:384]),
                  [PR[bt]], [r_ks[i]])
                V(lambda e, ptb=ptb, ts=ts: e.tensor_copy(out=KSA[64:128, 1, ts], in_=ptb[64:128, 256:384]),
                  [PR[bt]], [r_ks[i]])
                ck('p1g')
                V(lambda e, ptb=ptb, ts=ts: e.tensor_copy(out=KWT[:, ts], in_=ptb[:, 384:512]), [PR[bt]], [r_kw[i]])
                for st_ in W2STEPS[i * len(W2STEPS) // NT:(i + 1) * len(W2STEPS) // NT]:
                    stage_cast(st_[0], st_[1], scale=st_[2], rd=st_[3], wr=st_[4], nslots=2)
                ck('p1t%d' % i)

            ck('pass1')
            for kind in range(2):
                bb = nextpb()
                rows = slice(kind * 64, (kind + 1) * 64)
                for hc in range(2):
                    col = kind * 2 + hc
                    for l in range(32):
                        PE(lambda e, rows=rows, hc=hc, l=l, col=col, bb=bb: e.matmul(
                            pbs[bb][:, col:col + 1], lhsT=W1[rows, l, hc * 128:(hc + 1) * 128], rhs=POSB[rows, l:l + 1],
                            start=(l == 0), stop=(l == 31)),
                           [res('w1'), res('posb')], [PR[bb]], accum=not (hc == 0 and l == 0))
                V(lambda e, bb=bb, kind=kind: e.tensor_copy(out=CBIAS[:, kind * 2:kind * 2 + 2], in_=pbs[bb][:, kind * 2:kind * 2 + 2]),
                  [PR[bb]], [res('cbias')])
            ck('c1')
            ck('c1b')
            G(lambda e: e.memset(HTF, 0.0), [], [res('ht')])
            for kind in range(2):
                rows = slice(kind * 64, (kind + 1) * 64)
                for g in range(2):
                    for hc in range(2):
                        b = nextpb()
                        for l in range(32):
                            PE(lambda e, rows=rows, g=g, hc=hc, l=l, b=b: e.matmul(
                                pbs[b][:, 0:255], lhsT=W1[rows, l, hc * 128:(hc + 1) * 128],
                                rhs=CTS[rows, g, l % 16, (l // 16):(l // 16) + 255], start=(l == 0), stop=(l == 31)),
                               [res('w1')] + r_ct, [PR[b]], accum=(l > 0))
                        ck('c2')
                        col = kind * 2 + hc
                        ACT(lambda e, b=b, col=col, kind=kind, g=g, hc=hc: e.activation(
                            out=HT[:, kind, g, hc, 0:255], in_=pbs[b][:, 0:255], func=AF.Silu, bias=CBIAS[:, col:col + 1]),
                            [PR[b], res('cbias')], [res('ht')])
            ck('c3')
            for nt_ in range(2):
                ns = slice(nt_ * 128, (nt_ + 1) * 128)
                b = nextpb()
                for kind in range(2):
                    for g in range(2):
                        for hc in range(2):
                            cs = slice(kind * 128 + g * 64, kind * 128 + g * 64 + 64)
                            PE(lambda e, kind=kind, g=g, hc=hc, cs=cs, ns=ns, b=b: e.matmul(
                                pbs[b][:, cs], lhsT=HT[:, kind, g, hc, ns], rhs=W2C[:, kind, hc, :],
                                start=(hc == 0), stop=(hc == 1)),
                               [res('ht'), res('w2c')], [PR[b]], accum=not (kind == 0 and g == 0 and hc == 0))
                ck('c4')
                V(lambda e, b=b, nt_=nt_: e.tensor_copy(out=VCM[:, nt_, :, 0:64],
                                                       in_=pbs[b][:, 128:256].rearrange("p (g d) -> p g d", g=2)),
                  [PR[b]], [res('vcm')])
                ck('c5')
                ACT(lambda e, b=b: e.activation(out=SQ1[:, 0:128], in_=pbs[b][:, 0:128], func=AF.Square), [PR[b]], [res('sq1')])
                ck('c6')
                V(lambda e: e.tensor_reduce(out=ST1[:, 0:2], in_=SQ1[:, 0:128].rearrange("p (a b) -> p a b", b=64), axis=AX.X,
                                            op=ALU.add), [res('sq1')], [res('st1')])
                rsqrt_act(ST1[:, 0:2], ST1[:, 0:2], 1.0 / 64, [res('st1')], [res('st1')])
                V(lambda e, b=b: e.tensor_tensor(out=SQ1[:, 0:128].rearrange("p (a b) -> p a b", b=64),
                                                 in0=pbs[b][:, 0:128].rearrange("p (a b) -> p a b", b=64),
                                                 in1=ST1[:, 0:2, None].to_broadcast([128, 2, 64]), op=ALU.mult),
                  [PR[b], res('st1')], [res('sq1')])
                V(lambda e: e.tensor_tensor(out=KCTM, in0=SQ1[:, 0:128], in1=KCW[:], op=ALU.mult),
                  [res('sq1'), res('kcw')], [res('kctm')])
                ck('c7')
                bt = nextpb()
                ptb = pbs[bt][:, :].bitcast(BF16)
                PE(lambda e, ptb=ptb: e.transpose(out=ptb[:, 0:128], in_=KCTM, identity=IDB[:]),
                   [res('kctm'), res('idb')], [PR[bt]])
                V(lambda e, ptb=ptb, ns=ns: e.tensor_copy(out=KCT[:, ns], in_=ptb[:, 0:128]), [PR[bt]], [res('kct')])

            ck('compress')
            S.barrier()

            pbset[:] = [0, 1]
            ABF = [2, 3]
            ABB = [4, 5]
            UBB = 6
            UF = 7
            cb = Carver()
            RQK = cb.get(512, F32)
            TMP1 = cb.get(256, F32)
            TMP2 = cb.get(256, F32)
            ROT = cb.get(512, F32)
            QKB = [cb.get(1024, BF16) for _ in range(2)]
            VTM = [cb.get(512, BF16) for _ in range(2)]
            QKT = [cb.get(1024, BF16) for _ in range(2)]
            STB = cb.get(1024, BF16)
            GG = [cb.get(1024, BF16) for _ in range(3)]
            NQ = cb.get(512, F32)
            GT = NQ
            SQ = cb.get(512, F32)
            SQB = cb.get(512, F32)
            QN = cb.get(512, BF16)
            QA = [cb.get(1024, BF16).rearrange("p (v k q) -> p v k q", v=2, k=4) for _ in range(3)]
            NP = 6
            PB_ = [cb.get(512, BF16) for _ in range(NP)]
            USBF = [cb.get(2 * 2 * 260, F32).rearrange("p (x g c) -> p x g c", x=2, g=2) for _ in range(2)]
            USBS = cb.get(2 * 260, F32).rearrange("p (g c) -> p g c", g=2)
            IMP = cb.get(128, F32)
            SCO = cb.get(128, F32)
            SC2 = cb.get(128, F32)
            M8 = cb.get(32, F32)
            SELB = cb.get(128, BF16)
            ONS = cb.get(512, F32)
            ORT = cb.get(512, F32)
            YB = [cb.get(1024, BF16) for _ in range(2)]
            YT = cb.get(1024, BF16).rearrange("p (c t) -> p c t", c=8)
            GL = [cb.get(24, F32) for _ in range(3)]
            CO = cb.get(24, F32)
            DEN = cb.get(24, F32)
            SS = cb.get(16, F32)
            r_p = [Res("p%d" % i) for i in range(NP)]
            pcount = [0]
            COLS = {'rq': 0, 'rk': 512, 'rv': 1024, 'rg': 1536, 'nq': 2048, 'ng': 2560, 'gl': 3072}
            SCALE = 0.125

            def proj(k, c0, n, b):
                for c in range(8):
                    PE(lambda e, c=c: e.matmul(pbs[b][:, 0:n], lhsT=XB[k][:, c, :], rhs=WIN2[:, c, c0:c0 + n],
                                               start=(c == 0), stop=(c == 7)),
                       [r_xb[k], res('win2')], [PR[b]], accum=(c > 0))

            def run_blocks(blocks, abset):
                nb = len(blocks)
                if nb == 0:
                    return
                abl = [None] * nb

                def qk(t):
                    ab = abset[t % 2]
                    bl = blocks[t]
                    PE(lambda e: e.matmul(pbs[ab][:, :], lhsT=bl[0], rhs=bl[2], start=True, stop=(bl[4] is None)),
                       list(bl[1]) + list(bl[3]), [PR[ab]])
                    if bl[4] is not None:
                        PE(lambda e: e.matmul(pbs[ab][:, :], lhsT=IDB[:], rhs=bl[4], start=False, stop=True),
                           [res('idb'), res('mb')], [PR[ab]], accum=True)
                    abl[t] = ab
                qk(0)
                for t in range(nb):
                    if t + 1 < nb:
                        qk(t + 1)
                    (lhsT_, lres_, rhs_, rres_, mask_pe, mask_ap, mask_res, vfn, v_res, ub, first, last, after) = blocks[t]
                    ab = abl[t]
                    pk = pcount[0] % NP
                    pcount[0] += 1
                    P = PB_[pk]
                    ACT(lambda e: e.activation(out=P, in_=pbs[ab][:, :], func=AF.Exp, scale=SCALE), [PR[ab]], [r_p[pk]])
                    if mask_ap is not None:
                        V(lambda e: e.tensor_tensor(out=P.rearrange("p (h q) -> p h q", h=4), in0=P.rearrange("p (h q) -> p h q", h=4),
                                                    in1=mask_ap[:, None, :].to_broadcast([128, 4, 128]), op=ALU.mult),
                          [r_p[pk]] + list(mask_res), [r_p[pk]])
                    for h in range(4):
                        PE(lambda e, h=h: e.matmul(pbs[ub][:, h * 65:(h + 1) * 65], lhsT=P[:, h * 128:(h + 1) * 128], rhs=vfn,
                                                   start=(first and h == 0), stop=(last and h == 3)),
                           [r_p[pk]] + list(v_res), [PR[ub]], accum=not (first and h == 0))
                    if after is not None:
                        after(P, pk)
                    yield 0.75

            def load_xs(i):
                k = i % 2
                dma(XS[k][:, :], A['x'][i * 128:(i + 1) * 128, :], writes=[r_xs[k]])

            def load_xt(i):
                k = i % 2
                dma(XT[k][:], xTd[:, :, i * 128:(i + 1) * 128], writes=[r_xt[k]])

            def load_tabs(i):
                k = i % 2
                dma(TAB[k][:], A['tab'][i], writes=[r_tab[k]])
                dma(CMK[k][:], A['cmask'][i], writes=[r_cmk[k]])

            def cast_xb(i):
                k = i % 2
                V(lambda e: e.tensor_copy(out=XB[k][:], in_=XT[k][:]), [r_xt[k]], [r_xb[k]])

            def stageA(i):
                k = i % 2
                k3 = i % 3
                rs = RSTD[:, i:i + 1]
                r_rs = res('rstd%d' % i)
                COS = TAB[k][:, 0:32]
                SIN = TAB[k][:, 32:64]
                gg = GG[k3]
                qa = QA[k3]
                gl = GL[k3]
                qkb = QKB[k]
                qkt = QKT[k]
                vtm = VTM[k]
                r_gate = res('gate%d' % k3)
                r_qaq = res('qa_q%d' % k3)
                r_gl = res('gl%d' % k3)
                r_qkt = res('qkt%d' % k)
                r_vtm = res('vtm%d' % k)
                nrs = NRSTD[:, i:i + 1]
                r_nrs = res('nrstd%d' % i)

                def rope(nm, off, gcol):
                    src = RQK.rearrange("p (h d) -> p h d", h=8)
                    x1 = src[:, :, 0:32]
                    x2 = src[:, :, 32:64]
                    cosb = COS[:, None, :].to_broadcast([128, 8, 32])
                    sinb = SIN[:, None, :].to_broadcast([128, 8, 32])
                    rot = ROT.rearrange("p (h d) -> p h d", h=8)
                    t1 = TMP1.rearrange("p (h d) -> p h d", h=8)
                    t2 = TMP2.rearrange("p (h d) -> p h d", h=8)
                    rr = [res('rqk'), r_tab[k]]
                    V(lambda e: e.tensor_tensor(out=t1, in0=x1, in1=cosb, op=ALU.mult), rr, [res('t1')])
                    V(lambda e: e.tensor_tensor(out=t2, in0=x2, in1=sinb, op=ALU.mult), rr, [res('t2')])
                    V(lambda e: e.tensor_tensor(out=rot[:, :, 0:32], in0=t1, in1=t2, op=ALU.subtract),
                      [res('t1'), res('t2')], [res('rot')])
                    V(lambda e: e.tensor_tensor(out=t1, in0=x1, in1=sinb, op=ALU.mult), rr, [res('t1')])
                    V(lambda e: e.tensor_tensor(out=t2, in0=x2, in1=cosb, op=ALU.mult), rr, [res('t2')])
                    V(lambda e: e.tensor_tensor(out=rot[:, :, 32:64], in0=t1, in1=t2, op=ALU.add),
                      [res('t1'), res('t2')], [res('rot')])
                    gt = GQK[:, gcol:gcol + 8]
                    V(lambda e: e.tensor_tensor(out=qkb[:, off:off + 512].rearrange("p (h d) -> p h d", h=8), in0=rot,
                                                in1=gt[:, :, None].to_broadcast([128, 8, 64]), op=ALU.mult),
                      [res('rot'), res('gqk')], [res('qkb%s%d' % (nm, k))])

                b = nextpb()
                proj(k, COLS['rq'], 512, b)
                V(lambda e: e.tensor_scalar(out=RQK, in0=pbs[b][:, :], scalar1=rs, scalar2=None, op0=ALU.mult),
                  [PR[b], r_rs], [res('rqk')])
                rope('rq', 0, 0)
                b = nextpb()
                proj(k, COLS['rv'], 512, b)
                ACT(lambda e: e.activation(out=vtm, in_=pbs[b][:, :], func=AF.Copy, scale=rs), [PR[b], r_rs], [r_vtm])
                for nm, off in (('rg', 0), ('ng', 512)):
                    b = ABF[0] if nm == 'rg' else ABF[1]
                    proj(k, COLS[nm], 512, b)
                    ACT(lambda e: e.activation(out=GT, in_=pbs[b][:, :], func=AF.Exp, scale=nrs), [PR[b], r_nrs], [res('nq')])
                    ACT(lambda e: e.activation(out=GT, in_=GT, func=AF.Ln, bias=1.0), [res('nq')], [res('nq')])
                    ACT(lambda e: e.activation(out=GT, in_=GT, func=AF.Exp, scale=-1.0), [res('nq')], [res('nq')])
                    V(lambda e: e.scalar_tensor_tensor(out=gg[:, off:off + 512], in0=pbs[b][:, :], scalar=rs, in1=GT,
                                                       op0=ALU.mult, op1=ALU.mult), [PR[b], r_rs, res('nq')], [r_gate])
                b = UF
                proj(k, COLS['gl'], 24, b)
                V(lambda e: e.scalar_tensor_tensor(out=gl, in0=pbs[b][:, 0:24], scalar=rs, in1=BG[:], op0=ALU.mult, op1=ALU.add),
                  [PR[b], r_rs, res('bg')], [r_gl])
                ACT(lambda e: e.activation(out=gl, in_=gl, func=AF.Exp, scale=-1.0), [r_gl], [r_gl])
                ACT(lambda e: e.activation(out=gl, in_=gl, func=AF.Ln, bias=1.0), [r_gl], [r_gl])
                ACT(lambda e: e.activation(out=gl, in_=gl, func=AF.Exp, scale=-1.0), [r_gl], [r_gl])
                b = nextpb()
                proj(k, COLS['nq'], 512, b)
                V(lambda e: e.tensor_scalar(out=NQ, in0=pbs[b][:, :], scalar1=rs, scalar2=None, op0=ALU.mult),
                  [PR[b], r_rs], [res('nq')])
                b = nextpb()
                proj(k, COLS['rk'], 512, b)
                V(lambda e: e.tensor_scalar(out=RQK, in0=pbs[b][:, :], scalar1=rs, scalar2=None, op0=ALU.mult),
                  [PR[b], r_rs], [res('rqk')])
                yield 6.0
                rope('rk', 512, 8)
                ACT(lambda e: e.activation(out=SQ, in_=NQ, func=AF.Square), [res('nq')], [res('sq')])
                V(lambda e: e.tensor_reduce(out=SS[:, 0:8], in_=SQ.rearrange("p (h d) -> p h d", h=8), axis=AX.X, op=ALU.add),
                  [res('sq')], [res('ss')])
                rsqrt_act(SS[:, 0:8], SS[:, 0:8], 1.0 / 64, [res('ss')], [res('ss')])
                V(lambda e: e.tensor_tensor(out=SQ.rearrange("p (h d) -> p h d", h=8), in0=NQ.rearrange("p (h d) -> p h d", h=8),
                                            in1=SS[:, 0:8, None].to_broadcast([128, 8, 64]), op=ALU.mult),
                  [res('nq'), res('ss')], [res('sq')])
                V(lambda e: e.tensor_tensor(out=QN, in0=SQ, in1=QW[:], op=ALU.mult), [res('sq'), res('qw')], [res('qn')])
                yield 4.0
                bt = nextpb()
                ptb = pbs[bt][:, :].bitcast(BF16)
                for m in range(8):
                    nm = 'rq' if m < 4 else 'rk'
                    PE(lambda e, m=m: e.transpose(out=ptb[:, m * 128:(m + 1) * 128], in_=qkb[:, m * 128:(m + 1) * 128],
                                                  identity=IDB[:]),
                       [res('qkb%s%d' % (nm, k)), res('idb')], [PR[bt]], accum=(m > 0))
                V(lambda e: e.tensor_copy(out=qkt, in_=ptb), [PR[bt]], [r_qkt])
                if i + 1 < NT:
                    cast_xb(i + 1)
                yield 0.5
                bt2 = nextpb()
                ptq = pbs[bt2][:, :].bitcast(BF16)
                for m in range(4):
                    PE(lambda e, m=m: e.transpose(out=ptq[:, m * 128:(m + 1) * 128], in_=QN[:, m * 128:(m + 1) * 128],
                                                  identity=IDB[:]),
                       [res('qn'), res('idb')], [PR[bt2]], accum=(m > 0))
                V(lambda e: e.tensor_copy(out=qa[0:64, 0].rearrange("p k q -> p (k q)"), in_=ptq[0:64, 0:512]),
                  [PR[bt2]], [r_qaq])
                V(lambda e: e.tensor_copy(out=qa[64:128, 1].rearrange("p k q -> p (k q)"), in_=ptq[64:128, 0:512]),
                  [PR[bt2]], [r_qaq])
                yield 0.5

            def stageB(i):
                k = i % 2
                k3 = i % 3
                gg = GG[k3]
                qa = QA[k3]
                yb = YB[k]
                usbf = USBF[k]
                qkb = QKB[k]
                qkt = QKT[k]
                vtm = VTM[k]
                r_gate = res('gate%d' % k3)
                r_qaq = res('qa_q%d' % k3)
                r_qas = res('qa_s%d' % k3)
                r_qkt = res('qkt%d' % k)
                r_vtm = res('vtm%d' % k)
                QsT = qkt[:, 0:512].rearrange("p (m t) -> p m t", m=4)
                KsT = qkt[:, 512:1024].rearrange("p (m t) -> p m t", m=4)
                Kstm = qkb[:, 512:1024]
                for half in range(2):
                    ab = ABF[half]
                    for hh in range(4):
                        h = 2 * hh + half
                        rows = slice((h % 2) * 64, (h % 2) * 64 + 64)
                        PE(lambda e, h=h, hh=hh, rows=rows: e.matmul(pbs[ab][:, hh * 128:(hh + 1) * 128], lhsT=KsT[rows, h // 2, :],
                                                                    rhs=QsT[rows, h // 2, :], start=True, stop=True),
                           [r_qkt], [PR[ab]], accum=(hh > 0))
                    V(lambda e: e.tensor_tensor(
                        out=STB[:, half * 512:(half + 1) * 512].rearrange("p (h q) -> p h q", h=4),
                        in0=pbs[ab][:, :].rearrange("p (h q) -> p h q", h=4),
                        in1=MASKS[:, None, 0:128].to_broadcast([128, 4, 128]), op=ALU.mult),
                      [PR[ab], res('masks')], [res('stb%d' % half)])
                yield 2.0
                ob = UF
                for h in range(8):
                    rows = slice((h % 2) * 64, (h % 2) * 64 + 64)
                    so = (h % 2) * 512 + (h // 2) * 128
                    PE(lambda e, h=h, so=so: e.matmul(pbs[ob][:, h * 64:(h + 1) * 64], lhsT=STB[:, so:so + 128],
                                                      rhs=vtm[:, h * 64:(h + 1) * 64], start=True, stop=False),
                       [res('stb%d' % (h % 2)), r_vtm], [PR[ob]], accum=(h > 0))
                    PE(lambda e, h=h, rows=rows: e.matmul(pbs[ob][:, h * 64:(h + 1) * 64], lhsT=QsT[rows, h // 2, :],
                                                          rhs=RSB[rows, h // 2, :], start=False, stop=True),
                       [r_qkt, res('rsb')], [PR[ob]], accum=True)
                bkv = nextpb()
                for m in range(4):
                    PE(lambda e, m=m: e.matmul(pbs[bkv][:, m * 128:(m + 1) * 128], lhsT=Kstm[:, m * 128:(m + 1) * 128],
                                               rhs=vtm[:, m * 128:(m + 1) * 128], start=True, stop=True),
                       [res('qkbrk%d' % k), r_vtm], [PR[bkv]], accum=(m > 0))
                kvv = pbs[bkv][:, :].rearrange("p (m c) -> p m c", m=4)
                V(lambda e: e.tensor_tensor(out=RST[0:64], in0=RST[0:64], in1=kvv[0:64, :, 0:64], op=ALU.add),
                  [PR[bkv], res('rst')], [res('rst')])
                V(lambda e: e.tensor_tensor(out=RST[64:128], in0=RST[64:128], in1=kvv[64:128, :, 64:128], op=ALU.add),
                  [PR[bkv], res('rst')], [res('rst')])
                V(lambda e: e.tensor_tensor(out=RST[:].rearrange("p m c -> p (m c)"), in0=RST[:].rearrange("p m c -> p (m c)"),
                                            in1=GCT[:], op=ALU.mult), [res('rst'), res('gct')], [res('rst')])
                V(lambda e: e.tensor_copy(out=RSB[:], in_=RST[:]), [res('rst')], [res('rsb')])
                yield 0.5
                ACT(lambda e: e.activation(out=ORT, in_=pbs[ob][:, :], func=AF.Square), [PR[ob]], [res('ort')])
                V(lambda e: e.tensor_reduce(out=SS[:, 8:16], in_=ORT.rearrange("p (h d) -> p h d", h=8), axis=AX.X, op=ALU.add),
                  [res('ort')], [res('ss2')])
                rsqrt_act(SS[:, 8:16], SS[:, 8:16], 1.0 / 64, [res('ss2')], [res('ss2')])
                V(lambda e: e.tensor_tensor(out=ORT.rearrange("p (h d) -> p h d", h=8),
                                            in0=pbs[ob][:, :].rearrange("p (h d) -> p h d", h=8),
                                            in1=SS[:, 8:16, None].to_broadcast([128, 8, 64]), op=ALU.mult),
                  [PR[ob], res('ss2')], [res('ort')])
                V(lambda e: e.tensor_tensor(out=ORT, in0=ORT, in1=RETW[:], op=ALU.mult), [res('ort'), res('retw')], [res('ort')])
                V(lambda e: e.tensor_tensor(out=yb[:, 0:512], in0=ORT, in1=gg[:, 0:512], op=ALU.mult),
                  [res('ort'), r_gate], [res('yb0%d' % k)])
                yield 3.0
                nts = [0] if 8 * i + 6 < 128 else [0, 1]
                for g in range(2):
                    rows = slice(g * 64, (g + 1) * 64)
                    qrhs = qa[rows, g].rearrange("p k q -> p (k q)")
                    ub = UF
                    ib = nextpb()
                    cmp_blocks = []
                    for idx, nt_ in enumerate(nts):
                        ns = slice(nt_ * 128, (nt_ + 1) * 128)

                        def after(P, pk, ib=ib, nt_=nt_, idx=idx, g=g, ub=ub):
                            for h in range(4):
                                PE(lambda e, h=h: e.matmul(pbs[ib][:, h * 64:(h + 1) * 64], lhsT=P[:, h * 128:(h + 1) * 128],
                                                           rhs=MOV[:, nt_, :], start=(idx == 0 and h == 0),
                                                           stop=(idx == len(nts) - 1 and h == 3)),
                                   [r_p[pk], res('mov')], [PR[ib]], accum=not (idx == 0 and h == 0))
                            if idx == len(nts) - 1:
                                V(lambda e: e.tensor_copy(out=usbf[:, 0, g, :], in_=pbs[ub][:, 0:260]), [PR[ub]], [res('usbc%d%d' % (k, g))])
                        cmp_blocks.append((KCT[rows, ns], [res('kct')], qrhs, [r_qaq], None,
                                           CMK[k][:, nt_ * 128:(nt_ + 1) * 128], [r_cmk[k]],
                                           VCM[:, nt_, g, :], [res('vcm')], ub, idx == 0, idx == len(nts) - 1, after))
                    for _ in run_blocks(cmp_blocks, ABF):
                        pass
                    r_usb = res('usbc%d%d' % (k, g))
                    ucv = usbf[:, 0, g, :].rearrange("p (h c) -> p h c", h=4)
                    V(lambda e: e.tensor_scalar(out=DEN[:, g * 4:(g + 1) * 4], in0=ucv[:, :, 64], scalar1=1e-30, scalar2=None,
                                                op0=ALU.max), [r_usb], [res('den0%d' % g)])
                    V(lambda e: e.reciprocal(out=DEN[:, g * 4:(g + 1) * 4], in_=DEN[:, g * 4:(g + 1) * 4]),
                      [res('den0%d' % g)], [res('den0%d' % g)])
                    for h in range(4):
                        if h == 0:
                            V(lambda e: e.tensor_scalar(out=IMP[:, g * 64:(g + 1) * 64], in0=pbs[ib][:, 0:64],
                                                        scalar1=DEN[:, g * 4:g * 4 + 1], scalar2=None, op0=ALU.mult),
                              [PR[ib], res('den0%d' % g)], [res('imp%d' % g)])
                        else:
                            V(lambda e, h=h: e.scalar_tensor_tensor(
                                out=IMP[:, g * 64:(g + 1) * 64], in0=pbs[ib][:, h * 64:(h + 1) * 64],
                                scalar=DEN[:, g * 4 + h:g * 4 + h + 1], in1=IMP[:, g * 64:(g + 1) * 64], op0=ALU.mult, op1=ALU.add),
                              [PR[ib], res('den0%d' % g), res('imp%d' % g)], [res('imp%d' % g)])
                    yield 1.0
                    sco = SCO[:, g * 64:(g + 1) * 64]
                    sc2 = SC2[:, g * 64:(g + 1) * 64]
                    V(lambda e: e.tensor_tensor(out=sco, in0=IMP[:, g * 64:(g + 1) * 64], in1=TAB[k][:, 64:128], op=ALU.add),
                      [res('imp%d' % g), r_tab[k]], [res('sco%d' % g)])
                    V(lambda e: e.max(out=M8[:, g * 16:g * 16 + 8], in_=sco), [res('sco%d' % g)], [res('m8a%d' % g)])
                    V(lambda e: e.match_replace(out=sc2, in_to_replace=M8[:, g * 16:g * 16 + 8], in_values=sco, imm_value=-3e38),
                      [res('sco%d' % g), res('m8a%d' % g)], [res('sc2%d' % g)])
                    V(lambda e: e.max(out=M8[:, g * 16 + 8:g * 16 + 16], in_=sc2), [res('sc2%d' % g)], [res('m8b%d' % g)])
                    V(lambda e: e.tensor_scalar(out=SELB[:, (1 - g) * 64:(2 - g) * 64], in0=sco,
                                                scalar1=M8[:, g * 16 + 15:g * 16 + 16], scalar2=NEGB, op0=ALU.is_lt, op1=ALU.mult),
                      [res('sco%d' % g), res('m8b%d' % g)], [res('selb%d' % g)])
                    yield (0.5 if g == 0 else 3.0)
                win_blocks = []
                for g in range(2):
                    rows = slice(g * 64, (g + 1) * 64)
                    qrhs = qa[rows, g].rearrange("p k q -> p (k q)")
                    ub = UF
                    j0 = max(0, i - 4)
                    for j in range(j0, i + 1):
                        js = slice(j * 128, (j + 1) * 128)
                        if j == i:
                            mk = MB[:, 0, :]
                        elif j == i - 4:
                            mk = MB[:, 1, :]
                        else:
                            mk = None
                        after = None
                        if j == i:
                            def after(P, pk, ub=ub, g=g):
                                V(lambda e: e.tensor_copy(out=usbf[:, 1, g, :], in_=pbs[ub][:, 0:260]), [PR[ub]], [res('usbw%d%d' % (k, g))])
                        win_blocks.append((KWT[rows, js], [r_kw[j]], qrhs, [r_qaq], mk, None, [],
                                           VW[:, j, g, :], [r_vw[j]], ub, j == j0, j == i, after))
                yield from run_blocks(win_blocks, ABF)
                bs = nextpb()
                pts = pbs[bs][:, :].bitcast(BF16)
                PE(lambda e: e.transpose(out=pts[:, 0:128], in_=SELB, identity=IDB[:]),
                   [res('selb0'), res('selb1'), res('idb')], [PR[bs]])
                V(lambda e: e.tensor_copy(out=qa[64:128, 0], in_=pts[64:128, None, 0:128].to_broadcast([64, 4, 128])),
                  [PR[bs]], [r_qas])
                V(lambda e: e.tensor_copy(out=qa[0:64, 1], in_=pts[0:64, None, 0:128].to_broadcast([64, 4, 128])),
                  [PR[bs]], [r_qas])
                yield 0.5

            def back(i):
                k = i % 2
                k3 = i % 3
                ts = slice(i * 128, (i + 1) * 128)
                gg = GG[k3]
                qa = QA[k3]
                gl = GL[k3]
                yb = YB[k]
                usbf = USBF[k]
                r_gate = res('gate%d' % k3)
                r_qaq = res('qa_q%d' % k3)
                r_qas = res('qa_s%d' % k3)
                r_gl = res('gl%d' % k3)
                r_uf = [res('usbc%d%d' % (k, g)) for g in range(2)] + [res('usbw%d%d' % (k, g)) for g in range(2)]
                r_us = [res('usbs0'), res('usbs1')]
                uf = usbf.rearrange("p x g (h c) -> p x (g h) c", h=4)
                us = USBS.rearrange("p g (h c) -> p (g h) c", h=4)
                cov = CO.rearrange("p (x h) -> p x h", x=3)
                glv = gl.rearrange("p (x h) -> p x h", x=3)
                onv = ONS.rearrange("p (h d) -> p h d", h=8)
                sqv = SQB.rearrange("p (h d) -> p h d", h=8)
                for x_, xb_ in ((0, 0), (1, 2)):
                    V(lambda e: e.tensor_scalar(out=cov[:, xb_, :], in0=uf[:, x_, :, 64], scalar1=1e-30, scalar2=None, op0=ALU.max),
                      r_uf, [res('co%d' % xb_)])
                    V(lambda e: e.reciprocal(out=cov[:, xb_, :], in_=cov[:, xb_, :]), [res('co%d' % xb_)], [res('co%d' % xb_)])
                    V(lambda e: e.tensor_tensor(out=cov[:, xb_, :], in0=cov[:, xb_, :], in1=glv[:, xb_, :], op=ALU.mult),
                      [res('co%d' % xb_), r_gl], [res('co%d' % xb_)])
                V(lambda e: e.tensor_tensor(out=onv, in0=uf[:, 0, :, 0:64], in1=cov[:, 0, :, None].to_broadcast([128, 8, 64]), op=ALU.mult),
                  r_uf + [res('co0')], [res('ons')])
                V(lambda e: e.tensor_tensor(out=sqv, in0=uf[:, 1, :, 0:64], in1=cov[:, 2, :, None].to_broadcast([128, 8, 64]), op=ALU.mult),
                  r_uf + [res('co2')], [res('sqb')])
                V(lambda e: e.tensor_tensor(out=ONS, in0=ONS, in1=SQB, op=ALU.add), [res('ons'), res('sqb')], [res('ons')])
                yield

                def ytrans(half):
                    by = nextpb()
                    pty = pbs[by][:, :].bitcast(BF16)
                    for m in range(4):
                        c = half * 4 + m
                        PE(lambda e, m=m, c=c: e.transpose(out=pty[:, m * 128:(m + 1) * 128], in_=yb[:, c * 128:(c + 1) * 128],
                                                           identity=IDB[:]),
                           [res('yb%d%d' % (half, k)), res('idb')], [PR[by]], accum=(m > 0))
                    V(lambda e: e.tensor_copy(out=YT[:, half * 4:(half + 1) * 4, :].rearrange("p c t -> p (c t)"),
                                              in_=pty[:, 0:512]), [PR[by]], [res('yt%d' % half)])
                ytrans(0)
                yield
                def late_dve(g):
                    hs = slice(4 * g, 4 * g + 4)
                    cs = slice(g * 256, (g + 1) * 256)
                    r_c = res('co1%d' % g)
                    V(lambda e: e.tensor_scalar(out=cov[:, 1, hs], in0=us[:, hs, 64], scalar1=1e-30, scalar2=None, op0=ALU.max),
                      [res('usbs%d' % g)], [r_c])
                    V(lambda e: e.reciprocal(out=cov[:, 1, hs], in_=cov[:, 1, hs]), [r_c], [r_c])
                    V(lambda e: e.tensor_tensor(out=cov[:, 1, hs], in0=cov[:, 1, hs], in1=glv[:, 1, hs], op=ALU.mult),
                      [r_c, r_gl], [r_c])
                    V(lambda e: e.tensor_tensor(out=sqv[:, hs, :], in0=us[:, hs, 0:64],
                                                in1=cov[:, 1, hs, None].to_broadcast([128, 4, 64]), op=ALU.mult),
                      [res('usbs%d' % g), r_c, res('sqb')], [res('sqb%d' % g)])
                    V(lambda e: e.tensor_tensor(out=ONS[:, cs], in0=ONS[:, cs], in1=SQB[:, cs], op=ALU.add),
                      [res('ons'), res('sqb%d' % g), res('sqb')], [res('ons%d' % g)])
                    V(lambda e: e.tensor_tensor(out=yb[:, 512 + g * 256:512 + (g + 1) * 256], in0=ONS[:, cs],
                                                in1=gg[:, 512 + g * 256:512 + (g + 1) * 256], op=ALU.mult),
                      [res('ons%d' % g), res('ons'), r_gate], [res('yb1%d%d' % (k, g))])

                def late_tr(g, bank=None):
                    by = nextpb() if bank is None else bank
                    pty = pbs[by][:, :].bitcast(BF16)
                    for m in range(2):
                        c = 4 + 2 * g + m
                        PE(lambda e, m=m, c=c: e.transpose(out=pty[:, m * 128:(m + 1) * 128], in_=yb[:, c * 128:(c + 1) * 128],
                                                           identity=IDB[:]),
                           [res('yb1%d%d' % (k, g)), res('idb')], [PR[by]], accum=(m > 0))
                    V(lambda e: e.tensor_copy(out=YT[:, 4 + 2 * g:6 + 2 * g, :].rearrange("p c t -> p (c t)"),
                                              in_=pty[:, 0:256]), [PR[by]], [res('yt1%d' % g)])

                slc_blocks = []
                ntr = min(4, i)
                for g in range(2):
                    qaug = qa[:, g].rearrange("p k q -> p (k q)")
                    ub = UBB
                    for j in range(i + 1):
                        js = slice(j * 128, (j + 1) * 128)
                        after = None
                        if j == i:
                            def after(P, pk, ub=ub, g=g):
                                V(lambda e: e.tensor_copy(out=USBS[:, g, :], in_=pbs[ub][:, 0:260]), [PR[ub]], [res('usbs%d' % g)])
                                late_dve(g)
                                if g == 1 and ntr == i:
                                    late_tr(0)
                        elif g == 1 and j == ntr:
                            def after(P, pk):
                                late_tr(0)
                        slc_blocks.append((KSA[:, g, js], [r_ks[j], res('ksa_e')], qaug, [r_qaq, r_qas],
                                           MB[:, 0, :] if j == i else None, None, [],
                                           VS[:, j, g, :], [r_vs[j]], ub, j == 0, j == i, after))
                yield from run_blocks(slc_blocks, ABB)
                bos = [nextpb(), nextpb()]

                def oproj(nh, c):
                    bo = bos[nh]
                    PE(lambda e: e.matmul(pbs[bo][:, :], lhsT=YT[:, c, :], rhs=WOUT[:, c, nh * 512:(nh + 1) * 512],
                                          start=(c == 0), stop=(c == 7)),
                       [res('yt0') if c < 4 else res('yt1%d' % ((c - 4) // 2)), res('wout')], [PR[bo]], accum=(c > 0))
                for nh in range(2):
                    for c in range(6):
                        oproj(nh, c)
                late_tr(1, bank=ABB[0])
                for nh in range(2):
                    for c in (6, 7):
                        oproj(nh, c)
                    V(lambda e: e.tensor_tensor(out=XS[k][:, nh * 512:(nh + 1) * 512], in0=pbs[bos[nh]][:, :],
                                                in1=XS[k][:, nh * 512:(nh + 1) * 512], op=ALU.add),
                      [PR[bos[nh]], r_xs[k]], [r_xs[k]])
                dma(out[ts, :], XS[k][:, :], reads=[r_xs[k]])
                yield

            def merged3(gA, gB, gK, scale):
                gens = {'A': gA, 'B': gB, 'K': gK}
                alive = {n: (g is not None) for n, g in gens.items()}
                ready = {'A': 0.0, 'B': 0.0}
                now = 0.0
                while alive['A'] or alive['B'] or alive['K']:
                    cand = [n for n in ('A', 'B') if alive[n] and ready[n] <= now]
                    if cand:
                        n = min(cand, key=lambda z: ready[z])
                        try:
                            c = next(gens[n])
                            c = 0.5 if c is None else c
                            ready[n] = now + c * scale
                            now += 0.3
                        except StopIteration:
                            alive[n] = False
                    elif alive['K']:
                        try:
                            next(gens['K'])
                            now += 0.75
                        except StopIteration:
                            alive['K'] = False
                    else:
                        pend = [ready[n] for n in ('A', 'B') if alive[n]]
                        now = min(pend)

            FRONT_COST = 30.0
            load_xs(0)
            load_tabs(0)
            load_tabs(1)
            load_xt(0)
            cast_xb(0)
            load_xt(1)
            merged3(stageA(0), None, None, 1.0)
            load_xt(2)
            merged3(stageA(1), stageB(0), None, 1.0)
            for i in range(NT):
                if i + 1 < NT:
                    load_xs(i + 1)
                if i + 2 < NT:
                    load_tabs(i + 2)
                if i + 3 < NT:
                    load_xt(i + 3)
                back_time = 0.75 * 2 * (i + 1)
                scale = max(1.0, back_time / FRONT_COST)
                merged3(stageA(i + 2) if i + 2 < NT else None, stageB(i + 1) if i + 1 < NT else None, back(i), scale)

        try:
            record()
        except _Stop:
            pass
        fin = S.op('sync', None)
        fin.deps = list(S.dma_hist)
        S.emit(st)
    return nc


_CACHE = {}


def _prep_shared(inp):
    f = np.float32
    w_in = np.asarray(inp['w_in'][0], f)
    sp = np.cumsum([0, 512, 512, 512, 512, 512, 512, 128, 128, 128, 128, 128, 128, 24])
    seg = {n: (sp[i], sp[i + 1]) for i, n in enumerate(['rq', 'rk', 'rv', 'rg', 'nq', 'ng', 'ck', 'cv', 'sk', 'sv', 'wk', 'wv', 'gl'])}

    def cols(n, a=None, b=None):
        s0, s1 = seg[n]
        idx = np.arange(s0, s1)
        return idx if a is None else idx[a:b]
    c1 = np.concatenate([cols('ck', 0, 64), cols('cv', 0, 64), cols('ck', 64, 128), cols('cv', 64, 128),
                         cols('sk'), cols('wk'), cols('sv'), cols('wv')])
    nq_pairs = np.concatenate([np.concatenate([cols('nq', kk * 64, kk * 64 + 64), cols('nq', (4 + kk) * 64, (4 + kk) * 64 + 64)])
                               for kk in range(4)])
    c2 = np.concatenate([cols('rq'), cols('rk'), cols('rv'), cols('rg'), nq_pairs, cols('ng'), cols('gl')])
    sh = {}
    sh['w1'] = np.ascontiguousarray(w_in[:, c1])
    sh['w2'] = np.ascontiguousarray(w_in[:, c2])
    sh['wout'] = np.ascontiguousarray(np.asarray(inp['w_out'][0], f))
    sh['normw'] = np.ascontiguousarray(np.asarray(inp['norm_w'][0], f).reshape(8, 128).T)
    sh['retw'] = np.ascontiguousarray(np.broadcast_to(np.asarray(inp['ret_norm_w'][0], f).reshape(1, 512), (128, 512)))
    sh['qw'] = np.ascontiguousarray(np.broadcast_to(np.tile(np.asarray(inp['q_norm_w'][0], f), 8)[None, :], (128, 512)))
    for nm, key in (('kcw', 'k_norm_cmp'), ('ksw', 'k_norm_slc'), ('kww', 'k_norm_win')):
        sh[nm] = np.ascontiguousarray(np.broadcast_to(np.tile(np.asarray(inp[key][0], f), 2)[None, :], (128, 128)))
    sh['bg'] = np.ascontiguousarray(np.broadcast_to(np.asarray(inp['b_gate'][0], f)[None, :], (128, 24)))
    sh['pos'] = np.ascontiguousarray(np.concatenate([np.asarray(inp['cmp_pos_k'][0], f).T, np.asarray(inp['cmp_pos_v'][0], f).T], axis=0))
    sh['cw1k'] = np.ascontiguousarray(np.asarray(inp['cmp_w1_k'][0], f))
    sh['cw1v'] = np.ascontiguousarray(np.asarray(inp['cmp_w1_v'][0], f))
    w2k = np.asarray(inp['cmp_w2_k'][0], f).reshape(2, 128, 64).transpose(1, 0, 2)
    w2v = np.asarray(inp['cmp_w2_v'][0], f).reshape(2, 128, 64).transpose(1, 0, 2)
    sh['cw2'] = np.ascontiguousarray(np.stack([w2k, w2v], axis=1).reshape(128, 256))
    return sh


def kernel(**inp):
    if 'nc' not in _CACHE:
        _CACHE['nc'] = build_nc()
        _CACHE['consts'] = _consts()
    nc = _CACHE['nc']
    sh = _prep_shared(inp)
    sh.update(_CACHE['consts'])
    x = np.asarray(inp['x'], np.float32)
    in_maps = []
    for b in range(8):
        m = dict(sh)
        m['x'] = np.ascontiguousarray(x[b])
        m['xT'] = np.ascontiguousarray(x[b].T)
        in_maps.append(m)
    res = run_bass_kernel_spmd(nc, in_maps, core_ids=list(range(8)))
    return np.stack([np.asarray(r['out'], np.float32) for r in res.results], axis=0)
```
